# Optimizing a Trainium2 kernel written in Bass

```python
import math
import jax, jax.numpy as jnp
from jax import lax
import numpy as np

D_MODEL = 1024
BATCH = 8
SEQ = 2048
DEPTH = 2
DEC_BATCH = 128
DEC_SEQ = 4
PAST_LEN = 16384
PAGE_SIZE = 128

N_MIXERS = 2
N_GLA_LAYERS = (DEPTH + 1) // 2
N_S5_LAYERS = DEPTH // 2
GLA_HEADS = 4
GLA_DK = D_MODEL // 2 // GLA_HEADS
GLA_DV = D_MODEL // GLA_HEADS
GLA_QK = GLA_HEADS * GLA_DK
GLA_V = GLA_HEADS * GLA_DV
GLA_GATE_RANK = 16
GLA_GATE_TAU = 16.0
GLA_CHUNK = 64
GLA_IN = 2 * GLA_QK + 2 * GLA_V + GLA_GATE_RANK
S5_GROUP = 16
S5_GROUPS = D_MODEL // S5_GROUP
S5_STATE = 64
D_FF = 4 * D_MODEL
EPS = 1e-6

kernel_name = "gla_s5_interleaved_decoder_step"


def _rmsnorm(x, g):
    xf = x.astype(jnp.float32)
    y = xf * lax.rsqrt(jnp.mean(xf * xf, axis=-1, keepdims=True) + EPS)
    return (y * g.astype(jnp.float32)).astype(x.dtype)


def _gla_chunked(q, k, v, g, h0):
    b_, l_ = q.shape[0], q.shape[1]
    c = math.gcd(l_, GLA_CHUNK)
    n = l_ // c

    def chunks(t):
        return t.astype(jnp.float32).reshape(b_, n, c, GLA_HEADS, t.shape[-1]).transpose(1, 0, 3, 2, 4)

    q, k, v, g = chunks(q), chunks(k), chunks(v), chunks(g)
    bcum = jnp.cumsum(g, axis=3)
    b_last = bcum[..., -1:, :]
    b_ref = bcum[..., c // 2:c // 2 + 1, :]
    q_in = q * jnp.exp(bcum - b_ref)
    k_in = k * jnp.exp(b_ref - bcum)
    mask = jnp.tril(jnp.ones((c, c), dtype=bool))
    att = jnp.where(mask, jnp.einsum('nbhtk,nbhsk->nbhts', q_in, k_in), 0.0)
    o_intra = jnp.einsum('nbhts,nbhsv->nbhtv', att, v)
    u = jnp.einsum('nbhsk,nbhsv->nbhkv', k * jnp.exp(b_last - bcum), v)
    decay = jnp.exp(b_last[..., 0, :])

    def step(h, inp):
        d, uu = inp
        return d[..., None] * h + uu, h

    h_final, h_prev = lax.scan(step, h0.astype(jnp.float32), (decay, u))
    o_inter = jnp.einsum('nbhtk,nbhkv->nbhtv', q * jnp.exp(bcum), h_prev)
    o = (o_intra + o_inter).transpose(1, 0, 3, 2, 4).reshape(b_, l_, GLA_HEADS, GLA_DV)
    return o, h_final


def _gla_mixer(xn, h0, w_in, w_gate_up, b_gate, g_head, w_out):
    b_, l_, _ = xn.shape
    proj = xn @ w_in
    q, k, v, r, gl = jnp.split(proj, [GLA_QK, 2 * GLA_QK, 2 * GLA_QK + GLA_V, 2 * GLA_QK + 2 * GLA_V], axis=-1)
    gk = jax.nn.log_sigmoid((gl @ w_gate_up + b_gate).astype(jnp.float32)) / GLA_GATE_TAU
    q = q.reshape(b_, l_, GLA_HEADS, GLA_DK) * (GLA_DK ** -0.5)
    k = k.reshape(b_, l_, GLA_HEADS, GLA_DK)
    v = v.reshape(b_, l_, GLA_HEADS, GLA_DV)
    gk = gk.reshape(b_, l_, GLA_HEADS, GLA_DK)
    o, h = _gla_chunked(q, k, v, gk, h0)
    o = _rmsnorm(o, g_head.reshape(GLA_HEADS, GLA_DV)).reshape(b_, l_, GLA_V).astype(xn.dtype)
    o = o * jax.nn.silu(r)
    return o @ w_out, h.astype(h0.dtype)


def _s5_mixer(xn, h0_re, h0_im, w_in, a_re, a_im, log_step, b_re, b_im, c_re, c_im, d_skip, w_glu_a, w_glu_b):
    b_, l_, _ = xn.shape
    u = (xn @ w_in).astype(jnp.float32)
    ug = u.reshape(b_, l_, S5_GROUPS, S5_GROUP)
    ar, ai = a_re.astype(jnp.float32), a_im.astype(jnp.float32)
    dt = jnp.exp(log_step.astype(jnp.float32))[:, None]
    mag = jnp.exp(ar * dt)
    lam_re, lam_im = mag * jnp.cos(ai * dt), mag * jnp.sin(ai * dt)
    nr, ni = lam_re - 1.0, lam_im
    den = ar * ar + ai * ai
    z_re = (nr * ar + ni * ai) / den
    z_im = (ni * ar - nr * ai) / den
    br, bi = b_re.astype(jnp.float32), b_im.astype(jnp.float32)
    bb_re = z_re[..., None] * br - z_im[..., None] * bi
    bb_im = z_re[..., None] * bi + z_im[..., None] * br
    bu_re = jnp.einsum('blgc,gpc->lbgp', ug, bb_re)
    bu_im = jnp.einsum('blgc,gpc->lbgp', ug, bb_im)
    h0r, h0i = h0_re.astype(jnp.float32), h0_im.astype(jnp.float32)
    bu_re = bu_re.at[0].add(lam_re * h0r - lam_im * h0i)
    bu_im = bu_im.at[0].add(lam_re * h0i + lam_im * h0r)
    lam_re_t = jnp.broadcast_to(lam_re, (l_, 1, S5_GROUPS, S5_STATE))
    lam_im_t = jnp.broadcast_to(lam_im, (l_, 1, S5_GROUPS, S5_STATE))

    def combine(e1, e2):
        a1r, a1i, b1r, b1i = e1
        a2r, a2i, b2r, b2i = e2
        return (a1r * a2r - a1i * a2i,
                a1r * a2i + a1i * a2r,
                a2r * b1r - a2i * b1i + b2r,
                a2r * b1i + a2i * b1r + b2i)

    _, _, h_re, h_im = lax.associative_scan(combine, (lam_re_t, lam_im_t, bu_re, bu_im), axis=0)
    y = (jnp.einsum('lbgp,gcp->blgc', h_re, c_re.astype(jnp.float32))
         - jnp.einsum('lbgp,gcp->blgc', h_im, c_im.astype(jnp.float32))).reshape(b_, l_, D_MODEL)
    y = y + d_skip.astype(jnp.float32) * u
    z = jax.nn.gelu(y).astype(xn.dtype)
    out = (z @ w_glu_a) * jax.nn.sigmoid(z @ w_glu_b)
    return out, h_re[-1].astype(h0_re.dtype), h_im[-1].astype(h0_im.dtype)


def _trunk(x, gla_h0, s5_h0_re, s5_h0_im, prm):
    gla_states, s5_re, s5_im = [], [], []
    for i in range(DEPTH):
        j = i // N_MIXERS
        h = _rmsnorm(x, prm['g_pre_mix'][i])
        if i % N_MIXERS == 0:
            m, st = _gla_mixer(h, gla_h0[j], prm['gla_w_in'][j], prm['gla_w_gate_up'][j], prm['gla_b_gate'][j],
                               prm['gla_g_head'][j], prm['gla_w_out'][j])
            gla_states.append(st)
        else:
            m, sr, si = _s5_mixer(h, s5_h0_re[j], s5_h0_im[j], prm['s5_w_in'][j], prm['s5_a_re'][j], prm['s5_a_im'][j],
                                  prm['s5_log_step'][j], prm['s5_b_re'][j], prm['s5_b_im'][j], prm['s5_c_re'][j],
                                  prm['s5_c_im'][j], prm['s5_d'][j], prm['s5_w_glu_a'][j], prm['s5_w_glu_b'][j])
            s5_re.append(sr)
            s5_im.append(si)
        x = x + _rmsnorm(m, prm['g_post_mix'][i])
        h = _rmsnorm(x, prm['g_pre_mlp'][i])
        f = jnp.square(jax.nn.relu(h @ prm['w_up'][i])) @ prm['w_down'][i]
        x = x + _rmsnorm(f, prm['g_post_mlp'][i])
    return x, jnp.stack(gla_states), jnp.stack(s5_re), jnp.stack(s5_im)


def setup_inputs(seed: int = 0) -> dict:
    key = jax.random.key(seed)
    ks = jax.random.split(key, 32)
    nrm = lambda k, shape, s: jax.random.normal(k, shape, jnp.float32) * s
    n_idx = jnp.arange(S5_STATE, dtype=jnp.float32)
    return {
        'x_prompt': nrm(ks[0], (BATCH, SEQ, D_MODEL), 1.0),
        'x_sample': nrm(ks[1], (DEC_BATCH, DEC_SEQ, D_MODEL), 1.0),
        'state_gla': nrm(ks[2], (N_GLA_LAYERS, DEC_BATCH, GLA_HEADS, GLA_DK, GLA_DV), 0.5),
        'state_s5_re': nrm(ks[3], (N_S5_LAYERS, DEC_BATCH, S5_GROUPS, S5_STATE), 0.1),
        'state_s5_im': nrm(ks[4], (N_S5_LAYERS, DEC_BATCH, S5_GROUPS, S5_STATE), 0.1),
        'g_pre_mix': 1.0 + nrm(ks[5], (DEPTH, D_MODEL), 0.02),
        'g_post_mix': 1.0 + nrm(ks[6], (DEPTH, D_MODEL), 0.02),
        'g_pre_mlp': 1.0 + nrm(ks[7], (DEPTH, D_MODEL), 0.02),
        'g_post_mlp': 1.0 + nrm(ks[8], (DEPTH, D_MODEL), 0.02),
        'w_up': nrm(ks[9], (DEPTH, D_MODEL, D_FF), D_MODEL ** -0.5),
        'w_down': nrm(ks[10], (DEPTH, D_FF, D_MODEL), D_FF ** -0.5),
        'gla_w_in': nrm(ks[11], (N_GLA_LAYERS, D_MODEL, GLA_IN), D_MODEL ** -0.5),
        'gla_w_gate_up': nrm(ks[12], (N_GLA_LAYERS, GLA_GATE_RANK, GLA_QK), GLA_GATE_RANK ** -0.5),
        'gla_b_gate': nrm(ks[13], (N_GLA_LAYERS, GLA_QK), 0.1),
        'gla_g_head': 1.0 + nrm(ks[14], (N_GLA_LAYERS, GLA_V), 0.02),
        'gla_w_out': nrm(ks[15], (N_GLA_LAYERS, GLA_V, D_MODEL), GLA_V ** -0.5),
        's5_w_in': nrm(ks[16], (N_S5_LAYERS, D_MODEL, D_MODEL), D_MODEL ** -0.5),
        's5_a_re': -0.5 + nrm(ks[17], (N_S5_LAYERS, S5_GROUPS, S5_STATE), 0.01),
        's5_a_im': jnp.pi * n_idx + nrm(ks[18], (N_S5_LAYERS, S5_GROUPS, S5_STATE), 0.01),
        's5_log_step': jax.random.uniform(ks[19], (N_S5_LAYERS, S5_GROUPS), jnp.float32, math.log(1e-3), math.log(1e-1)),
        's5_b_re': nrm(ks[20], (N_S5_LAYERS, S5_GROUPS, S5_STATE, S5_GROUP), (2 * S5_GROUP) ** -0.5),
        's5_b_im': nrm(ks[21], (N_S5_LAYERS, S5_GROUPS, S5_STATE, S5_GROUP), (2 * S5_GROUP) ** -0.5),
        's5_c_re': nrm(ks[22], (N_S5_LAYERS, S5_GROUPS, S5_GROUP, S5_STATE), S5_STATE ** -0.5),
        's5_c_im': nrm(ks[23], (N_S5_LAYERS, S5_GROUPS, S5_GROUP, S5_STATE), S5_STATE ** -0.5),
        's5_d': nrm(ks[24], (N_S5_LAYERS, D_MODEL), 1.0),
        's5_w_glu_a': nrm(ks[25], (N_S5_LAYERS, D_MODEL, D_MODEL), D_MODEL ** -0.5),
        's5_w_glu_b': nrm(ks[26], (N_S5_LAYERS, D_MODEL, D_MODEL), D_MODEL ** -0.5),
    }


def reference(x_prompt, x_sample, state_gla, state_s5_re, state_s5_im, g_pre_mix, g_post_mix, g_pre_mlp, g_post_mlp,
              w_up, w_down, gla_w_in, gla_w_gate_up, gla_b_gate, gla_g_head, gla_w_out, s5_w_in, s5_a_re, s5_a_im,
              s5_log_step, s5_b_re, s5_b_im, s5_c_re, s5_c_im, s5_d, s5_w_glu_a, s5_w_glu_b):
    prm = {
        'g_pre_mix': g_pre_mix, 'g_post_mix': g_post_mix, 'g_pre_mlp': g_pre_mlp, 'g_post_mlp': g_post_mlp,
        'w_up': w_up, 'w_down': w_down,
        'gla_w_in': gla_w_in, 'gla_w_gate_up': gla_w_gate_up, 'gla_b_gate': gla_b_gate,
        'gla_g_head': gla_g_head, 'gla_w_out': gla_w_out,
        's5_w_in': s5_w_in, 's5_a_re': s5_a_re, 's5_a_im': s5_a_im, 's5_log_step': s5_log_step,
        's5_b_re': s5_b_re, 's5_b_im': s5_b_im, 's5_c_re': s5_c_re, 's5_c_im': s5_c_im, 's5_d': s5_d,
        's5_w_glu_a': s5_w_glu_a, 's5_w_glu_b': s5_w_glu_b,
    }
    b_p = x_prompt.shape[0]
    gla0 = jnp.zeros((N_GLA_LAYERS, b_p, GLA_HEADS, GLA_DK, GLA_DV), x_prompt.dtype)
    s50 = jnp.zeros((N_S5_LAYERS, b_p, S5_GROUPS, S5_STATE), x_prompt.dtype)
    y_prompt, gla_prompt, s5_re_prompt, s5_im_prompt = _trunk(x_prompt, gla0, s50, s50, prm)
    y_sample, gla_sample, s5_re_sample, s5_im_sample = _trunk(x_sample, state_gla, state_s5_re, state_s5_im, prm)
    return (y_prompt, y_sample, gla_prompt, gla_sample, s5_re_prompt, s5_im_prompt, s5_re_sample, s5_im_sample)
```

```python
import contextlib
import math
import os

import numpy as np
import ml_dtypes
import concourse.bass as bass
import concourse.mybir as mybir
from concourse.bass_utils import run_bass_kernel_spmd

F32 = mybir.dt.float32
BF16 = mybir.dt.bfloat16
I32 = mybir.dt.int32
AF = mybir.ActivationFunctionType
ALU = mybir.AluOpType
AX = mybir.AxisListType

D = 1024
FF = 4096
SEQ = 2048
NS = 64
NB = 16
H = 4
DK = 128
DV = 256
GIN = 3088
EPS = 1e-6
NG = 64
NP = 64
TB = 8
ENGS = ("pe", "act", "dve", "pool", "sp")


def I(method, *a, **k):
    return lambda e: getattr(e, method)(*a, **k)


class Prog:
    def __init__(self, nc, es):
        self.nc = nc
        self.es = es
        self.q = {e: [] for e in ENGS}
        self.val = {}
        self.known = {e: {} for e in ENGS}
        self.lastw = {}
        self.readers = {}
        self.semkeys = []
        self.sems = {}
        self.bank_last = {}

    def _key(self, k):
        if k not in self.val:
            self.val[k] = 0
            self.semkeys.append(k)

    def _deps(self, eng, reads, writes):
        deps = {}
        for b in list(reads) + list(writes):
            if "/" in b:
                for k, v in self.bank_last.get(b.split("/")[0], {}).items():
                    if k != eng:
                        deps[k] = max(deps.get(k, 0), v)
        for b in reads:
            w = self.lastw.get(b)
            if w:
                deps[w[0]] = max(deps.get(w[0], 0), w[1])
        for b in writes:
            w = self.lastw.get(b)
            if w:
                deps[w[0]] = max(deps.get(w[0], 0), w[1])
            for k, v in self.readers.get(b, {}).items():
                deps[k] = max(deps.get(k, 0), v)
        for k, v in deps.items():
            if k.startswith("d:"):
                v = self.val[k]
            if self.known[eng].get(k, 0) < v:
                self.q[eng].append(("wait", k, v))
                self.known[eng][k] = v

    def _mark(self, key, v, reads, writes):
        for b in list(reads) + list(writes):
            if "/" in b:
                self.bank_last.setdefault(b.split("/")[0], {})[key] = v
        for b in writes:
            self.lastw[b] = (key, v)
            self.readers[b] = {}
        for b in reads:
            r = self.readers.setdefault(b, {})
            r[key] = max(r.get(key, 0), v)

    def op(self, eng, fns, reads=(), writes=()):
        if callable(fns):
            fns = [fns]
        self._deps(eng, reads, writes)
        self._key(eng)
        self.val[eng] += 1
        v = self.val[eng]
        for f in fns[:-1]:
            self.q[eng].append(("op", f, None, 0))
        self.q[eng].append(("op", fns[-1], eng, 1))
        self._mark(eng, v, reads, writes)

    def dma(self, eng, fn, sem, reads=(), writes=()):
        k = "d:" + sem
        self._deps(eng, reads, writes)
        self._key(k)
        self.val[k] += 16
        v = self.val[k]
        self.q[eng].append(("op", fn, k, 16))
        self._mark(k, v, reads, writes)

    def barrier(self):
        for e in ENGS:
            for k in self.semkeys:
                v = self.val[k]
                if v > 0 and self.known[e].get(k, 0) < v:
                    self.q[e].append(("wait", k, v))
                    self.known[e][k] = v

    def flush(self):
        nc = self.nc
        for k in self.semkeys:
            if k not in self.sems:
                self.sems[k] = self.es.enter_context(nc.semaphore("s_" + k.replace(":", "_")))
        sems = self.sems
        q = self.q
        self.q = {e: [] for e in ENGS}
        with nc.Block() as block:
            def run(engname):
                def body(eng):
                    for it in q[engname]:
                        if it[0] == "wait":
                            eng.wait_ge(sems[it[1]], it[2])
                        else:
                            ins = it[1](eng)
                            if it[2] is not None:
                                ins.then_inc(sems[it[2]], it[3])
                return body

            block.tensor(run("pe"))
            block.scalar(run("act"))
            block.vector(run("dve"))
            block.gpsimd(run("pool"))
            block.sync(run("sp"))


def _consts():
    c = {}
    c["ident_bf"] = np.eye(128, dtype=ml_dtypes.bfloat16)
    c["ident_f"] = np.eye(128, dtype=np.float32)
    s = np.arange(64)
    c["maskT_p"] = (s[:, None] <= s[None, :]).astype(np.float32)
    c["maskT_s"] = ((s[:, None] <= s[None, :]) & (s[:, None] // 4 == s[None, :] // 4)).astype(np.float32)
    m = np.ones((128, 1024), np.float32)
    m[:, ::64] = 0
    c["m01p"] = m
    m = np.ones((128, 256), np.float32)
    m[:, ::4] = 0
    c["m01s"] = m
    cm = np.zeros((128, NB, 64), np.float32)
    for b in range(NB):
        cm[:, b, 4 * b:4 * b + 4] = 1
    c["colmask"] = cm
    rm = np.zeros((64, NB), np.float32)
    for b in range(NB):
        rm[4 * b:4 * b + 4, b] = 1
    c["rowmask"] = rm
    sel = np.zeros((128, 8, 8, 128), np.float32)
    selT = np.zeros((128, 8, 8, 128), np.float32)
    for gl in range(8):
        for s_ in range(8):
            for cc in range(16):
                sel[gl * 16 + cc, gl, s_, s_ * 16 + cc] = 1
                selT[s_ * 16 + cc, s_, gl, gl * 16 + cc] = 1
    c["sel"] = sel.astype(ml_dtypes.bfloat16)
    c["selT"] = selT.astype(ml_dtypes.bfloat16)
    i = np.arange(128) // 16
    c["mask1"] = (i[:, None] <= i[None, :]).astype(np.float32)
    return c


CONST_SPECS = {
    "ident_bf": ([128, 128], BF16), "ident_f": ([128, 128], F32),
    "maskT_p": ([64, 64], F32), "maskT_s": ([64, 64], F32),
    "m01p": ([128, 1024], F32), "m01s": ([128, 256], F32),
    "colmask": ([128, NB, 64], F32), "rowmask": ([64, NB], F32),
    "sel": ([128, 8, 8, 128], BF16), "selT": ([128, 8, 8, 128], BF16),
    "mask1": ([128, 128], F32),
}

IN_SPECS = {
    "xp": [SEQ, D], "xs": [NS, D], "sg": [NB, H, DK, DV], "sre": [NB, NG * NP], "sim": [NB, NG * NP],
    "g_pre_mix": [2, D], "g_post_mix": [2, D], "g_pre_mlp": [2, D], "g_post_mlp": [2, D],
    "w_up": [2, D, FF], "w_down": [2, FF, D],
    "gla_w_in": [1, D, GIN], "gla_w_gate_up": [1, 16, 512], "gla_b_gate": [1, 512], "gla_g_head": [1, D],
    "gla_w_out": [1, D, D], "s5_w_in": [1, D, D], "s5_a_re": [1, NG, NP], "s5_a_im": [1, NG, NP],
    "s5_log_step": [1, NG], "s5_b_re": [1, NG, NP, 16], "s5_b_im": [1, NG, NP, 16],
    "s5_c_re": [1, NG, 16, NP], "s5_c_im": [1, NG, 16, NP], "s5_d": [1, D],
    "s5_w_glu_a": [1, D, D], "s5_w_glu_b": [1, D, D],
}
OUT_SPECS = {
    "yp": [SEQ, D], "ys": [NS, D], "glap": [H, DK, DV], "glas": [NB, H, DK, DV],
    "s5p": [2, 32, 128], "s5s": [NB, 2, NG * NP],
}


def s5_phase(nc, P, T, phase, mk_norm_env, prenorm, postnorm_res, load, store, load_gain, load_w, prenorm_a, prenorm_b):
    NT = SEQ + NS
    NBLK = SEQ // TB
    TWO_PI = 2.0 * math.pi
    with phase() as (sbo, pso_):
        W1 = sbo("W1", [128, NG, 128], BF16)
        W2r = sbo("W2r", [128, 32, 128], BF16)
        W2i = sbo("W2i", [128, 32, 128], BF16)
        W3 = [sbo("W3r", [128, NG, 64], BF16), sbo("W3i", [128, NG, 64], BF16)]
        Lr = sbo("Lr", [128, 8, 32], F32)
        Li = sbo("Li", [128, 8, 32], F32)
        nLi = sbo("nLi", [128, 8, 32], F32)
        Em4 = sbo("Em4", [128, 3, 32], F32)
        h0T = sbo("h0T", [128, 32, 2, NB], F32)
        HsOut = sbo("HsOut", [128, 2, NB, 32], F32)
        HfinP = sbo("HfinP", [128, 128], F32)
        P.op("pool", I("memset", HfinP[:, :], 0.0), writes=["HfinP"])
        HfinV = HfinP[:, 0:64].rearrange("p (ri pr) -> p ri pr", ri=2)
        idf = sbo("idf", [128, 128], F32)
        load(idf[:, :], "idf", T["ident_f"])

        with phase() as (sb, ps):
            idb = sb("idb", [128, 128], BF16)
            load(idb[:, :], "idb", T["ident_bf"])
            mask1 = sb("mask1", [128, 128], F32)
            load(mask1[:, :], "mask1", T["mask1"])
            Ain = sb("Ain", [32, 3, 128], F32)
            load(Ain[:, 0, :], "Ain", T["s5_a_re"][0].rearrange("(pr g2) p -> pr (g2 p)", g2=2))
            load(Ain[:, 1, :], "Ain", T["s5_a_im"][0].rearrange("(pr g2) p -> pr (g2 p)", g2=2))
            Lin = sb("Lin", [32, 2], F32)
            load(Lin[:, :], "Lin", T["s5_log_step"].rearrange("o (pr g2) -> (o pr) g2", g2=2))
            P.op("dve", I("tensor_copy", out=Ain[:, 2, :].rearrange("p (g q) -> p g q", g=2),
                          in_=Lin[:, :].rearrange("p (g o) -> p g o", o=1).broadcast_to([32, 2, 64])), reads=["Lin", "Ain"], writes=["Ain"])
            psA = ps("psA", [128, 3, 32], F32)
            P.op("pe", [I("transpose", out=psA[:, k, :], in_=Ain[:, k, :], identity=idf[0:32, 0:32]) for k in range(3)], reads=["Ain", "idf"], writes=["psA/x"])
            aT = sb("aT", [128, 3, 32], F32)
            P.op("dve", I("tensor_copy", out=aT[:, :, :], in_=psA[:, :, :]), reads=["psA/x"], writes=["aT"])
            dtT = sb("dtT", [128, 32], F32)
            P.op("act", I("activation", out=dtT[:, :], in_=aT[:, 2, :], func=AF.Exp), reads=["aT"], writes=["dtT"])
            ad = sb("ad", [128, 2, 32], F32)
            P.op("dve", I("tensor_tensor", out=ad[:, :, :], in0=aT[:, 0:2, :], in1=dtT[:, :].rearrange("p (o q) -> p o q", o=1).broadcast_to([128, 2, 32]), op=ALU.mult),
                 reads=["aT", "dtT"], writes=["ad"])
            tsc = sb("tsc", [128, 8, 32], F32)
            for t in range(8):
                P.op("pool", I("memset", tsc[:, t, :], float(t + 1)), writes=["tsc"])
            ang = sb("ang", [128, 8, 32], F32)
            P.op("dve", I("tensor_tensor", out=ang[:, :, :], in0=tsc[:, :, :], in1=ad[:, 1:2, :].broadcast_to([128, 8, 32]), op=ALU.mult), reads=["tsc", "ad"], writes=["ang"])
            art = sb("art", [128, 8, 32], F32)
            P.op("dve", I("tensor_tensor", out=art[:, :, :], in0=tsc[:, :, :], in1=ad[:, 0:1, :].broadcast_to([128, 8, 32]), op=ALU.mult), reads=["tsc", "ad"], writes=["art"])
            npi = sb("npi", [128, 1], F32)
            P.op("dve", I("memset", npi[:, :], -math.pi), writes=["npi"])
            sc = sb("sc", [128, 2, 8, 32], F32)
            u = sb("u", [128, 8, 32], F32)
            ki = sb("ki", [128, 8, 32], I32)
            kf = sb("kf", [128, 8, 32], F32)
            for w, off in ((0, 0.5), (1, 0.75)):
                P.op("dve", I("tensor_scalar", out=u[:, :, :], in0=ang[:, :, :], scalar1=1.0 / TWO_PI, scalar2=off + 64.0, op0=ALU.mult, op1=ALU.add), reads=["ang"], writes=["u"])
                P.op("dve", I("tensor_copy", out=ki[:, :, :], in_=u[:, :, :]), reads=["u"], writes=["ki"])
                P.op("dve", I("tensor_copy", out=kf[:, :, :], in_=ki[:, :, :]), reads=["ki"], writes=["kf"])
                P.op("dve", I("tensor_tensor", out=u[:, :, :], in0=u[:, :, :], in1=kf[:, :, :], op=ALU.subtract), reads=["u", "kf"], writes=["u"])
                P.op("dve", I("tensor_single_scalar", out=kf[:, :, :], in_=u[:, :, :], scalar=0.0, op=ALU.is_lt), reads=["u"], writes=["kf"])
                P.op("dve", I("tensor_tensor", out=u[:, :, :], in0=u[:, :, :], in1=kf[:, :, :], op=ALU.add), reads=["u", "kf"], writes=["u"])
                P.op("act", I("activation", out=sc[:, w, :, :], in_=u[:, :, :], func=AF.Sin, scale=TWO_PI, bias=npi[:, 0:1]), reads=["u", "npi"], writes=["sc"])
            mag = sb("mag", [128, 2, 8, 32], F32)
            P.op("act", I("activation", out=mag[:, 0, :, :], in_=art[:, :, :], func=AF.Exp), reads=["art"], writes=["mag"])
            P.op("act", I("activation", out=mag[:, 1, :, :], in_=art[:, :, :], func=AF.Exp, scale=-1.0), reads=["art"], writes=["mag"])
            Ep = sb("Ep", [128, 2, 9, 32], F32)
            En = sb("En", [128, 2, 8, 32], F32)
            P.op("dve", I("memset", Ep[:, 0, 0, :], 1.0), writes=["Ep"])
            P.op("dve", I("memset", Ep[:, 1, 0, :], 0.0), writes=["Ep"])
            P.op("dve", I("tensor_tensor", out=Ep[:, 0, 1:9, :], in0=mag[:, 0, :, :], in1=sc[:, 1, :, :], op=ALU.mult), reads=["mag", "sc"], writes=["Ep"])
            P.op("dve", I("tensor_tensor", out=Ep[:, 1, 1:9, :], in0=mag[:, 0, :, :], in1=sc[:, 0, :, :], op=ALU.mult), reads=["mag", "sc"], writes=["Ep"])
            P.op("dve", I("tensor_tensor", out=En[:, 0, :, :], in0=mag[:, 1, :, :], in1=sc[:, 1, :, :], op=ALU.mult), reads=["mag", "sc"], writes=["En"])
            P.op("dve", I("scalar_tensor_tensor", out=En[:, 1, :, :], in0=mag[:, 1, :, :], scalar=-1.0, in1=sc[:, 0, :, :], op0=ALU.mult, op1=ALU.mult), reads=["mag", "sc"], writes=["En"])
            P.op("dve", I("tensor_copy", out=Em4[:, 0:2, :], in_=En[:, :, 3, :]), reads=["En"], writes=["Em4"])
            P.op("dve", I("tensor_scalar", out=Em4[:, 2, :], in0=En[:, 1, 3, :], scalar1=-1.0, scalar2=None, op0=ALU.mult), reads=["En"], writes=["Em4"])
            P.op("dve", I("tensor_copy", out=Lr[:, 0, :], in_=Ep[:, 0, 8, :]), reads=["Ep"], writes=["Lr"])
            P.op("dve", I("tensor_copy", out=Li[:, 0, :], in_=Ep[:, 1, 8, :]), reads=["Ep"], writes=["Li"])
            t1 = sb("t1", [128, 32], F32)
            t2 = sb("t2", [128, 32], F32)
            for k in range(7):
                P.op("dve", I("tensor_tensor", out=t1[:, :], in0=Lr[:, k, :], in1=Lr[:, k, :], op=ALU.mult), reads=["Lr"], writes=["t1"])
                P.op("dve", I("tensor_tensor", out=t2[:, :], in0=Li[:, k, :], in1=Li[:, k, :], op=ALU.mult), reads=["Li"], writes=["t2"])
                P.op("dve", I("scalar_tensor_tensor", out=Li[:, k + 1, :], in0=Lr[:, k, :], scalar=2.0, in1=Li[:, k, :], op0=ALU.mult, op1=ALU.mult), reads=["Lr", "Li"], writes=["Li"])
                P.op("dve", I("tensor_tensor", out=Lr[:, k + 1, :], in0=t1[:, :], in1=t2[:, :], op=ALU.subtract), reads=["t1", "t2", "Li"], writes=["Lr"])
            P.op("dve", I("tensor_scalar", out=nLi[:, :, :], in0=Li[:, :, :], scalar1=-1.0, scalar2=None, op0=ALU.mult), reads=["Li"], writes=["nLi"])
            zt = sb("zt", [128, 8, 32], F32)
            P.op("dve", I("tensor_scalar", out=zt[:, 0, :], in0=Ep[:, 0, 1, :], scalar1=-1.0, scalar2=None, op0=ALU.add), reads=["Ep"], writes=["zt0"])
            P.op("dve", I("tensor_tensor", out=zt[:, 1, :], in0=aT[:, 0, :], in1=aT[:, 0, :], op=ALU.mult), reads=["aT"], writes=["zt1"])
            P.op("dve", I("tensor_tensor", out=zt[:, 3, :], in0=aT[:, 1, :], in1=aT[:, 1, :], op=ALU.mult), reads=["aT"], writes=["zt3"])
            P.op("dve", I("tensor_tensor", out=zt[:, 1, :], in0=zt[:, 1, :], in1=zt[:, 3, :], op=ALU.add), reads=["zt1", "zt3"], writes=["zt1"])
            P.op("dve", I("reciprocal", out=zt[:, 2, :], in_=zt[:, 1, :]), reads=["zt1"], writes=["zt2"])
            P.op("dve", I("tensor_tensor", out=zt[:, 3, :], in0=zt[:, 0, :], in1=aT[:, 0, :], op=ALU.mult), reads=["zt0", "aT", "zt1"], writes=["zt3"])
            P.op("dve", I("tensor_tensor", out=zt[:, 4, :], in0=Ep[:, 1, 1, :], in1=aT[:, 1, :], op=ALU.mult), reads=["Ep", "aT"], writes=["zt4"])
            P.op("dve", I("tensor_tensor", out=zt[:, 3, :], in0=zt[:, 3, :], in1=zt[:, 4, :], op=ALU.add), reads=["zt3", "zt4"], writes=["zt3"])
            P.op("dve", I("tensor_tensor", out=zt[:, 5, :], in0=zt[:, 3, :], in1=zt[:, 2, :], op=ALU.mult), reads=["zt3", "zt2"], writes=["zt5"])
            P.op("dve", I("tensor_tensor", out=zt[:, 3, :], in0=Ep[:, 1, 1, :], in1=aT[:, 0, :], op=ALU.mult), reads=["Ep", "aT", "zt5"], writes=["zt3"])
            P.op("dve", I("tensor_tensor", out=zt[:, 4, :], in0=zt[:, 0, :], in1=aT[:, 1, :], op=ALU.mult), reads=["zt0", "aT", "zt3"], writes=["zt4"])
            P.op("dve", I("tensor_tensor", out=zt[:, 3, :], in0=zt[:, 3, :], in1=zt[:, 4, :], op=ALU.subtract), reads=["zt3", "zt4"], writes=["zt3"])
            P.op("dve", I("tensor_tensor", out=zt[:, 6, :], in0=zt[:, 3, :], in1=zt[:, 2, :], op=ALU.mult), reads=["zt3", "zt2"], writes=["zt6"])
            zre = zt[:, 5, :].rearrange("p (q o) -> p q o", o=1).broadcast_to([128, 32, 16])
            zim = zt[:, 6, :].rearrange("p (q o) -> p q o", o=1).broadcast_to([128, 32, 16])
            Bt = sb("Bt", [128, 2, 32, 16], F32)
            load(Bt[:, 0, :, :], "Bt", T["s5_b_re"][0].rearrange("g p c -> (g p) c").rearrange("(pr q) c -> q pr c", q=128))
            load(Bt[:, 1, :, :], "Bt", T["s5_b_im"][0].rearrange("g p c -> (g p) c").rearrange("(pr q) c -> q pr c", q=128))
            bb = sb("bb", [128, 2, 32, 16], F32)
            mA = sb("mA", [128, 32, 16], F32)
            mB = sb("mB", [128, 32, 16], F32)
            P.op("dve", I("tensor_tensor", out=mA[:, :, :], in0=Bt[:, 0, :, :], in1=zre, op=ALU.mult), reads=["Bt", "zt5"], writes=["mA"])
            P.op("dve", I("tensor_tensor", out=mB[:, :, :], in0=Bt[:, 1, :, :], in1=zim, op=ALU.mult), reads=["Bt", "zt6"], writes=["mB"])
            P.op("dve", I("tensor_tensor", out=bb[:, 0, :, :], in0=mA[:, :, :], in1=mB[:, :, :], op=ALU.subtract), reads=["mA", "mB"], writes=["bb"])
            P.op("dve", I("tensor_tensor", out=mA[:, :, :], in0=Bt[:, 1, :, :], in1=zre, op=ALU.mult), reads=["Bt", "zt5", "bb"], writes=["mA"])
            P.op("dve", I("tensor_tensor", out=mB[:, :, :], in0=Bt[:, 0, :, :], in1=zim, op=ALU.mult), reads=["Bt", "zt6", "bb"], writes=["mB"])
            P.op("dve", I("tensor_tensor", out=bb[:, 1, :, :], in0=mA[:, :, :], in1=mB[:, :, :], op=ALU.add), reads=["mA", "mB"], writes=["bb"])
            Cin = sb("Cin", [32, 2, 16, 128], F32)
            for ri_, nm_ in ((0, "s5_c_re"), (1, "s5_c_im")):
                for g2_ in range(2):
                    load(Cin[:, ri_, :, g2_ * 64:(g2_ + 1) * 64], "Cin", T[nm_][0].rearrange("(pr g2) c p -> g2 pr c p", g2=2)[g2_])
            psC = ps("psC", [128, 2, 16, 32], F32)
            P.op("pe", [I("transpose", out=psC[:, ri, c, :], in_=Cin[:, ri, c, :], identity=idf[0:32, 0:32]) for ri in range(2) for c in range(16)],
                 reads=["Cin", "idf"], writes=["psC/x"])
            Ct = sb("Ct", [128, 2, 32, 16], F32)
            for ri in range(2):
                P.op("dve", I("tensor_copy", out=Ct[:, ri, :, :], in_=psC[:, ri, :, :].rearrange("q c pr -> q pr c")), reads=["psC/x"], writes=["Ct"])
            W2v = [W2r[:, :, :].rearrange("p q (t c) -> p q t c", t=8), W2i[:, :, :].rearrange("p q (t c) -> p q t c", t=8)]
            Xm = sb("Xm", [128, 32, 2, 8, 16], BF16)
            Xb = sb("Xb", [128, 32, 2, 8, 16], BF16)
            bgA, bgB = sb("bgA", [128, 32, 8, 16], F32), sb("bgB", [128, 32, 8, 16], F32)
            big = {"dve": (bgA, bgB, "bgA", "bgB"), "pool": (bgA, bgB, "bgA", "bgB")}

            def abc(a3):
                return a3.rearrange("p q (o c) -> p q o c", o=1).broadcast_to([128, 32, 8, 16])

            def ebc(tab, ri, t0, rev=False):
                e = tab[:, ri, t0:t0 + 8, :].rearrange("p t q -> p q t")
                return e.rearrange("p q (t o) -> p q t o", o=1).broadcast_to([128, 32, 8, 16])
            Epr = sb("Epr", [128, 2, 8, 32], F32)
            for t in range(8):
                P.op("pool", I("tensor_copy", out=Epr[:, :, t, :], in_=Ep[:, :, 7 - t, :]), reads=["Ep"], writes=["Epr"])
            jobs = [
                (W2v[0], "W2r", Ct, Ep, 1, "re", "dve"), (W2v[1], "W2i", Ct, Ep, 1, "nim", "dve"),
                (Xm[:, :, 0, :, :], "Xm0", bb, En, 0, "re", "dve"), (Xm[:, :, 1, :, :], "Xm1", bb, En, 0, "im", "dve"),
                (Xb[:, :, 0, :, :], "Xb0", bb, Epr, 0, "re", "dve"), (Xb[:, :, 1, :, :], "Xb1", bb, Epr, 0, "im", "dve"),
            ]
            allj = {"W2r": ["W2r"], "W2i": ["W2i"], "Xm": ["Xm0", "Xm1"], "Xb": ["Xb0", "Xb1"]}
            for (o_ap, on, a_t, tab, t0, kind, eng) in jobs:
                tA, tB, nA, nB = big[eng]
                e1, e2 = (0, 1) if kind == "re" else (1, 0)
                an = "Ct" if a_t is Ct else "bb"
                P.op(eng, I("tensor_tensor", out=tA[:, :, :, :], in0=abc(a_t[:, 0, :, :]), in1=ebc(tab, e1, t0), op=ALU.mult), reads=[an, "Ep", "En", "Epr"], writes=[nA])
                P.op("pool", I("tensor_tensor", out=tB[:, :, :, :], in0=abc(a_t[:, 1, :, :]), in1=ebc(tab, e2, t0), op=ALU.mult), reads=[an, "Ep", "En", "Epr"], writes=[nB])
                if kind == "re":
                    P.op(eng, I("tensor_tensor", out=o_ap, in0=tA[:, :, :, :], in1=tB[:, :, :, :], op=ALU.subtract), reads=[nA, nB], writes=[on])
                elif kind == "im":
                    P.op(eng, I("tensor_tensor", out=o_ap, in0=tA[:, :, :, :], in1=tB[:, :, :, :], op=ALU.add), reads=[nA, nB], writes=[on])
                else:
                    P.op(eng, I("tensor_scalar", out=tA[:, :, :, :], in0=tA[:, :, :, :], scalar1=-1.0, scalar2=None, op0=ALU.mult), reads=[nA], writes=[nA])
                    P.op(eng, I("tensor_tensor", out=o_ap, in0=tA[:, :, :, :], in1=tB[:, :, :, :], op=ALU.subtract), reads=[nA, nB], writes=[on])
            psW = [ps(f"psW{i}", [128, 128], F32) for i in range(2)]
            psX = [ps(f"psX{i}", [128, 2, 64], BF16) for i in range(2)]
            for g in range(NG):
                pr, g2 = g // 2, g % 2
                rows = slice(g2 * 64, g2 * 64 + 64)
                pw = psW[g % 2]
                P.op("pe", [I("matmul", pw[:, :], lhsT=Xm[rows, pr, ri, :, :].rearrange("p s c -> p (s c)"), rhs=(W2r, W2i)[ri][rows, pr, :], start=(ri == 0), stop=(ri == 1)) for ri in range(2)],
                     reads=allj["Xm"] + allj["W2r"] + allj["W2i"], writes=[f"psW{g % 2}/x"])
                P.op("dve", I("tensor_tensor", out=W1[:, g, :], in0=pw[:, :], in1=mask1[:, :], op=ALU.mult), reads=[f"psW{g % 2}/x", "mask1"], writes=["W1"])
                for ri in range(2):
                    k2 = g % 2
                    P.op("pe", I("transpose", out=psX[ri][:, k2, :], in_=Xb[rows, pr, ri, :, :].rearrange("p s c -> p (s c)"), identity=idb[rows, rows]), reads=allj["Xb"] + ["idb"], writes=[f"psX{ri}/{k2}"])
                    if ri == 0:
                        P.op("act", I("copy", out=W3[ri][:, g, :], in_=psX[ri][:, k2, :]), reads=[f"psX{ri}/{k2}"], writes=[f"W3{ri}"])
                    else:
                        P.op("dve", I("tensor_copy", out=W3[ri][:, g, :], in_=psX[ri][:, k2, :]), reads=[f"psX{ri}/{k2}"], writes=[f"W3{ri}"])

        KS = os.environ.get("KS_CUT", "Z")
        if KS == "b0":
            return
        with phase() as (sb, ps):
            X0 = sb("X0", [NB, 2, NG * NP], F32)
            load(X0[:, 0, :], "X0", T["sre"])
            load(X0[:, 1, :], "X0", T["sim"])
            psh = ps("psh", [128, 32, 2, NB], F32)
            P.op("pe", [I("transpose", out=psh[:, pr, ri, :], in_=X0[0:NB, ri, pr * 128:(pr + 1) * 128], identity=idf[0:NB, 0:NB]) for pr in range(32) for ri in range(2)],
                 reads=["X0", "idf"], writes=["psh/x"])
            P.op("dve", I("tensor_copy", out=h0T[:, 0:16, :, :], in_=psh[:, 0:16, :, :]), reads=["psh/x"], writes=["h0T"])
            P.op("act", I("copy", out=h0T[:, 16:32, :, :], in_=psh[:, 16:32, :, :]), reads=["psh/x"], writes=["h0T"])

        if KS == "s0":
            return
        with phase() as (sbm, psm_):
          zT = sbm("zT", [128, 8, NT], BF16)
          with phase() as (sbu, psu_):
            ufm = sbu("ufm", [128, 8, NT], BF16)
            with phase() as (sb, ps):
                env = mk_norm_env(sb, "a")
                gpre = load_gain(sb, "gpre", T["g_pre_mix"][1:2, :])
                win = load_w(sb, "s5win", T["s5_w_in"][0], 8, D)
                hT = sb("hT", [128, 8, NT], BF16)
                xt = [sb(f"xt{i}", [128, D], F32) for i in range(2)]
                psU = [ps(f"psU{i}", [128, 512], F32) for i in range(2)]
                psT = ps("psT", [128, 8, 128], BF16)
                rows = [(T["xa_p"], r, 128, r) for r in range(0, SEQ, 128)] + [(T["xa_s"], 0, 64, SEQ)]
                blks = [(b0, 512) for b0 in range(0, SEQ, 512)] + [(SEQ, NS)]

                def pre_blk(bi_):
                    b0, n = blks[bi_]
                    its = [rr for rr in rows if b0 <= rr[3] < b0 + n]
                    for p0 in range(0, len(its), 2):
                        ks = []
                        for i_, (src, r, m, col) in enumerate(its[p0:p0 + 2]):
                            load(xt[i_][0:m, :], f"xt{i_}", src[r:r + m, :], sem=f"xt{i_}")
                            ks.append(prenorm_a(env, xt[i_][0:m, :], f"xt{i_}", m, gpre, "gpre"))
                        for i_, (src, r, m, col) in enumerate(its[p0:p0 + 2]):
                            prenorm_b(env, ks[i_], psT, "psT/x", m, hT[:, :, col:col + m], f"hT{bi_}")

                k = 0
                pre_blk(0)
                for bi_, (b0, n) in enumerate(blks):
                    if bi_ + 1 < len(blks):
                        pre_blk(bi_ + 1)
                    for j in range(8):
                        pu = psU[k % 2]
                        P.op("pe", [I("matmul", pu[:, 0:n], lhsT=win[:, kc, j * 128:(j + 1) * 128], rhs=hT[:, kc, b0:b0 + n], start=(kc == 0), stop=(kc == 7)) for kc in range(8)],
                             reads=["s5win", f"hT{bi_}"], writes=[f"psU{k % 2}/x"])
                        if k % 2 == 0:
                            P.op("act", I("copy", out=ufm[:, j, b0:b0 + n], in_=pu[:, 0:n]), reads=[f"psU{k % 2}/x"], writes=["ufm"])
                        else:
                            P.op("dve", I("tensor_copy", out=ufm[:, j, b0:b0 + n], in_=pu[:, 0:n]), reads=[f"psU{k % 2}/x"], writes=["ufm"])
                        k += 1

            if KS == "a":
                return
            with phase() as (sb, ps):
                sel = sb("sel", [128, 8, 8, 128], BF16)
                load(sel[:, :, :, :], "sel", T["sel"])
                selT = sb("selT", [128, 8, 8, 128], BF16)
                load(selT[:, :, :, :], "selT", T["selT"])
                dsk = sb("dsk", [128, 8], F32)
                load(dsk[:, :], "dsk", T["s5_d"].rearrange("o (j p) -> p (o j)", p=128), allow_slow_non_contiguous=True)
                idb2 = sb("idb2", [128, 128], BF16)
                load(idb2[:, :], "idb2", T["ident_bf"])
                Dg = sb("Dg", [128, 8, 128], BF16)
                for j_ in range(8):
                    P.op("dve", I("tensor_scalar", out=Dg[:, j_, :], in0=idb2[:, :], scalar1=dsk[:, j_:j_ + 1], scalar2=None, op0=ALU.mult), reads=["idb2", "dsk"], writes=["Dg"])
                HlS = sb("HlS", [128, 32, 2, NB], F32)
                U8 = [sb(f"U8{i}", [128, 8, NBLK], BF16) for i in range(2)]
                U8s = [sb(f"U8s{i}", [128, 8, NB], BF16) for i in range(2)]
                HA = [sb(f"HA{i}", [128, 2, NBLK], F32) for i in range(4)]
                HB = [sb(f"HB{i}", [128, 2, NBLK], F32) for i in range(4)]
                Hin = [sb(f"Hin{i}", [128, 2, NBLK], BF16) for i in range(4)]
                HsIn = sb("HsIn", [128, 32, 2, NB], F32)
                HsInb = sb("HsInb", [128, 32, 2, NB], BF16)
                hs1 = sb("hs1", [128, 32, NB], F32)
                hs2 = sb("hs2", [128, 32, NB], F32)
                hst = sb("hst", [128, 2, NB], F32)
                Yg = sb("Yg", [128, 8, NBLK], BF16)
                Ygs = sb("Ygs", [128, 8, NB], BF16)
                psH = [ps(f"psH{i}", [128, 2, NBLK], F32) for i in range(2)]
                psu8 = [ps(f"psu8{i}", [128, NBLK], F32) for i in range(2)]
                psY_ = ps("psY", [128, 2 * NBLK], F32)
                psY = [psY_[:, 0:NBLK], psY_[:, NBLK:2 * NBLK]]
                psZ = [ps(f"psZ{i}", [128, NBLK], F32) for i in range(2)]
                psS = ps("psS", [128, 16, NB], F32)
                for i in range(4):
                    P.op("pool", I("memset", Hin[i][:, :, 0:1], 0.0), writes=[f"Hin{i}"])

                def e4b(k):
                    return Em4[:, k, :].rearrange("p (q o) -> p q o", o=1).broadcast_to([128, 32, NB])
                h0r_, h0i_ = h0T[:, :, 0, :], h0T[:, :, 1, :]
                P.op("dve", I("tensor_tensor", out=hs1[:, :, :], in0=h0r_, in1=e4b(0), op=ALU.mult), reads=["h0T", "Em4"], writes=["hs1"])
                P.op("pool", I("tensor_tensor", out=hs2[:, :, :], in0=h0i_, in1=e4b(1), op=ALU.mult), reads=["h0T", "Em4"], writes=["hs2"])
                P.op("dve", I("tensor_tensor", out=HsIn[:, :, 0, :], in0=hs1[:, :, :], in1=hs2[:, :, :], op=ALU.subtract), reads=["hs1", "hs2"], writes=["HsIn"])
                P.op("dve", I("tensor_tensor", out=hs1[:, :, :], in0=h0i_, in1=e4b(0), op=ALU.mult), reads=["h0T", "Em4", "HsIn"], writes=["hs1"])
                P.op("pool", I("tensor_tensor", out=hs2[:, :, :], in0=h0r_, in1=e4b(1), op=ALU.mult), reads=["h0T", "Em4", "HsIn"], writes=["hs2"])
                P.op("dve", I("tensor_tensor", out=HsIn[:, :, 1, :], in0=hs1[:, :, :], in1=hs2[:, :, :], op=ALU.add), reads=["hs1", "hs2"], writes=["HsIn"])
                P.op("act", I("copy", out=HsInb[:, :, :, :], in_=HsIn[:, :, :, :]), reads=["HsIn"], writes=["HsInb"])

                def upv(j):
                    return ufm[:, j, 0:SEQ].rearrange("p (n s) -> p s n", s=TB)

                def usv(j):
                    return ufm[:, j, SEQ:NT].rearrange("p (b t) -> p t b", t=4)

                def stageU(j):
                    up, us = upv(j), usv(j)
                    u8, u8s = U8[j % 2], U8s[j % 2]
                    for gl in range(8):
                        pu = psu8[gl % 2]
                        P.op("pe", [I("matmul", pu[:, :], lhsT=sel[:, gl, s_, :], rhs=up[:, s_, :], start=(s_ == 0), stop=(s_ == 7)) for s_ in range(8)],
                             reads=["sel", "ufm"], writes=[f"psu8{gl % 2}/x"])
                        P.op("act", I("copy", out=u8[:, gl, :], in_=pu[:, :]), reads=[f"psu8{gl % 2}/x"], writes=[f"U8_{j % 2}_{gl}"])
                        P.op("pe", [I("matmul", psS[:, gl % 2, :], lhsT=sel[:, gl, s_, :], rhs=us[:, s_ - 4, :], start=(s_ == 4), stop=(s_ == 7)) for s_ in range(4, 8)],
                             reads=["sel", "ufm"], writes=[f"psS/{gl % 2}"])
                        P.op("act", I("copy", out=u8s[:, gl, :], in_=psS[:, gl % 2, :]), reads=[f"psS/{gl % 2}"], writes=[f"U8s_{j % 2}_{gl}"])

                def stage1(pr):
                    j, p4, e2, e4 = pr // 4, pr % 4, pr % 2, pr % 4
                    u8, u8s = U8[j % 2], U8s[j % 2]
                    ph = psH[e2]
                    fns, fns2 = [], []
                    for g2 in range(2):
                        gl = 2 * p4 + g2
                        g = 2 * pr + g2
                        for ri in range(2):
                            fns.append(I("matmul", ph[g2 * 64:(g2 + 1) * 64, ri, :], lhsT=W3[ri][:, g, :], rhs=u8[:, gl, :], start=True, stop=True))
                            fns2.append(I("matmul", psS[g2 * 64:(g2 + 1) * 64, 2 + 2 * e4 + ri, :], lhsT=W3[ri][:, g, :], rhs=u8s[:, gl, :], start=True, stop=True))
                    P.op("pe", fns, reads=["W30", "W31", f"U8_{j % 2}_{2 * p4}", f"U8_{j % 2}_{2 * p4 + 1}"], writes=[f"psH{e2}/x"])
                    P.op("pe", fns2, reads=["W30", "W31", f"U8s_{j % 2}_{2 * p4}", f"U8s_{j % 2}_{2 * p4 + 1}"], writes=[f"psS/{2 + 2 * e4}", f"psS/{3 + 2 * e4}"])
                    P.op("act", I("copy", out=HA[e4][:, :, :], in_=ph[:, :, :]), reads=[f"psH{e2}/x"], writes=[f"HA{e4}r", f"HA{e4}i", f"HA{e4}h"])

                def stage2(prs):
                    st = {}
                    for pr in prs:
                        e4 = pr % 4
                        st[pr] = [HA[e4], HB[e4], f"HA{e4}", f"HB{e4}"]
                    for k in range(8):
                        d = 1 << k
                        for pr in prs:
                            src, dst, sn, dn = st[pr]
                            P.op("pool", I("tensor_copy", out=dst[:, :, 0:d], in_=src[:, :, 0:d]), reads=[sn + "r", sn + "i", sn + "h"], writes=[dn + "h"])
                        for pr in prs:
                            src, dst, sn, dn = st[pr]
                            lr = Lr[:, k, pr:pr + 1]
                            P.op("dve", I("scalar_tensor_tensor", out=dst[:, :, d:NBLK], in0=src[:, :, 0:NBLK - d], scalar=lr, in1=src[:, :, d:NBLK], op0=ALU.mult, op1=ALU.add),
                                 reads=[sn + "r", sn + "i", sn + "h", "Lr"], writes=[dn + "r", dn + "i"])
                        for pr in prs:
                            src, dst, sn, dn = st[pr]
                            nli = nLi[:, k, pr:pr + 1]
                            P.op("dve", I("scalar_tensor_tensor", out=dst[:, 0, d:NBLK], in0=src[:, 1, 0:NBLK - d], scalar=nli, in1=dst[:, 0, d:NBLK], op0=ALU.mult, op1=ALU.add),
                                 reads=[sn + "i", sn + "h", dn + "r", "nLi"], writes=[dn + "r"])
                        for pr in prs:
                            src, dst, sn, dn = st[pr]
                            li = Li[:, k, pr:pr + 1]
                            P.op("dve", I("scalar_tensor_tensor", out=dst[:, 1, d:NBLK], in0=src[:, 0, 0:NBLK - d], scalar=li, in1=dst[:, 1, d:NBLK], op0=ALU.mult, op1=ALU.add),
                                 reads=[sn + "r", sn + "h", dn + "i", "Li"], writes=[dn + "i"])
                        for pr in prs:
                            src, dst, sn, dn = st[pr]
                            st[pr] = [dst, src, dn, sn]
                    return {pr: (st[pr][0], st[pr][2]) for pr in prs}

                def stage3(pr, src, sn):
                    j, p4, e2, e4 = pr // 4, pr % 4, pr % 2, pr % 4
                    u8, u8s = U8[j % 2], U8s[j % 2]
                    hin = Hin[e4]
                    P.op("act", I("copy", out=hin[:, :, 1:NBLK], in_=src[:, :, 0:NBLK - 1]), reads=[sn + "r", sn + "i", sn + "h"], writes=[f"Hin{e4}"])
                    P.op("act", I("copy", out=HfinV[:, :, pr:pr + 1], in_=src[:, :, NBLK - 1:NBLK]), reads=[sn + "r", sn + "i", sn + "h"], writes=["HfinP"])
                    sr_, si_ = f"psS/{2 + 2 * e4}", f"psS/{3 + 2 * e4}"
                    P.op("act", I("copy", out=HlS[:, pr, :, :], in_=psS[:, 2 + 2 * e4:4 + 2 * e4, :]), reads=[sr_, si_], writes=["HlS"])
                    for g2 in range(2):
                        gl = 2 * p4 + g2
                        g = 2 * pr + g2
                        rows = slice(g2 * 64, g2 * 64 + 64)
                        py = psY[g2]
                        P.op("pe", [I("matmul", py[:, 0:NBLK], lhsT=W1[:, g, :], rhs=u8[:, gl, :], start=True, stop=False),
                                    I("matmul", py[:, 0:NBLK], lhsT=W2r[rows, pr, :], rhs=hin[rows, 0, :], start=False, stop=False),
                                    I("matmul", py[:, 0:NBLK], lhsT=W2i[rows, pr, :], rhs=hin[rows, 1, :], start=False, stop=True)],
                             reads=["W1", "W2r", "W2i", f"U8_{j % 2}_{gl}", f"Hin{e4}"], writes=[f"psY/{g2}"])
                        P.op("act", I("copy", out=Yg[:, gl, :], in_=py[:, 0:NBLK]), reads=[f"psY/{g2}"], writes=[f"Yg{gl}"])
                        P.op("pe", [I("matmul", psS[:, 10 + g2, :], lhsT=W1[:, g, :], rhs=u8s[:, gl, :], start=True, stop=False),
                                    I("matmul", psS[:, 10 + g2, :], lhsT=W2r[rows, pr, :], rhs=HsInb[rows, pr, 0, :], start=False, stop=False),
                                    I("matmul", psS[:, 10 + g2, :], lhsT=W2i[rows, pr, :], rhs=HsInb[rows, pr, 1, :], start=False, stop=True)],
                             reads=["W1", "W2r", "W2i", f"U8s_{j % 2}_{gl}", "HsInb"], writes=[f"psS/{10 + g2}"])
                        P.op("act", I("copy", out=Ygs[:, gl, :], in_=psS[:, 10 + g2, :]), reads=[f"psS/{10 + g2}"], writes=[f"Ygs{gl}"])

                def stageZ(j):
                    up, us = upv(j), usv(j)
                    allY = [f"Yg{gl}" for gl in range(8)]
                    allYs = [f"Ygs{gl}" for gl in range(8)]
                    zp = zT[:, j, 0:SEQ].rearrange("p (n s) -> p s n", s=TB)
                    zs = zT[:, j, SEQ:NT].rearrange("p (b t) -> p t b", t=4)
                    for t in range(8):
                        pz = psZ[t % 2]
                        P.op("pe", [I("matmul", pz[:, :], lhsT=selT[:, t, gl, :], rhs=Yg[:, gl, :], start=(gl == 0), stop=False) for gl in range(8)]
                             + [I("matmul", pz[:, :], lhsT=Dg[:, j, :], rhs=up[:, t, :], start=False, stop=True)],
                             reads=["selT", "Dg", "ufm"] + allY, writes=[f"psZ{t % 2}/x"])
                        P.op("act", I("activation", out=zp[:, t, :], in_=pz[:, :], func=AF.Gelu_apprx_tanh), reads=[f"psZ{t % 2}/x"], writes=["zT"])
                    for t in range(4, 8):
                        k2 = 12 + t % 2
                        P.op("pe", [I("matmul", psS[:, k2, :], lhsT=selT[:, t, gl, :], rhs=Ygs[:, gl, :], start=(gl == 0), stop=False) for gl in range(8)]
                             + [I("matmul", psS[:, k2, :], lhsT=Dg[:, j, :], rhs=us[:, t - 4, :], start=False, stop=True)],
                             reads=["selT", "Dg", "ufm"] + allYs, writes=[f"psS/{k2}"])
                        P.op("act", I("activation", out=zs[:, t - 4, :], in_=psS[:, k2, :], func=AF.Gelu_apprx_tanh), reads=[f"psS/{k2}"], writes=["zT"])

                NPR = 4 * int(os.environ.get("KS_J", "8"))
                groups = [(a, a + 1) for a in range(0, NPR, 2)]
                stageU(0)
                for pr in groups[0]:
                    stage1(pr)
                for gi, grp in enumerate(groups):
                    if gi + 1 < len(groups):
                        nxt = groups[gi + 1]
                        if nxt[0] % 4 == 0:
                            stageU(nxt[0] // 4)
                        for pr in nxt:
                            stage1(pr)
                    res = stage2(grp)
                    for pr in grp:
                        stage3(pr, res[pr][0], res[pr][1])
                    if grp[1] % 4 == 3:
                        stageZ(grp[1] // 4)

                def l8b(tab):
                    return tab[:, 0, :].rearrange("p (q o) -> p q o", o=1).broadcast_to([128, 32, NB])
                hr_, hi_ = HsIn[:, :, 0, :], HsIn[:, :, 1, :]
                for (ri_, a_, b_, tb_, opb) in ((0, hr_, hi_, nLi, ALU.add), (1, hi_, hr_, Li, ALU.add)):
                    P.op("dve", I("tensor_tensor", out=hs1[:, :, :], in0=a_, in1=l8b(Lr), op=ALU.mult), reads=["HsIn", "Lr", "HsOut"], writes=["hs1"])
                    P.op("pool", I("tensor_tensor", out=hs2[:, :, :], in0=b_, in1=l8b(tb_), op=ALU.mult), reads=["HsIn", "Li", "nLi", "HsOut"], writes=["hs2"])
                    P.op("dve", I("tensor_tensor", out=hs1[:, :, :], in0=hs1[:, :, :], in1=hs2[:, :, :], op=ALU.add), reads=["hs1", "hs2"], writes=["hs1"])
                    P.op("dve", I("tensor_tensor", out=HsOut[:, ri_, :, :].rearrange("p b q -> p q b"), in0=hs1[:, :, :], in1=HlS[:, :, ri_, :], op=ALU.add), reads=["hs1", "HlS"], writes=["HsOut"])

          if KS == "b":
              return
          with phase() as (sb, ps):
              env = mk_norm_env(sb, "c")
              gpost = load_gain(sb, "gpost", T["g_post_mix"][1:2, :])
              wa = load_w(sb, "wga", T["s5_w_glu_a"][0], 8, D)
              wb = load_w(sb, "wgb", T["s5_w_glu_b"][0], 8, D)
              xt = [sb(f"xt{i}", [128, D], F32) for i in range(2)]
              xo = [sb(f"xo{i}", [128, D], F32) for i in range(2)]
              sg_ = [sb(f"sg{i}", [128, D], F32) for i in range(2)]
              mt = [sb(f"mt{i}", [128, D], F32) for i in range(2)]
              psA_ = [ps(f"pga{i}", [128, D], F32) for i in range(2)]
              psB_ = [ps(f"pgb{i}", [128, D], F32) for i in range(2)]
              rows = [(T["xa_p"], r, 128, r) for r in range(0, SEQ, 128)] + [(T["xa_s"], 0, 64, SEQ)]
              for i, (xa, r, n, col) in enumerate(rows):
                  k = i % 2
                  load(xt[k][0:n, :], f"xt{k}", xa[r:r + n, :], sem=f"xt{k}")
                  for (pp, w_, nm) in ((psA_[k], wa, f"pga{k}"), (psB_[k], wb, f"pgb{k}")):
                      fns = []
                      for half in range(2):
                          for kc in range(8):
                              fns.append(I("matmul", pp[0:n, half * 512:(half + 1) * 512], lhsT=zT[:, kc, col:col + n], rhs=w_[:, kc, half * 512:(half + 1) * 512], start=(kc == 0), stop=(kc == 7)))
                      P.op("pe", fns, reads=["zT", ("wga" if w_ is wa else "wgb")], writes=[nm + "/x"])
                  P.op("act", I("activation", out=sg_[k][0:n, :], in_=psB_[k][0:n, :], func=AF.Sigmoid), reads=[f"pgb{k}/x"], writes=["sg0"])
                  P.op("dve", I("tensor_tensor", out=mt[k][0:n, :], in0=psA_[k][0:n, :], in1=sg_[k][0:n, :], op=ALU.mult), reads=[f"pga{k}/x", "sg0"], writes=["mt0"])
                  postnorm_res(env, mt[k][0:n, :], "mt0", n, gpost, "gpost", xt[k][0:n, :], f"xt{k}", xo[k][0:n, :], f"xo{k}", eng2="pool")
                  store(xa[r:r + n, :], xo[k][0:n, :], f"xo{k}", f"xo{k}")

        if KS == "c":
            return
        with phase() as (sb, ps):
            psF = ps("psF", [128, 128], F32)
            Fo = sb("Fo", [128, 128], F32)
            P.op("pe", I("transpose", out=psF[:, :], in_=HfinP[:, :], identity=idf[:, :]), reads=["HfinP", "idf"], writes=["psF/x"])
            P.op("dve", I("tensor_copy", out=Fo[:, :], in_=psF[:, :]), reads=["psF/x"], writes=["Fo"])
            store(T["s5p"].rearrange("ri pr q -> (ri pr) q"), Fo[0:64, :], "Fo", "s5p")
            psO = [ps(f"psO{i}", [128, 128], F32) for i in range(2)]
            Xo = [sb(f"Xo{i}", [128, 128], F32) for i in range(2)]
            k = 0
            for ri in range(2):
                for b4 in range(NB // 4):
                    po, xo_ = psO[k % 2], Xo[k % 2]
                    P.op("pe", I("transpose", out=po[:, :], in_=HsOut[:, ri, 4 * b4:4 * b4 + 4, :].rearrange("p b q -> p (b q)"), identity=idf[:, :]),
                         reads=["HsOut", "idf"], writes=[f"psO{k % 2}/x"])
                    if k % 2 == 0:
                        P.op("dve", I("tensor_copy", out=xo_[:, :], in_=po[:, :]), reads=[f"psO{k % 2}/x"], writes=[f"Xo{k % 2}"])
                    else:
                        P.op("act", I("copy", out=xo_[:, :], in_=po[:, :]), reads=[f"psO{k % 2}/x"], writes=[f"Xo{k % 2}"])
                    for i4 in range(4):
                        store(T["s5s"][4 * b4 + i4, ri, :].rearrange("(pr q) -> pr q", q=128), xo_[32 * i4:32 * i4 + 32, :], f"Xo{k % 2}", f"s5s{k % 2}")
                    k += 1

def build(stage=99):
    nc = bass.Bass("TRN2", target_bir_lowering=False)
    T = {}
    for k, shp in IN_SPECS.items():
        T[k] = nc.dram_tensor(k, shp, F32, kind="ExternalInput").ap()
    for k, (shp, dt_) in CONST_SPECS.items():
        T[k] = nc.dram_tensor(k, shp, dt_, kind="ExternalInput").ap()
    for k, shp in OUT_SPECS.items():
        T[k] = nc.dram_tensor(k, shp, F32, kind="ExternalOutput").ap()
    T["xa_p"] = nc.dram_tensor("xa_p", [SEQ, D], F32).ap()
    T["xa_s"] = nc.dram_tensor("xa_s", [NS, D], F32).ap()

    with contextlib.ExitStack() as ges:
        P = Prog(nc, ges)
        uid = [0]

        def load(dst, dname, src, eng="sp", sem=None, **kw):
            P.dma(eng, I("dma_start", out=dst, in_=src, **kw), sem or ("ld_" + dname), writes=[dname])

        def store(dst, src, sname, sem):
            P.dma("sp", I("dma_start", out=dst, in_=src), sem, reads=(sname if isinstance(sname, list) else [sname]))

        @contextlib.contextmanager
        def phase():
            with contextlib.ExitStack() as st:
                def sb(n, s, d):
                    uid[0] += 1
                    return st.enter_context(nc.sbuf_tensor(f"{n}_u{uid[0]}", s, d))

                def ps(n, s, d):
                    uid[0] += 1
                    return st.enter_context(nc.psum_tensor(f"{n}_u{uid[0]}", s, d))
                yield sb, ps
                P.barrier()
                if os.environ.get("K_MULTIBLOCK", "0") == "1":
                    P.flush()

        def mk_norm_env(sb, tag, nhn=2, ntmp=1):
            env = {"r": 0}
            env["junk"] = sb("junk" + tag, [128, D], F32)
            env["stat"] = sb("stat" + tag, [128, 4, 4], F32)
            env["eps"] = sb("eps" + tag, [128, 1], F32)
            env["hn"] = [sb(f"hn{tag}{i}", [128, D], BF16) for i in range(nhn)]
            env["tmps"] = [sb(f"ntmp{tag}{i}", [128, D], F32) for i in range(ntmp)]
            env["tk"] = 0
            env["ident"] = sb("ident" + tag, [128, 128], BF16)
            P.op("dve", I("memset", env["eps"][:], EPS), writes=["eps"])
            load(env["ident"][:], "ident", T["ident_bf"])
            return env

        def rstd_ops(env, n, src_ap, srcn, width):
            r = env["r"] = (env["r"] + 1) % 4
            st_ = env["stat"]
            P.op("act", I("activation", out=env["junk"][0:n, 0:width], in_=src_ap, func=AF.Square, accum_out=st_[0:n, r, 0:1]),
                 reads=(srcn if isinstance(srcn, list) else [srcn]), writes=[f"st{r}0"])
            P.op("act", I("activation", out=st_[0:n, r, 1:2], in_=st_[0:n, r, 0:1], func=AF.Sqrt, scale=1.0 / width, bias=env["eps"][0:n, 0:1]),
                 reads=[f"st{r}0", "eps"], writes=[f"st{r}1"])
            P.op("dve", I("reciprocal", out=st_[0:n, r, 2:3], in_=st_[0:n, r, 1:2]), reads=[f"st{r}1"], writes=[f"st{r}2"])
            return st_[0:n, r, 2:3], f"st{r}2"

        def prenorm_a(env, x_ap, xn, n, gain, gn):
            rs, rsn = rstd_ops(env, n, x_ap, xn, D)
            k = env["hk"] = (env.get("hk", -1) + 1) % len(env["hn"])
            hn = env["hn"][k]
            P.op("dve", I("scalar_tensor_tensor", out=hn[0:n, :], in0=x_ap, scalar=rs, in1=gain[0:n, :], op0=ALU.mult, op1=ALU.mult),
                 reads=[xn, rsn, gn], writes=[f"hn{k}"])
            return k

        def prenorm_b(env, k, psT, psTn, n, dst, dstn):
            hn = env["hn"][k]
            idt = env["ident"]
            P.op("pe", [I("transpose", out=psT[:, kc, 0:n], in_=hn[0:n, kc * 128:(kc + 1) * 128], identity=idt[0:n, 0:n]) for kc in range(8)],
                 reads=[f"hn{k}", "ident"], writes=[psTn])
            P.op("dve", I("tensor_copy", out=dst, in_=psT[:, :, 0:n]), reads=[psTn], writes=[dstn])

        def prenorm(env, psT, psTn, x_ap, xn, n, gain, gn, dst, dstn):
            k = prenorm_a(env, x_ap, xn, n, gain, gn)
            prenorm_b(env, k, psT, psTn, n, dst, dstn)

        def postnorm_res(env, m_ap, mn, n, gain, gn, x_ap, xn, out_ap, outn, eng2="dve"):
            rs, rsn = rstd_ops(env, n, m_ap, mn, D)
            tk = env["tk"] = (env["tk"] + 1) % len(env["tmps"])
            tmp = env["tmps"][tk]
            P.op("dve", I("scalar_tensor_tensor", out=tmp[0:n, :], in0=m_ap, scalar=rs, in1=gain[0:n, :], op0=ALU.mult, op1=ALU.mult),
                 reads=(mn if isinstance(mn, list) else [mn]) + [rsn, gn], writes=[f"ntmp{tk}"])
            P.op(eng2, I("tensor_tensor", out=out_ap, in0=tmp[0:n, :], in1=x_ap, op=ALU.add), reads=[f"ntmp{tk}", xn], writes=[outn])

        def load_gain(sb, name, src_row):
            t = sb(name, [128, D], F32)
            load(t[:], name, src_row.partition_broadcast(128))
            return t

        def load_w(sb, name, src, kchunks, ncols):
            t = sb(name, [128, kchunks, ncols], BF16)
            half = max(1, kchunks // 2)
            for c0 in range(0, kchunks, half):
                load(t[:, c0:c0 + half, :], name, src[c0 * 128:(c0 + half) * 128, :].rearrange("(c p) n -> p c n", p=128), eng="pool")
            return t

        def gla_core(env, C, NCH, ntok, maskT, m01, x_rows_src, x_rows_dst, sample, W, x_rows_src2=None, x_rows_dst2=None, pre_done=False, next_src=None):
            win, wout, wg, nbg, gpre, gpost, ghead = W["win"], W["wout"], W["wg"], W["nbg"], W["gpre"], W["gpost"], W["ghead"]
            B = W["banks"]
            HN = H * NCH
            nct = ntok // 64
            tl = W["tiles"]
            hT, qT, kT, spt, csp, dtmp, etmp = tl["hT"], tl["qT"], tl["kT"], tl["sp"], tl["csp"], tl["dtmp"], tl["etmp"]
            qin, kin, qe, kdec, glT, vt, srt, decay = tl["qin"], tl["kin"], tl["qe"], tl["kdec"], tl["glT"], tl["v"], tl["sr"], tl["decay"]
            psT = B["psT"]
            def pre_chunk(c, src_fn):
                xt = tl["xt"][c % 4]
                load(xt[0:64, :], f"xt{c % 4}", src_fn(c), sem=f"xt{c % 4}")
                k_ = prenorm_a(env, xt[0:64, :], f"xt{c % 4}", 64, gpre, "gpre")
                prenorm_b(env, k_, psT, "bq0/psT", 64, hT[:, :, c * 64:(c + 1) * 64], f"hT{c}")

            if not pre_done:
                ks = []
                for c in range(nct):
                    xt = tl["xt"][c % 4]
                    load(xt[0:64, :], f"xt{c % 4}", x_rows_src(c), sem=f"xt{c % 4}")
                    ks.append(prenorm_a(env, xt[0:64, :], f"xt{c % 4}", 64, gpre, "gpre"))
                for c in range(nct):
                    prenorm_b(env, ks[c], psT, "bq0/psT", 64, hT[:, :, c * 64:(c + 1) * 64], f"hT{c}")
            CUT = os.environ.get("KG_CUT", "Z")
            if CUT == "A":
                return
            for j in range(8):
                pq = B["psq"][j % 2]
                P.op("pe", [I("matmul", pq[:, 0:ntok], lhsT=win[:, kc, j * 128:(j + 1) * 128], rhs=hT[:, kc, 0:ntok], start=(kc == 0), stop=(kc == 7)) for kc in range(8)],
                     reads=["win_qk"] + [f"hT{c_}" for c_ in range(nct)], writes=[f"bq{j % 2}/psq"])
                if j < 4:
                    P.op("act", I("mul", out=qT[:, j, 0:ntok], in_=pq[:, 0:ntok], mul=DK ** -0.5), reads=[f"bq{j % 2}/psq"], writes=["qT"])
                else:
                    P.op("dve", I("tensor_copy", out=kT[:, j - 4, 0:ntok], in_=pq[:, 0:ntok]), reads=[f"bq{j % 2}/psq"], writes=["kT"])
            if CUT == "B1":
                return
            pq = B["psq"][0]
            P.op("pe", [I("matmul", pq[0:16, 0:ntok], lhsT=win[:, kc, 3072:3088], rhs=hT[:, kc, 0:ntok], start=(kc == 0), stop=(kc == 7)) for kc in range(8)],
                 reads=["win_gl"] + [f"hT{c_}" for c_ in range(nct)], writes=["bq0/psq"])
            P.op("act", I("copy", out=glT[0:16, 0:ntok], in_=pq[0:16, 0:ntok]), reads=["bq0/psq"], writes=["glT"])
            if CUT == "B2":
                return
            for h in range(H):
                k2 = (h + 1) % 2
                pq = B["psq"][k2]
                P.op("pe", I("matmul", pq[:, 0:ntok], lhsT=wg[0:16, h * 128:(h + 1) * 128], rhs=glT[0:16, 0:ntok], start=True, stop=True),
                     reads=["wg", "glT"], writes=[f"bq{k2}/psq"])
                P.op("act", I("activation", out=spt[:, h, 0:ntok], in_=pq[:, 0:ntok], func=AF.Exp, scale=-1.0, bias=nbg[:, h:h + 1]),
                     reads=[f"bq{k2}/psq", "nbg"], writes=["sp"])
            if CUT == "B3":
                return
            P.op("act", I("activation", out=spt[:, :, 0:ntok], in_=spt[:, :, 0:ntok], func=AF.Ln, bias=1.0), reads=["sp"], writes=["sp"])
            if CUT == "B4":
                return
            def emit_piece(c, piece):
                pv = B["psv"][piece % 2]
                P.op("pe", [I("matmul", pv[0:64, :], lhsT=hT[:, kc, c * 64:(c + 1) * 64], rhs=win[:, kc, 1024 + piece * 512:1536 + piece * 512], start=(kc == 0), stop=(kc == 7)) for kc in range(8)],
                     reads=["win_vr", f"hT{c}"], writes=[f"bv{piece % 2}/psv"])
                if piece < 2:
                    P.op("dve", I("tensor_copy", out=vt[0:64, c, piece * 512:(piece + 1) * 512], in_=pv[0:64, :]), reads=[f"bv{piece % 2}/psv"], writes=["v"])
                else:
                    sl = slice((piece - 2) * 512, (piece - 1) * 512)
                    P.op("act", I("activation", out=srt[0:64, c, sl], in_=pv[0:64, :], func=AF.Silu), reads=[f"bv{piece % 2}/psv"], writes=["sr"])
                    P.op("pool", I("tensor_tensor", out=srt[0:64, c, sl], in0=srt[0:64, c, sl], in1=ghead[0:64, sl], op=ALU.mult), reads=["sr", "ghead"], writes=["sr"])

            b2_ops = [(lambda c=c, piece=piece: emit_piece(c, piece)) for c in range(nct) for piece in (0, 1)]
            if CUT == "B":
                for f in b2_ops:
                    f()
                return

            def v3(t):
                return t[:, :, 0:ntok].rearrange("p h (c t) -> p (h c) t", t=C)
            n_el = H * ntok
            cv = v3(csp)
            EX = lambda src_, sc_: I("activation", out=etmp[:, :, 0:ntok], in_=src_[:, :, 0:ntok], func=AF.Exp, scale=sc_)
            MUL = lambda dst_, a_: I("tensor_tensor", out=dst_[:, :, 0:ntok], in0=a_[:, :, 0:ntok], in1=etmp[:, :, 0:ntok], op=ALU.mult)
            c_ops = [
                lambda: P.op("dve", I("tensor_tensor_scan", out=csp[:, :, :].rearrange("p h t -> p (h t)"), data0=m01[:, 0:n_el], data1=spt[:, :, :].rearrange("p h t -> p (h t)"), initial=0.0, op0=ALU.mult, op1=ALU.add),
                             reads=["sp", "m01"], writes=["csp"]),
                lambda: P.op("dve", I("tensor_tensor", out=v3(dtmp), in0=cv, in1=cv[:, :, C // 2:C // 2 + 1].broadcast_to([128, HN, C]), op=ALU.subtract), reads=["csp"], writes=["dtmp"]),
                lambda: P.op("act", EX(dtmp, -1.0 / 16.0), reads=["dtmp"], writes=["etmp"]),
                lambda: P.op("dve", MUL(qin, qT), reads=["qT", "etmp"], writes=["qin"]),
                lambda: P.op("act", EX(dtmp, 1.0 / 16.0), reads=["dtmp"], writes=["etmp"]),
                lambda: P.op("dve", MUL(kin, kT), reads=["kT", "etmp"], writes=["kin"]),
                lambda: P.op("act", EX(csp, -1.0 / 16.0), reads=["csp"], writes=["etmp"]),
                lambda: P.op("dve", MUL(qe, qT), reads=["qT", "etmp"], writes=["qe"]),
                lambda: P.op("pool", I("tensor_copy", out=decay[:, 0:HN, :], in_=v3(etmp)[:, :, C - 1:C]), reads=["etmp"], writes=["decay"]),
                lambda: P.op("dve", I("tensor_tensor", out=v3(dtmp), in0=cv, in1=cv[:, :, C - 1:C].broadcast_to([128, HN, C]), op=ALU.subtract), reads=["csp"], writes=["dtmp"]),
                lambda: P.op("act", EX(dtmp, 1.0 / 16.0), reads=["dtmp"], writes=["etmp"]),
                lambda: P.op("dve", MUL(kdec, kT), reads=["kT", "etmp"], writes=["kdec"]),
            ]
            bi = ci = 0
            while bi < len(b2_ops) or ci < len(c_ops):
                if bi < len(b2_ops):
                    b2_ops[bi]()
                    bi += 1
                if ci < len(c_ops):
                    c_ops[ci]()
                    ci += 1
            if CUT == "C":
                return
            S, Sbf = W["S"], W["Sbf"]
            att, kds, sq, ssum, ogb, oT, xr, xo = tl["att"], tl["kds"], tl["sq"], tl["ssum"], tl["ogb"], tl["oT"], tl["xr"], tl["xo"]
            psa, psK, pso, psu, psTo, psm = B["psa"], B["psK"], B["pso"], B["psu"], B["psTo"], B["psm"]
            idt = env["ident"]
            osb = tl["osb"]

            def og_stage(c):
                ob = ogb[c % 2]
                P.op("act", I("activation", out=sq[0:64, :], in_=osb[0:64, :], func=AF.Square), reads=["osb"], writes=["sq"])
                P.op("dve", I("tensor_reduce", out=ssum[0:64, 0, :], in_=sq[0:64, :].rearrange("p (h v) -> p h v", h=H), axis=AX.X, op=ALU.add), reads=["sq"], writes=["ssum0"])
                P.op("act", I("activation", out=ssum[0:64, 1, :], in_=ssum[0:64, 0, :], func=AF.Sqrt, scale=1.0 / DV, bias=env["eps"][0:64, 0:1]), reads=["ssum0", "eps"], writes=["ssum1"])
                P.op("dve", I("reciprocal", out=ssum[0:64, 2, :], in_=ssum[0:64, 1, :]), reads=["ssum1"], writes=["ssum2"])
                for h in range(H):
                    P.op("dve", I("scalar_tensor_tensor", out=ob[0:64, h * DV:(h + 1) * DV], in0=osb[0:64, h * DV:(h + 1) * DV], scalar=ssum[0:64, 2, h:h + 1], in1=srt[0:64, c, h * DV:(h + 1) * DV], op0=ALU.mult, op1=ALU.mult),
                         reads=["osb", "ssum2", "sr"], writes=[f"ogb{c % 2}"])

            def op_stage(cs):
                n = 64 * len(cs)
                fns = []
                for i_, c in enumerate(cs):
                    ob = ogb[c % 2]
                    for kc in range(8):
                        fns.append(I("transpose", out=psTo[:, kc, 64 * i_:64 * i_ + 64], in_=ob[0:64, kc * 128:(kc + 1) * 128], identity=idt[0:64, 0:64]))
                P.op("pe", fns, reads=[f"ogb{c % 2}" for c in cs] + ["ident"], writes=["b0/psTo"])
                P.op("act", I("copy", out=oT[:, :, 0:n], in_=psTo[:, :, 0:n]), reads=["b0/psTo"], writes=["oT"])
                fns = []
                for half in range(2):
                    for kc in range(8):
                        fns.append(I("matmul", psm[0:n, half * 512:(half + 1) * 512], lhsT=oT[:, kc, 0:n], rhs=wout[:, kc, half * 512:(half + 1) * 512], start=(kc == 0), stop=(kc == 7)))
                P.op("pe", fns, reads=["oT", "wout"], writes=["bv0/psv", "bv1/psv"])
                k2 = (cs[0] // 2) % 2
                xr_, xo_ = xr[0], xo[k2]
                src_ap, dst_ap = x_rows_src2(cs[0], n), x_rows_dst2(cs[0], n)
                load(xr_[0:n, :], "xr0", src_ap, sem="xr0")
                postnorm_res(env, psm[0:n, :], ["bv0/psv", "bv1/psv"], n, gpost, "gpost", xr_[0:n, :], "xr0", xo_[0:n, :], f"xo{k2}", eng2="pool")
                store(dst_ap, xo_[0:n, :], f"xo{k2}", f"xo{k2}")

            def p1_stage(c):
                c0, c1 = c * 64, (c + 1) * 64
                k = c % 2
                for h in range(H):
                    P.op("pe", I("matmul", psa[0:64, h * 64:(h + 1) * 64], lhsT=kin[:, h, c0:c1], rhs=qin[:, h, c0:c1], start=True, stop=True), reads=["kin", "qin"], writes=[f"bq0/psa{h}"])
                    P.op("pe", I("transpose", out=psK[0:64, h * 128:(h + 1) * 128], in_=kdec[:, h, c0:c1], identity=idt[:, :]), reads=["kdec", "ident"], writes=[f"bq1/psK{h}"])
                for h in range(H):
                    P.op("dve", I("tensor_tensor", out=att[k * H + h][0:64, :], in0=psa[0:64, h * 64:(h + 1) * 64], in1=maskT[0:64, :], op=ALU.mult), reads=[f"bq0/psa{h}", "maskT"], writes=[f"att{k}_{h}"])
                    P.op("act", I("copy", out=kds[k * H + h][0:64, :], in_=psK[0:64, h * 128:(h + 1) * 128]), reads=[f"bq1/psK{h}"], writes=[f"kds{k}_{h}"])

            p1_stage(0)
            for c in range(nct):
                c0, c1 = c * 64, (c + 1) * 64
                if c + 1 < nct:
                    p1_stage(c + 1)
                for h in range(H):
                    k = c % 2
                    att_h, kds_h = att[k * H + h], kds[k * H + h]
                    attn, kdsn = f"att{k}_{h}", f"kds{k}_{h}"
                    vh = vt[0:64, c, h * DV:(h + 1) * DV]
                    if not sample:
                        P.op("pe", [I("matmul", pso[0:64, h, :], lhsT=att_h[0:64, :], rhs=vh, start=True, stop=False),
                                    I("matmul", pso[0:64, h, :], lhsT=qe[:, h, c0:c1], rhs=Sbf[:, h, :], start=False, stop=True)],
                             reads=[attn, "v", "qe", f"Sbf{h}"], writes=[f"bo{h // 2}/pso{h}"])
                        pu = psu[h % 2]
                        P.op("pe", I("matmul", pu[:, :], lhsT=kds_h[0:64, :], rhs=vh, start=True, stop=True), reads=[kdsn, "v"], writes=[f"bu/psu{h % 2}"])
                        di = h * NCH + c
                        P.op("dve", I("scalar_tensor_tensor", out=S[:, h, :], in0=S[:, h, :], scalar=decay[:, di, 0:1], in1=pu[:, :], op0=ALU.mult, op1=ALU.add),
                             reads=[f"S{h}", "decay", f"bu/psu{h % 2}"], writes=[f"S{h}"])
                        P.op("act", I("copy", out=Sbf[:, h, :], in_=S[:, h, :]), reads=[f"S{h}"], writes=[f"Sbf{h}"])
                    else:
                        S0, S0bf, Sn, qeM, kdM = W["S0"], W["S0bf"], W["Sn"], W["qeM"], W["kdM"]
                        colmask, rowmask = W["colmask"], W["rowmask"]
                        HB_ = NB // 2
                        for hf in range(2):
                            bs = slice(hf * HB_, (hf + 1) * HB_)
                            load(S0[:, bs, :], f"S0_{hf}", T["sg"][bs, h, :, :].rearrange("b k v -> k b v"), sem=f"S0_{hf}")
                        P.op("dve", I("tensor_tensor", out=qeM[:, :, :], in0=colmask[:, :, :], in1=qe[:, h:h + 1, 0:64].broadcast_to([128, NB, 64]), op=ALU.mult),
                             reads=["qe", "colmask"], writes=["qeM"])
                        P.op("dve", I("tensor_tensor", out=kdM[0:64, :, :], in0=kds_h[0:64, :].rearrange("p (o k) -> p o k", o=1).broadcast_to([64, NB, 128]),
                                      in1=rowmask[0:64, :].rearrange("p (b o) -> p b o", o=1).broadcast_to([64, NB, 128]), op=ALU.mult),
                             reads=[kdsn, "rowmask"], writes=["kdM"])
                        P.op("pe", I("matmul", pso[0:64, h, :], lhsT=att_h[0:64, :], rhs=vh, start=True, stop=False), reads=[attn, "v"], writes=[f"bo{h // 2}/pso{h}"])
                        for hf in range(2):
                            bs = slice(hf * HB_, (hf + 1) * HB_)
                            P.op("act", I("copy", out=S0bf[:, bs, :], in_=S0[:, bs, :]), reads=[f"S0_{hf}"], writes=[f"S0bf_{hf}"])
                            fns = []
                            for b in range(hf * HB_, (hf + 1) * HB_):
                                fns.append(I("matmul", pso[0:64, h, :], lhsT=qeM[:, b, :], rhs=S0bf[:, b, :], start=False, stop=(b == NB - 1)))
                            P.op("pe", fns, reads=["qeM", f"S0bf_{hf}"], writes=[f"bo{h // 2}/pso{h}"])
                            for b in range(hf * HB_, (hf + 1) * HB_):
                                pu = psu[b % 2]
                                P.op("pe", I("matmul", pu[:, :], lhsT=kdM[0:64, b, :], rhs=vh, start=True, stop=True), reads=["kdM", "v"], writes=[f"bu/psu{b % 2}"])
                                di = h * NCH + b
                                P.op("dve", I("scalar_tensor_tensor", out=Sn[:, b, :], in0=S0[:, b, :], scalar=decay[:, di, 0:1], in1=pu[:, :], op0=ALU.mult, op1=ALU.add),
                                     reads=[f"S0_{hf}", "decay", f"bu/psu{b % 2}"], writes=[f"Sn_{hf}"])
                            store(T["glas"][bs, h, :, :].rearrange("b k v -> k b v"), Sn[:, bs, :], f"Sn_{hf}", f"glas{hf}")
                if CUT == "D":
                    continue
                emit_piece(c, 2)
                emit_piece(c, 3)
                allpso = [f"bo{h // 2}/pso{h}" for h in range(H)]
                P.op("act", I("copy", out=osb[0:64, :], in_=pso[0:64, :, :].rearrange("p h v -> p (h v)")), reads=allpso, writes=["osb"])
                if c >= 2 and c % 2 == 0:
                    op_stage([c - 2, c - 1])
                og_stage(c)
                if next_src is not None:
                    pre_chunk(c, next_src)
            if CUT != "D":
                if nct % 2 == 0:
                    op_stage([nct - 2, nct - 1])
                else:
                    op_stage([nct - 1])

        def gla_phase():
            with phase() as (sb, ps):
                env = mk_norm_env(sb, "g", nhn=4)
                W = {}
                win_t = sb("win", [128, 8, GIN], BF16)
                for c0 in (0, 4):
                    P.dma("pool", I("dma_start", out=win_t[:, c0:c0 + 4, :], in_=T["gla_w_in"][0][c0 * 128:(c0 + 4) * 128, :].rearrange("(c p) n -> p c n", p=128)), "ld_win", writes=["win_qk", "win_gl", "win_vr"])
                W["win"] = win_t
                W["wout"] = load_w(sb, "wout", T["gla_w_out"][0], 8, D)
                wg = sb("wg", [16, 512], BF16)
                load(wg[:, :], "wg", T["gla_w_gate_up"][0], eng="pool")
                W["wg"] = wg
                nbg = sb("nbg", [128, H], F32)
                load(nbg[:, :], "nbg", T["gla_b_gate"].rearrange("o (h p) -> p (o h)", p=128), allow_slow_non_contiguous=True)
                P.op("dve", I("tensor_scalar", out=nbg[:, :], in0=nbg[:, :], scalar1=-1.0, scalar2=None, op0=ALU.mult), reads=["nbg"], writes=["nbg"])
                W["nbg"] = nbg
                W["gpre"] = load_gain(sb, "gpre", T["g_pre_mix"][0:1, :])
                W["gpost"] = load_gain(sb, "gpost", T["g_post_mix"][0:1, :])
                W["ghead"] = load_gain(sb, "ghead", T["gla_g_head"][0:1, :])
                b23 = ps("b23", [128, 1024], F32)
                b67 = ps("b67", [128, H, DV], F32)
                bA = ps("bA", [128, 8, 128], BF16)
                bq0 = ps("bq0", [128, 512], F32)
                bq1 = ps("bq1", [128, 512], F32)
                bE = ps("bE", [128, 512], F32)
                bq0b = bq0.bitcast(BF16)
                bq1b = bq1.bitcast(BF16)
                W["banks"] = {"psT": bq0b[:, 512:1024].rearrange("p (k t) -> p k t", k=8), "psTo": bA, "psq": [bq0[:, 0:256], bq1[:, 0:256]],
                              "psv": [b23[:, 0:512], b23[:, 512:1024]], "psm": b23, "psu": [bE[:, 0:256], bE[:, 256:512]],
                              "psa": bq0[:, 0:256], "psK": bq1b[:, 0:512], "pso": b67}
                m01p = sb("m01p", [128, 1024], F32)
                load(m01p[:, :], "m01", T["m01p"])
                maskTp = sb("maskTp", [64, 64], F32)
                load(maskTp[:, :], "maskT", T["maskT_p"])
                S = sb("S", [128, H, DV], F32)
                Sbf = sb("Sbf", [128, H, DV], BF16)
                for h in range(H):
                    P.op("dve", I("memset", S[:, h, :], 0.0), writes=[f"S{h}"])
                    P.op("pool", I("memset", Sbf[:, h, :], 0.0), writes=[f"Sbf{h}"])
                W["S"], W["Sbf"] = S, Sbf

                def mk_tiles(sbx, ntok, HNmax):
                    tl = {}
                    tl["xt"] = [sbx(f"xt{i}", [64, D], F32) for i in range(4)] if ntok > 64 else [sbx("xt0", [64, D], F32)] * 4
                    tl["xr"] = [sbx("xr0", [128, D], F32)] * 2
                    tl["xo"] = [sbx(f"xo{i}", [128, D], F32) for i in range(2)] if ntok > 64 else [sbx("xo0", [128, D], F32)] * 2
                    tl["hT"] = sbx("hT", [128, 8, ntok], BF16)
                    for nme in ("qT", "kT", "sp", "csp", "dtmp", "etmp"):
                        tl[nme] = sbx(nme, [128, H, ntok], F32)
                    for nme in ("qin", "kin", "qe", "kdec"):
                        tl[nme] = sbx(nme, [128, H, ntok], BF16)
                    tl["glT"] = sbx("glT", [16, ntok], BF16)
                    tl["v"] = sbx("v", [64, ntok // 64, D], BF16)
                    tl["sr"] = sbx("sr", [64, ntok // 64, D], BF16)
                    tl["decay"] = sbx("decay", [128, HNmax, 1], F32)
                    tl["att"] = [sbx(f"att{i}", [64, 64], BF16) for i in range(2 * H)]
                    tl["kds"] = [sbx(f"kds{i}", [64, 128], BF16) for i in range(2 * H)]
                    tl["sq"] = sbx("sq", [64, D], F32)
                    tl["ssum"] = sbx("ssum", [64, 3, H], F32)
                    tl["osb"] = sbx("osb", [64, D], F32)
                    tl["ogb"] = [sbx(f"ogb{i}", [64, D], BF16) for i in range(2)]
                    tl["oT"] = sbx("oT", [128, 8, 128], BF16)
                    return tl

                with contextlib.ExitStack() as st2:
                    def sb2(n, s, d, st2=st2):
                        uid[0] += 1
                        return st2.enter_context(nc.sbuf_tensor(f"{n}_u{uid[0]}", s, d))
                    W["tiles"] = mk_tiles(sb2, 256, 16)
                    NTL = int(os.environ.get("KG_TILES", SEQ // 256))
                    for t in range(NTL):
                        nsrc = (lambda c, t=t: T["xp"][(t + 1) * 256 + c * 64:(t + 1) * 256 + (c + 1) * 64, :]) if t + 1 < NTL else None
                        gla_core(env, 64, 4, 256, maskTp, m01p,
                                 lambda c, t=t: T["xp"][t * 256 + c * 64:t * 256 + (c + 1) * 64, :],
                                 lambda c, t=t: T["xa_p"][t * 256 + c * 64:t * 256 + (c + 1) * 64, :], False, W,
                                 lambda c, n, t=t: T["xp"][t * 256 + c * 64:t * 256 + c * 64 + n, :],
                                 lambda c, n, t=t: T["xa_p"][t * 256 + c * 64:t * 256 + c * 64 + n, :],
                                 pre_done=(t > 0), next_src=nsrc)
                    store(T["glap"].rearrange("h k v -> k h v"), S[:, :, :], [f"S{h}" for h in range(H)], "glap")
                    P.barrier()
                    if os.environ.get("K_MULTIBLOCK", "0") == "1":
                        P.flush()
                with contextlib.ExitStack() as st2:
                  if os.environ.get("KG_SAMPLE", "1") == "1":
                    def sb2(n, s, d, st2=st2):
                        uid[0] += 1
                        return st2.enter_context(nc.sbuf_tensor(f"{n}_u{uid[0]}", s, d))
                    W["tiles"] = mk_tiles(sb2, 64, 64)
                    m01s = sb2("m01s", [128, 256], F32)
                    load(m01s[:, :], "m01", T["m01s"])
                    maskTs = sb2("maskTs", [64, 64], F32)
                    load(maskTs[:, :], "maskT", T["maskT_s"])
                    W["colmask"] = sb2("colmask", [128, NB, 64], F32)
                    load(W["colmask"][:, :, :], "colmask", T["colmask"])
                    W["rowmask"] = sb2("rowmask", [64, NB], F32)
                    load(W["rowmask"][:, :], "rowmask", T["rowmask"])
                    W["S0"] = sb2("S0", [128, NB, DV], F32)
                    W["S0bf"] = sb2("S0bf", [128, NB, DV], BF16)
                    W["Sn"] = sb2("Sn", [128, NB, DV], F32)
                    W["qeM"] = sb2("qeM", [128, NB, 64], BF16)
                    W["kdM"] = sb2("kdM", [64, NB, 128], BF16)
                    gla_core(env, 4, 16, 64, maskTs, m01s, lambda c: T["xs"][0:64, :], lambda c: T["xa_s"][0:64, :], True, W,
                             lambda c, n: T["xs"][0:n, :], lambda c, n: T["xa_s"][0:n, :])
                    P.barrier()
                    if os.environ.get("K_MULTIBLOCK", "0") == "1":
                        P.flush()

        def mlp_phase(layer, src_p, src_s, dst_p, dst_s):
            NT = SEQ + NS
            NBF = FF // 512
            with phase() as (sb, ps):
                env = mk_norm_env(sb, "m", nhn=2, ntmp=2)
                gpre = load_gain(sb, "gpre", T["g_pre_mlp"][layer:layer + 1, :])
                gpost = load_gain(sb, "gpost", T["g_post_mlp"][layer:layer + 1, :])
                hT = sb("hT", [128, 8, NT], BF16)
                facc = sb("facc", [128, 17, D], F32)
                aT = [sb(f"aT{i}", [128, 4, NT], BF16) for i in range(2)]
                wu = [sb(f"wu{i}", [128, 8, 512], BF16) for i in range(2)]
                wd = sb("wd", [128, 4, D], BF16)
                xt = [sb(f"xt{i}", [128, D], F32) for i in range(2)]
                xo = [sb(f"xo{i}", [128, D], F32) for i in range(2)]
                ar = [sb(f"ar{i}", [128, 512], BF16) for i in range(2)]
                psd = [ps(f"psd{i}", [128, D], F32) for i in range(2)]
                psT = ps("psT", [128, 8, 128], BF16)
                psu = [ps(f"psu{i}", [128, 512], F32) for i in range(2)]
                tiles = [(src_p, dst_p, r, 128, r) for r in range(0, SEQ, 128)] + [(src_s, dst_s, 0, NS, SEQ)]
                blocks = [(t0, 512) for t0 in range(0, SEQ, 512)] + [(SEQ, NS)]

                def load_wu(b):
                    for k0 in (0, 4):
                        load(wu[b % 2][:, k0:k0 + 4, :], f"wu{b % 2}", T["w_up"][layer][k0 * 128:(k0 + 4) * 128, b * 512:(b + 1) * 512].rearrange("(c p) n -> p c n", p=128), eng="pool", sem=f"wu{b % 2}")

                def load_wd(b):
                    load(wd[:, :, :], "wd", T["w_down"][layer][b * 512:(b + 1) * 512, :].rearrange("(c p) n -> p c n", p=128), eng="pool", sem="wd")

                def pre_block(bi_):
                    t0, n = blocks[bi_]
                    its = [tt for tt in tiles if t0 <= tt[4] < t0 + n]
                    for p0 in range(0, len(its), 2):
                        ks = []
                        for i_, (src, dst, r, m, col) in enumerate(its[p0:p0 + 2]):
                            load(xt[i_][0:m, :], f"xt{i_}", src[r:r + m, :], sem=f"xt{i_}")
                            ks.append(prenorm_a(env, xt[i_][0:m, :], f"xt{i_}", m, gpre, "gpre"))
                        for i_, (src, dst, r, m, col) in enumerate(its[p0:p0 + 2]):
                            prenorm_b(env, ks[i_], psT, "psT/x", m, hT[:, :, col:col + m], f"hT{bi_}")

                load_wu(0)
                load_wd(0)
                load_wu(1)
                ku = [0]

                def up_tb(b, bi_):
                    t0, n = blocks[bi_]
                    a_b = aT[b % 2]
                    for fc in range(4):
                        k_ = ku[0] % 2
                        pu, a_ = psu[k_], ar[k_]
                        P.op("pe", [I("matmul", pu[:, 0:n], lhsT=wu[b % 2][:, kc, fc * 128:(fc + 1) * 128], rhs=hT[:, kc, t0:t0 + n], start=(kc == 0), stop=(kc == 7)) for kc in range(8)],
                             reads=[f"wu{b % 2}", f"hT{bi_}"], writes=[f"psu{k_}/x"])
                        P.op("act", I("activation", out=a_[:, 0:n], in_=pu[:, 0:n], func=AF.Relu), reads=[f"psu{k_}/x"], writes=[f"ar{k_}"])
                        P.op("dve", I("tensor_tensor", out=a_b[:, fc, t0:t0 + n], in0=a_[:, 0:n], in1=a_[:, 0:n], op=ALU.mult), reads=[f"ar{k_}"], writes=[f"aT{b % 2}_{bi_}"])
                        ku[0] += 1

                def down_tb(b, bi_):
                    t0, n = blocks[bi_]
                    a_b = aT[b % 2]
                    for i_, (src, dst, r, m, col) in enumerate(tiles):
                        if not (t0 <= col < t0 + n):
                            continue
                        pd = psd[i_ % 2]
                        fns = []
                        for half in range(2):
                            for fc in range(4):
                                fns.append(I("matmul", pd[0:m, half * 512:(half + 1) * 512], lhsT=a_b[:, fc, col:col + m], rhs=wd[:, fc, half * 512:(half + 1) * 512], start=(fc == 0), stop=(fc == 3)))
                        P.op("pe", fns, reads=[f"aT{b % 2}_{bi_}", "wd"], writes=[f"psd{i_ % 2}/x"])
                        if b == 0:
                            P.op("act", I("copy", out=facc[0:m, i_, :], in_=pd[0:m, :]), reads=[f"psd{i_ % 2}/x"], writes=[f"f{i_}"])
                        else:
                            P.op("dve", I("tensor_tensor", out=facc[0:m, i_, :], in0=pd[0:m, :], in1=facc[0:m, i_, :], op=ALU.add), reads=[f"psd{i_ % 2}/x", f"f{i_}"], writes=[f"f{i_}"])
                        if b == NBF - 1:
                            xi = i_ % 2
                            load(xt[xi][0:m, :], f"xt{xi}", src[r:r + m, :], sem=f"xt{xi}")
                            postnorm_res(env, facc[0:m, i_, :], f"f{i_}", m, gpost, "gpost", xt[xi][0:m, :], f"xt{xi}", xo[i_ % 2][0:m, :], f"xo{i_ % 2}", eng2="pool")
                            store(dst[r:r + m, :], xo[i_ % 2][0:m, :], f"xo{i_ % 2}", f"xo{i_ % 2}")

                nb_ = len(blocks)
                for b in range(NBF):
                    if b < NBF - 1:
                        for bi_ in range(nb_):
                            if b == 0:
                                if bi_ == 0:
                                    pre_block(0)
                                if bi_ + 1 < nb_:
                                    pre_block(bi_ + 1)
                            up_tb(b, bi_)
                        if b + 2 < NBF:
                            load_wu(b + 2)
                        for bi_ in range(nb_):
                            down_tb(b, bi_)
                        load_wd(b + 1)
                    else:
                        up_tb(b, 0)
                        for bi_ in range(nb_):
                            if bi_ + 1 < nb_:
                                up_tb(b, bi_ + 1)
                            down_tb(b, bi_)

        def copy_phase(src_p, src_s, dst_p, dst_s):
            with phase() as (sb, ps):
                t = [sb(f"cp{i}", [128, D], F32) for i in range(2)]
                rows = [(src_p, dst_p, r, 128) for r in range(0, SEQ, 128)] + [(src_s, dst_s, 0, 64)]
                for i, (s_, d_, r, n) in enumerate(rows):
                    load(t[i % 2][0:n, :], f"cp{i % 2}", s_[r:r + n, :], sem=f"cpl{i % 2}")
                    store(d_[r:r + n, :], t[i % 2][0:n, :], f"cp{i % 2}", f"cps{i % 2}")

        if stage == 0:
            copy_phase(T["xp"], T["xs"], T["xa_p"], T["xa_s"])
            copy_phase(T["xa_p"], T["xa_s"], T["yp"], T["ys"])
            return nc
        gla_phase()
        if stage == 1:
            copy_phase(T["xa_p"], T["xa_s"], T["yp"], T["ys"])
            return nc
        if stage == 2:
            mlp_phase(0, T["xa_p"], T["xa_s"], T["yp"], T["ys"])
            return nc
        mlp_phase(0, T["xa_p"], T["xa_s"], T["xa_p"], T["xa_s"])
        s5_phase(nc, P, T, phase, mk_norm_env, prenorm, postnorm_res, load, store, load_gain, load_w, prenorm_a, prenorm_b)
        if stage == 3:
            copy_phase(T["xa_p"], T["xa_s"], T["yp"], T["ys"])
            return nc
        mlp_phase(1, T["xa_p"], T["xa_s"], T["yp"], T["ys"])
        P.barrier()
        P.flush()
    return nc


_NC_CACHE = {}


def _in_maps(inputs):
    consts = _consts()
    maps = []
    f = lambda a: np.ascontiguousarray(np.asarray(a, dtype=np.float32))
    for i in range(8):
        m = {}
        m["xp"] = f(inputs["x_prompt"][i])
        m["xs"] = f(inputs["x_sample"][NB * i:NB * (i + 1)]).reshape(NS, D)
        m["sg"] = f(inputs["state_gla"][0, NB * i:NB * (i + 1)])
        m["sre"] = f(inputs["state_s5_re"][0, NB * i:NB * (i + 1)]).reshape(NB, NG * NP)
        m["sim"] = f(inputs["state_s5_im"][0, NB * i:NB * (i + 1)]).reshape(NB, NG * NP)
        for k in IN_SPECS:
            if k not in m:
                m[k] = f(inputs[k])
        m.update(consts)
        maps.append(m)
    return maps


def kernel(**inputs):
    stage = int(os.environ.get("KSTAGE", "99"))
    ncores = int(os.environ.get("KCORES", "8"))
    if stage not in _NC_CACHE:
        _NC_CACHE[stage] = build(stage)
    nc = _NC_CACHE[stage]
    maps = _in_maps(inputs)[:ncores]
    res = run_bass_kernel_spmd(nc, maps, core_ids=list(range(ncores)))
    R = res.results
    nb = ncores
    y_prompt = np.stack([R[i]["yp"] for i in range(nb)])
    y_sample = np.concatenate([R[i]["ys"].reshape(NB, 4, D) for i in range(nb)])
    gla_prompt = np.stack([R[i]["glap"] for i in range(nb)])[None]
    gla_sample = np.concatenate([R[i]["glas"] for i in range(nb)])[None]
    s5p = np.stack([R[i]["s5p"].reshape(2, NG, NP) for i in range(nb)])
    s5s = np.concatenate([R[i]["s5s"].reshape(NB, 2, NG, NP) for i in range(nb)])
    outs = (y_prompt, y_sample, gla_prompt, gla_sample,
            s5p[:, 0][None], s5p[:, 1][None], s5s[:, 0][None], s5s[:, 1][None])
    return tuple(np.ascontiguousarray(o, dtype=np.float32) for o in outs)
```

```python
import contextlib
import math
import os

import numpy as np
import ml_dtypes
import concourse.bass as bass
import concourse.mybir as mybir
from concourse.bass_utils import run_bass_kernel_spmd

F32 = mybir.dt.float32
BF16 = mybir.dt.bfloat16
I32 = mybir.dt.int32
AF = mybir.ActivationFunctionType
ALU = mybir.AluOpType
AX = mybir.AxisListType

D = 1024
FF = 4096
SEQ = 2048
NS = 64
NB = 16
H = 4
DK = 128
DV = 256
GIN = 3088
EPS = 1e-6
NG = 64
NP = 64
TB = 8
ENGS = ("pe", "act", "dve", "pool", "sp")


def I(method, *a, **k):
    return lambda e: getattr(e, method)(*a, **k)


class Prog:
    def __init__(self, nc, es):
        self.nc = nc
        self.es = es
        self.q = {e: [] for e in ENGS}
        self.val = {}
        self.known = {e: {} for e in ENGS}
        self.lastw = {}
        self.readers = {}
        self.semkeys = []
        self.sems = {}
        self.bank_last = {}

    def _key(self, k):
        if k not in self.val:
            self.val[k] = 0
            self.semkeys.append(k)

    def _deps(self, eng, reads, writes):
        deps = {}
        for b in list(reads) + list(writes):
            if "/" in b:
                for k, v in self.bank_last.get(b.split("/")[0], {}).items():
                    if k != eng:
                        deps[k] = max(deps.get(k, 0), v)
        for b in reads:
            w = self.lastw.get(b)
            if w:
                deps[w[0]] = max(deps.get(w[0], 0), w[1])
        for b in writes:
            w = self.lastw.get(b)
            if w:
                deps[w[0]] = max(deps.get(w[0], 0), w[1])
            for k, v in self.readers.get(b, {}).items():
                deps[k] = max(deps.get(k, 0), v)
        for k, v in deps.items():
            if k.startswith("d:"):
                v = self.val[k]
            if self.known[eng].get(k, 0) < v:
                self.q[eng].append(("wait", k, v))
                self.known[eng][k] = v

    def _mark(self, key, v, reads, writes):
        for b in list(reads) + list(writes):
            if "/" in b:
                self.bank_last.setdefault(b.split("/")[0], {})[key] = v
        for b in writes:
            self.lastw[b] = (key, v)
            self.readers[b] = {}
        for b in reads:
            r = self.readers.setdefault(b, {})
            r[key] = max(r.get(key, 0), v)

    def op(self, eng, fns, reads=(), writes=()):
        if callable(fns):
            fns = [fns]
        self._deps(eng, reads, writes)
        self._key(eng)
        self.val[eng] += 1
        v = self.val[eng]
        for f in fns[:-1]:
            self.q[eng].append(("op", f, None, 0))
        self.q[eng].append(("op", fns[-1], eng, 1))
        self._mark(eng, v, reads, writes)

    def dma(self, eng, fn, sem, reads=(), writes=()):
        k = "d:" + sem
        self._deps(eng, reads, writes)
        self._key(k)
        self.val[k] += 16
        v = self.val[k]
        self.q[eng].append(("op", fn, k, 16))
        self._mark(k, v, reads, writes)

    def barrier(self):
        for e in ENGS:
            for k in self.semkeys:
                v = self.val[k]
                if v > 0 and self.known[e].get(k, 0) < v:
                    self.q[e].append(("wait", k, v))
                    self.known[e][k] = v

    def flush(self):
        nc = self.nc
        for k in self.semkeys:
            if k not in self.sems:
                self.sems[k] = self.es.enter_context(nc.semaphore("s_" + k.replace(":", "_")))
        sems = self.sems
        q = self.q
        self.q = {e: [] for e in ENGS}
        with nc.Block() as block:
            def run(engname):
                def body(eng):
                    for it in q[engname]:
                        if it[0] == "wait":
                            eng.wait_ge(sems[it[1]], it[2])
                        else:
                            ins = it[1](eng)
                            if it[2] is not None:
                                ins.then_inc(sems[it[2]], it[3])
                return body

            block.tensor(run("pe"))
            block.scalar(run("act"))
            block.vector(run("dve"))
            block.gpsimd(run("pool"))
            block.sync(run("sp"))


def _consts():
    c = {}
    c["ident_bf"] = np.eye(128, dtype=ml_dtypes.bfloat16)
    c["ident_f"] = np.eye(128, dtype=np.float32)
    s = np.arange(64)
    c["maskT_p"] = (s[:, None] <= s[None, :]).astype(np.float32)
    c["maskT_s"] = ((s[:, None] <= s[None, :]) & (s[:, None] // 4 == s[None, :] // 4)).astype(np.float32)
    m = np.ones((128, 1024), np.float32)
    m[:, ::64] = 0
    c["m01p"] = m
    m = np.ones((128, 256), np.float32)
    m[:, ::4] = 0
    c["m01s"] = m
    cm = np.zeros((128, NB, 64), np.float32)
    for b in range(NB):
        cm[:, b, 4 * b:4 * b + 4] = 1
    c["colmask"] = cm
    rm = np.zeros((64, NB), np.float32)
    for b in range(NB):
        rm[4 * b:4 * b + 4, b] = 1
    c["rowmask"] = rm
    sel = np.zeros((128, 8, 8, 128), np.float32)
    selT = np.zeros((128, 8, 8, 128), np.float32)
    for gl in range(8):
        for s_ in range(8):
            for cc in range(16):
                sel[gl * 16 + cc, gl, s_, s_ * 16 + cc] = 1
                selT[s_ * 16 + cc, s_, gl, gl * 16 + cc] = 1
    c["sel"] = sel.astype(ml_dtypes.bfloat16)
    c["selT"] = selT.astype(ml_dtypes.bfloat16)
    i = np.arange(128) // 16
    c["mask1"] = (i[:, None] <= i[None, :]).astype(np.float32)
    return c


CONST_SPECS = {
    "ident_bf": ([128, 128], BF16), "ident_f": ([128, 128], F32),
    "maskT_p": ([64, 64], F32), "maskT_s": ([64, 64], F32),
    "m01p": ([128, 1024], F32), "m01s": ([128, 256], F32),
    "colmask": ([128, NB, 64], F32), "rowmask": ([64, NB], F32),
    "sel": ([128, 8, 8, 128], BF16), "selT": ([128, 8, 8, 128], BF16),
    "mask1": ([128, 128], F32),
}

IN_SPECS = {
    "xp": [SEQ, D], "xs": [NS, D], "sg": [NB, H, DK, DV], "sre": [NB, NG * NP], "sim": [NB, NG * NP],
    "g_pre_mix": [2, D], "g_post_mix": [2, D], "g_pre_mlp": [2, D], "g_post_mlp": [2, D],
    "w_up": [2, D, FF], "w_down": [2, FF, D],
    "gla_w_in": [1, D, GIN], "gla_w_gate_up": [1, 16, 512], "gla_b_gate": [1, 512], "gla_g_head": [1, D],
    "gla_w_out": [1, D, D], "s5_w_in": [1, D, D], "s5_a_re": [1, NG, NP], "s5_a_im": [1, NG, NP],
    "s5_log_step": [1, NG], "s5_b_re": [1, NG, NP, 16], "s5_b_im": [1, NG, NP, 16],
    "s5_c_re": [1, NG, 16, NP], "s5_c_im": [1, NG, 16, NP], "s5_d": [1, D],
    "s5_w_glu_a": [1, D, D], "s5_w_glu_b": [1, D, D],
}
OUT_SPECS = {
    "yp": [SEQ, D], "ys": [NS, D], "glap": [H, DK, DV], "glas": [NB, H, DK, DV],
    "s5p": [2, 32, 128], "s5s": [NB, 2, NG * NP],
}


def s5_phase(nc, P, T, phase, mk_norm_env, prenorm, postnorm_res, load, store, load_gain, load_w, prenorm_a, prenorm_b):
    NT = SEQ + NS
    NBLK = SEQ // TB
    TWO_PI = 2.0 * math.pi
    with phase() as (sbo, pso_):
        W1 = sbo("W1", [128, NG, 128], BF16)
        W2r = sbo("W2r", [128, 32, 128], BF16)
        W2i = sbo("W2i", [128, 32, 128], BF16)
        W3 = [sbo("W3r", [128, NG, 64], BF16), sbo("W3i", [128, NG, 64], BF16)]
        Lr = sbo("Lr", [128, 8, 32], F32)
        Li = sbo("Li", [128, 8, 32], F32)
        nLi = sbo("nLi", [128, 8, 32], F32)
        Em4 = sbo("Em4", [128, 3, 32], F32)
        h0T = sbo("h0T", [128, 32, 2, NB], F32)
        HsOut = sbo("HsOut", [128, 2, NB, 32], F32)
        HfinP = sbo("HfinP", [128, 128], F32)
        P.op("pool", I("memset", HfinP[:, :], 0.0), writes=["HfinP"])
        HfinV = HfinP[:, 0:64].rearrange("p (ri pr) -> p ri pr", ri=2)
        idf = sbo("idf", [128, 128], F32)
        load(idf[:, :], "idf", T["ident_f"])

        with phase() as (sb, ps):
            idb = sb("idb", [128, 128], BF16)
            load(idb[:, :], "idb", T["ident_bf"])
            mask1 = sb("mask1", [128, 128], F32)
            load(mask1[:, :], "mask1", T["mask1"])
            Ain = sb("Ain", [32, 3, 128], F32)
            load(Ain[:, 0, :], "Ain", T["s5_a_re"][0].rearrange("(pr g2) p -> pr (g2 p)", g2=2))
            load(Ain[:, 1, :], "Ain", T["s5_a_im"][0].rearrange("(pr g2) p -> pr (g2 p)", g2=2))
            Lin = sb("Lin", [32, 2], F32)
            load(Lin[:, :], "Lin", T["s5_log_step"].rearrange("o (pr g2) -> (o pr) g2", g2=2))
            P.op("dve", I("tensor_copy", out=Ain[:, 2, :].rearrange("p (g q) -> p g q", g=2),
                          in_=Lin[:, :].rearrange("p (g o) -> p g o", o=1).broadcast_to([32, 2, 64])), reads=["Lin", "Ain"], writes=["Ain"])
            psA = ps("psA", [128, 3, 32], F32)
            P.op("pe", [I("transpose", out=psA[:, k, :], in_=Ain[:, k, :], identity=idf[0:32, 0:32]) for k in range(3)], reads=["Ain", "idf"], writes=["psA/x"])
            aT = sb("aT", [128, 3, 32], F32)
            P.op("dve", I("tensor_copy", out=aT[:, :, :], in_=psA[:, :, :]), reads=["psA/x"], writes=["aT"])
            dtT = sb("dtT", [128, 32], F32)
            P.op("act", I("activation", out=dtT[:, :], in_=aT[:, 2, :], func=AF.Exp), reads=["aT"], writes=["dtT"])
            ad = sb("ad", [128, 2, 32], F32)
            P.op("dve", I("tensor_tensor", out=ad[:, :, :], in0=aT[:, 0:2, :], in1=dtT[:, :].rearrange("p (o q) -> p o q", o=1).broadcast_to([128, 2, 32]), op=ALU.mult),
                 reads=["aT", "dtT"], writes=["ad"])
            tsc = sb("tsc", [128, 8, 32], F32)
            for t in range(8):
                P.op("pool", I("memset", tsc[:, t, :], float(t + 1)), writes=["tsc"])
            ang = sb("ang", [128, 8, 32], F32)
            P.op("dve", I("tensor_tensor", out=ang[:, :, :], in0=tsc[:, :, :], in1=ad[:, 1:2, :].broadcast_to([128, 8, 32]), op=ALU.mult), reads=["tsc", "ad"], writes=["ang"])
            art = sb("art", [128, 8, 32], F32)
            P.op("dve", I("tensor_tensor", out=art[:, :, :], in0=tsc[:, :, :], in1=ad[:, 0:1, :].broadcast_to([128, 8, 32]), op=ALU.mult), reads=["tsc", "ad"], writes=["art"])
            npi = sb("npi", [128, 1], F32)
            P.op("dve", I("memset", npi[:, :], -math.pi), writes=["npi"])
            sc = sb("sc", [128, 2, 8, 32], F32)
            u = sb("u", [128, 8, 32], F32)
            ki = sb("ki", [128, 8, 32], I32)
            kf = sb("kf", [128, 8, 32], F32)
            for w, off in ((0, 0.5), (1, 0.75)):
                P.op("dve", I("tensor_scalar", out=u[:, :, :], in0=ang[:, :, :], scalar1=1.0 / TWO_PI, scalar2=off + 64.0, op0=ALU.mult, op1=ALU.add), reads=["ang"], writes=["u"])
                P.op("dve", I("tensor_copy", out=ki[:, :, :], in_=u[:, :, :]), reads=["u"], writes=["ki"])
                P.op("dve", I("tensor_copy", out=kf[:, :, :], in_=ki[:, :, :]), reads=["ki"], writes=["kf"])
                P.op("dve", I("tensor_tensor", out=u[:, :, :], in0=u[:, :, :], in1=kf[:, :, :], op=ALU.subtract), reads=["u", "kf"], writes=["u"])
                P.op("dve", I("tensor_single_scalar", out=kf[:, :, :], in_=u[:, :, :], scalar=0.0, op=ALU.is_lt), reads=["u"], writes=["kf"])
                P.op("dve", I("tensor_tensor", out=u[:, :, :], in0=u[:, :, :], in1=kf[:, :, :], op=ALU.add), reads=["u", "kf"], writes=["u"])
                P.op("act", I("activation", out=sc[:, w, :, :], in_=u[:, :, :], func=AF.Sin, scale=TWO_PI, bias=npi[:, 0:1]), reads=["u", "npi"], writes=["sc"])
            mag = sb("mag", [128, 2, 8, 32], F32)
            P.op("act", I("activation", out=mag[:, 0, :, :], in_=art[:, :, :], func=AF.Exp), reads=["art"], writes=["mag"])
            P.op("act", I("activation", out=mag[:, 1, :, :], in_=art[:, :, :], func=AF.Exp, scale=-1.0), reads=["art"], writes=["mag"])
            Ep = sb("Ep", [128, 2, 9, 32], F32)
            En = sb("En", [128, 2, 8, 32], F32)
            P.op("dve", I("memset", Ep[:, 0, 0, :], 1.0), writes=["Ep"])
            P.op("dve", I("memset", Ep[:, 1, 0, :], 0.0), writes=["Ep"])
            P.op("dve", I("tensor_tensor", out=Ep[:, 0, 1:9, :], in0=mag[:, 0, :, :], in1=sc[:, 1, :, :], op=ALU.mult), reads=["mag", "sc"], writes=["Ep"])
            P.op("dve", I("tensor_tensor", out=Ep[:, 1, 1:9, :], in0=mag[:, 0, :, :], in1=sc[:, 0, :, :], op=ALU.mult), reads=["mag", "sc"], writes=["Ep"])
            P.op("dve", I("tensor_tensor", out=En[:, 0, :, :], in0=mag[:, 1, :, :], in1=sc[:, 1, :, :], op=ALU.mult), reads=["mag", "sc"], writes=["En"])
            P.op("dve", I("scalar_tensor_tensor", out=En[:, 1, :, :], in0=mag[:, 1, :, :], scalar=-1.0, in1=sc[:, 0, :, :], op0=ALU.mult, op1=ALU.mult), reads=["mag", "sc"], writes=["En"])
            P.op("dve", I("tensor_copy", out=Em4[:, 0:2, :], in_=En[:, :, 3, :]), reads=["En"], writes=["Em4"])
            P.op("dve", I("tensor_scalar", out=Em4[:, 2, :], in0=En[:, 1, 3, :], scalar1=-1.0, scalar2=None, op0=ALU.mult), reads=["En"], writes=["Em4"])
            P.op("dve", I("tensor_copy", out=Lr[:, 0, :], in_=Ep[:, 0, 8, :]), reads=["Ep"], writes=["Lr"])
            P.op("dve", I("tensor_copy", out=Li[:, 0, :], in_=Ep[:, 1, 8, :]), reads=["Ep"], writes=["Li"])
            t1 = sb("t1", [128, 32], F32)
            t2 = sb("t2", [128, 32], F32)
            for k in range(7):
                P.op("dve", I("tensor_tensor", out=t1[:, :], in0=Lr[:, k, :], in1=Lr[:, k, :], op=ALU.mult), reads=["Lr"], writes=["t1"])
                P.op("dve", I("tensor_tensor", out=t2[:, :], in0=Li[:, k, :], in1=Li[:, k, :], op=ALU.mult), reads=["Li"], writes=["t2"])
                P.op("dve", I("scalar_tensor_tensor", out=Li[:, k + 1, :], in0=Lr[:, k, :], scalar=2.0, in1=Li[:, k, :], op0=ALU.mult, op1=ALU.mult), reads=["Lr", "Li"], writes=["Li"])
                P.op("dve", I("tensor_tensor", out=Lr[:, k + 1, :], in0=t1[:, :], in1=t2[:, :], op=ALU.subtract), reads=["t1", "t2", "Li"], writes=["Lr"])
            P.op("dve", I("tensor_scalar", out=nLi[:, :, :], in0=Li[:, :, :], scalar1=-1.0, scalar2=None, op0=ALU.mult), reads=["Li"], writes=["nLi"])
            zt = sb("zt", [128, 8, 32], F32)
            P.op("dve", I("tensor_scalar", out=zt[:, 0, :], in0=Ep[:, 0, 1, :], scalar1=-1.0, scalar2=None, op0=ALU.add), reads=["Ep"], writes=["zt0"])
            P.op("dve", I("tensor_tensor", out=zt[:, 1, :], in0=aT[:, 0, :], in1=aT[:, 0, :], op=ALU.mult), reads=["aT"], writes=["zt1"])
            P.op("dve", I("tensor_tensor", out=zt[:, 3, :], in0=aT[:, 1, :], in1=aT[:, 1, :], op=ALU.mult), reads=["aT"], writes=["zt3"])
            P.op("dve", I("tensor_tensor", out=zt[:, 1, :], in0=zt[:, 1, :], in1=zt[:, 3, :], op=ALU.add), reads=["zt1", "zt3"], writes=["zt1"])
            P.op("dve", I("reciprocal", out=zt[:, 2, :], in_=zt[:, 1, :]), reads=["zt1"], writes=["zt2"])
            P.op("dve", I("tensor_tensor", out=zt[:, 3, :], in0=zt[:, 0, :], in1=aT[:, 0, :], op=ALU.mult), reads=["zt0", "aT", "zt1"], writes=["zt3"])
            P.op("dve", I("tensor_tensor", out=zt[:, 4, :], in0=Ep[:, 1, 1, :], in1=aT[:, 1, :], op=ALU.mult), reads=["Ep", "aT"], writes=["zt4"])
            P.op("dve", I("tensor_tensor", out=zt[:, 3, :], in0=zt[:, 3, :], in1=zt[:, 4, :], op=ALU.add), reads=["zt3", "zt4"], writes=["zt3"])
            P.op("dve", I("tensor_tensor", out=zt[:, 5, :], in0=zt[:, 3, :], in1=zt[:, 2, :], op=ALU.mult), reads=["zt3", "zt2"], writes=["zt5"])
            P.op("dve", I("tensor_tensor", out=zt[:, 3, :], in0=Ep[:, 1, 1, :], in1=aT[:, 0, :], op=ALU.mult), reads=["Ep", "aT", "zt5"], writes=["zt3"])
            P.op("dve", I("tensor_tensor", out=zt[:, 4, :], in0=zt[:, 0, :], in1=aT[:, 1, :], op=ALU.mult), reads=["zt0", "aT", "zt3"], writes=["zt4"])
            P.op("dve", I("tensor_tensor", out=zt[:, 3, :], in0=zt[:, 3, :], in1=zt[:, 4, :], op=ALU.subtract), reads=["zt3", "zt4"], writes=["zt3"])
            P.op("dve", I("tensor_tensor", out=zt[:, 6, :], in0=zt[:, 3, :], in1=zt[:, 2, :], op=ALU.mult), reads=["zt3", "zt2"], writes=["zt6"])
            zre = zt[:, 5, :].rearrange("p (q o) -> p q o", o=1).broadcast_to([128, 32, 16])
            zim = zt[:, 6, :].rearrange("p (q o) -> p q o", o=1).broadcast_to([128, 32, 16])
            Bt = sb("Bt", [128, 2, 32, 16], F32)
            load(Bt[:, 0, :, :], "Bt", T["s5_b_re"][0].rearrange("g p c -> (g p) c").rearrange("(pr q) c -> q pr c", q=128))
            load(Bt[:, 1, :, :], "Bt", T["s5_b_im"][0].rearrange("g p c -> (g p) c").rearrange("(pr q) c -> q pr c", q=128))
            bb = sb("bb", [128, 2, 32, 16], F32)
            mA = sb("mA", [128, 32, 16], F32)
            mB = sb("mB", [128, 32, 16], F32)
            P.op("dve", I("tensor_tensor", out=mA[:, :, :], in0=Bt[:, 0, :, :], in1=zre, op=ALU.mult), reads=["Bt", "zt5"], writes=["mA"])
            P.op("dve", I("tensor_tensor", out=mB[:, :, :], in0=Bt[:, 1, :, :], in1=zim, op=ALU.mult), reads=["Bt", "zt6"], writes=["mB"])
            P.op("dve", I("tensor_tensor", out=bb[:, 0, :, :], in0=mA[:, :, :], in1=mB[:, :, :], op=ALU.subtract), reads=["mA", "mB"], writes=["bb"])
            P.op("dve", I("tensor_tensor", out=mA[:, :, :], in0=Bt[:, 1, :, :], in1=zre, op=ALU.mult), reads=["Bt", "zt5", "bb"], writes=["mA"])
            P.op("dve", I("tensor_tensor", out=mB[:, :, :], in0=Bt[:, 0, :, :], in1=zim, op=ALU.mult), reads=["Bt", "zt6", "bb"], writes=["mB"])
            P.op("dve", I("tensor_tensor", out=bb[:, 1, :, :], in0=mA[:, :, :], in1=mB[:, :, :], op=ALU.add), reads=["mA", "mB"], writes=["bb"])
            Cin = sb("Cin", [32, 2, 16, 128], F32)
            for ri_, nm_ in ((0, "s5_c_re"), (1, "s5_c_im")):
                for g2_ in range(2):
                    load(Cin[:, ri_, :, g2_ * 64:(g2_ + 1) * 64], "Cin", T[nm_][0].rearrange("(pr g2) c p -> g2 pr c p", g2=2)[g2_])
            psC = ps("psC", [128, 2, 16, 32], F32)
            P.op("pe", [I("transpose", out=psC[:, ri, c, :], in_=Cin[:, ri, c, :], identity=idf[0:32, 0:32]) for ri in range(2) for c in range(16)],
                 reads=["Cin", "idf"], writes=["psC/x"])
            Ct = sb("Ct", [128, 2, 32, 16], F32)
            for ri in range(2):
                P.op("dve", I("tensor_copy", out=Ct[:, ri, :, :], in_=psC[:, ri, :, :].rearrange("q c pr -> q pr c")), reads=["psC/x"], writes=["Ct"])
            W2v = [W2r[:, :, :].rearrange("p q (t c) -> p q t c", t=8), W2i[:, :, :].rearrange("p q (t c) -> p q t c", t=8)]
            Xm = sb("Xm", [128, 32, 2, 8, 16], BF16)
            Xb = sb("Xb", [128, 32, 2, 8, 16], BF16)
            bgA, bgB = sb("bgA", [128, 32, 8, 16], F32), sb("bgB", [128, 32, 8, 16], F32)
            big = {"dve": (bgA, bgB, "bgA", "bgB"), "pool": (bgA, bgB, "bgA", "bgB")}

            def abc(a3):
                return a3.rearrange("p q (o c) -> p q o c", o=1).broadcast_to([128, 32, 8, 16])

            def ebc(tab, ri, t0, rev=False):
                e = tab[:, ri, t0:t0 + 8, :].rearrange("p t q -> p q t")
                return e.rearrange("p q (t o) -> p q t o", o=1).broadcast_to([128, 32, 8, 16])
            Epr = sb("Epr", [128, 2, 8, 32], F32)
            for t in range(8):
                P.op("pool", I("tensor_copy", out=Epr[:, :, t, :], in_=Ep[:, :, 7 - t, :]), reads=["Ep"], writes=["Epr"])
            jobs = [
                (W2v[0], "W2r", Ct, Ep, 1, "re", "dve"), (W2v[1], "W2i", Ct, Ep, 1, "nim", "dve"),
                (Xm[:, :, 0, :, :], "Xm0", bb, En, 0, "re", "dve"), (Xm[:, :, 1, :, :], "Xm1", bb, En, 0, "im", "dve"),
                (Xb[:, :, 0, :, :], "Xb0", bb, Epr, 0, "re", "dve"), (Xb[:, :, 1, :, :], "Xb1", bb, Epr, 0, "im", "dve"),
            ]
            allj = {"W2r": ["W2r"], "W2i": ["W2i"], "Xm": ["Xm0", "Xm1"], "Xb": ["Xb0", "Xb1"]}
            for (o_ap, on, a_t, tab, t0, kind, eng) in jobs:
                tA, tB, nA, nB = big[eng]
                e1, e2 = (0, 1) if kind == "re" else (1, 0)
                an = "Ct" if a_t is Ct else "bb"
                P.op(eng, I("tensor_tensor", out=tA[:, :, :, :], in0=abc(a_t[:, 0, :, :]), in1=ebc(tab, e1, t0), op=ALU.mult), reads=[an, "Ep", "En", "Epr"], writes=[nA])
                P.op("pool", I("tensor_tensor", out=tB[:, :, :, :], in0=abc(a_t[:, 1, :, :]), in1=ebc(tab, e2, t0), op=ALU.mult), reads=[an, "Ep", "En", "Epr"], writes=[nB])
                if kind == "re":
                    P.op(eng, I("tensor_tensor", out=o_ap, in0=tA[:, :, :, :], in1=tB[:, :, :, :], op=ALU.subtract), reads=[nA, nB], writes=[on])
                elif kind == "im":
                    P.op(eng, I("tensor_tensor", out=o_ap, in0=tA[:, :, :, :], in1=tB[:, :, :, :], op=ALU.add), reads=[nA, nB], writes=[on])
                else:
                    P.op(eng, I("tensor_scalar", out=tA[:, :, :, :], in0=tA[:, :, :, :], scalar1=-1.0, scalar2=None, op0=ALU.mult), reads=[nA], writes=[nA])
                    P.op(eng, I("tensor_tensor", out=o_ap, in0=tA[:, :, :, :], in1=tB[:, :, :, :], op=ALU.subtract), reads=[nA, nB], writes=[on])
            psW = [ps(f"psW{i}", [128, 128], F32) for i in range(2)]
            psX = [ps(f"psX{i}", [128, 2, 64], BF16) for i in range(2)]
            for g in range(NG):
                pr, g2 = g // 2, g % 2
                rows = slice(g2 * 64, g2 * 64 + 64)
                pw = psW[g % 2]
                P.op("pe", [I("matmul", pw[:, :], lhsT=Xm[rows, pr, ri, :, :].rearrange("p s c -> p (s c)"), rhs=(W2r, W2i)[ri][rows, pr, :], start=(ri == 0), stop=(ri == 1)) for ri in range(2)],
                     reads=allj["Xm"] + allj["W2r"] + allj["W2i"], writes=[f"psW{g % 2}/x"])
                P.op("dve", I("tensor_tensor", out=W1[:, g, :], in0=pw[:, :], in1=mask1[:, :], op=ALU.mult), reads=[f"psW{g % 2}/x", "mask1"], writes=["W1"])
                for ri in range(2):
                    k2 = g % 2
                    P.op("pe", I("transpose", out=psX[ri][:, k2, :], in_=Xb[rows, pr, ri, :, :].rearrange("p s c -> p (s c)"), identity=idb[rows, rows]), reads=allj["Xb"] + ["idb"], writes=[f"psX{ri}/{k2}"])
                    if ri == 0:
                        P.op("act", I("copy", out=W3[ri][:, g, :], in_=psX[ri][:, k2, :]), reads=[f"psX{ri}/{k2}"], writes=[f"W3{ri}"])
                    else:
                        P.op("dve", I("tensor_copy", out=W3[ri][:, g, :], in_=psX[ri][:, k2, :]), reads=[f"psX{ri}/{k2}"], writes=[f"W3{ri}"])

        KS = os.environ.get("KS_CUT", "Z")
        if KS == "b0":
            return
        with phase() as (sb, ps):
            X0 = sb("X0", [NB, 2, NG * NP], F32)
            load(X0[:, 0, :], "X0", T["sre"])
            load(X0[:, 1, :], "X0", T["sim"])
            psh = ps("psh", [128, 32, 2, NB], F32)
            P.op("pe", [I("transpose", out=psh[:, pr, ri, :], in_=X0[0:NB, ri, pr * 128:(pr + 1) * 128], identity=idf[0:NB, 0:NB]) for pr in range(32) for ri in range(2)],
                 reads=["X0", "idf"], writes=["psh/x"])
            P.op("dve", I("tensor_copy", out=h0T[:, 0:16, :, :], in_=psh[:, 0:16, :, :]), reads=["psh/x"], writes=["h0T"])
            P.op("act", I("copy", out=h0T[:, 16:32, :, :], in_=psh[:, 16:32, :, :]), reads=["psh/x"], writes=["h0T"])

        if KS == "s0":
            return
        with phase() as (sbm, psm_):
          zT = sbm("zT", [128, 8, NT], BF16)
          with phase() as (sbu, psu_):
            ufm = sbu("ufm", [128, 8, NT], BF16)
            with phase() as (sb, ps):
                env = mk_norm_env(sb, "a")
                gpre = load_gain(sb, "gpre", T["g_pre_mix"][1:2, :])
                win = load_w(sb, "s5win", T["s5_w_in"][0], 8, D)
                hT = sb("hT", [128, 8, NT], BF16)
                xt = [sb(f"xt{i}", [128, D], F32) for i in range(2)]
                psU = [ps(f"psU{i}", [128, 512], F32) for i in range(2)]
                psT = ps("psT", [128, 8, 128], BF16)
                rows = [(T["xa_p"], r, 128, r) for r in range(0, SEQ, 128)] + [(T["xa_s"], 0, 64, SEQ)]
                blks = [(b0, 512) for b0 in range(0, SEQ, 512)] + [(SEQ, NS)]

                def pre_blk(bi_):
                    b0, n = blks[bi_]
                    its = [rr for rr in rows if b0 <= rr[3] < b0 + n]
                    for p0 in range(0, len(its), 2):
                        ks = []
                        for i_, (src, r, m, col) in enumerate(its[p0:p0 + 2]):
                            load(xt[i_][0:m, :], f"xt{i_}", src[r:r + m, :], sem=f"xt{i_}")
                            ks.append(prenorm_a(env, xt[i_][0:m, :], f"xt{i_}", m, gpre, "gpre"))
                        for i_, (src, r, m, col) in enumerate(its[p0:p0 + 2]):
                            prenorm_b(env, ks[i_], psT, "psT/x", m, hT[:, :, col:col + m], f"hT{bi_}")

                k = 0
                pre_blk(0)
                for bi_, (b0, n) in enumerate(blks):
                    if bi_ + 1 < len(blks):
                        pre_blk(bi_ + 1)
                    for j in range(8):
                        pu = psU[k % 2]
                        P.op("pe", [I("matmul", pu[:, 0:n], lhsT=win[:, kc, j * 128:(j + 1) * 128], rhs=hT[:, kc, b0:b0 + n], start=(kc == 0), stop=(kc == 7)) for kc in range(8)],
                             reads=["s5win", f"hT{bi_}"], writes=[f"psU{k % 2}/x"])
                        if k % 2 == 0:
                            P.op("act", I("copy", out=ufm[:, j, b0:b0 + n], in_=pu[:, 0:n]), reads=[f"psU{k % 2}/x"], writes=["ufm"])
                        else:
                            P.op("dve", I("tensor_copy", out=ufm[:, j, b0:b0 + n], in_=pu[:, 0:n]), reads=[f"psU{k % 2}/x"], writes=["ufm"])
                        k += 1

            if KS == "a":
                return
            with phase() as (sb, ps):
                sel = sb("sel", [128, 8, 8, 128], BF16)
                load(sel[:, :, :, :], "sel", T["sel"])
                selT = sb("selT", [128, 8, 8, 128], BF16)
                load(selT[:, :, :, :], "selT", T["selT"])
                dsk = sb("dsk", [128, 8], F32)
                load(dsk[:, :], "dsk", T["s5_d"].rearrange("o (j p) -> p (o j)", p=128), allow_slow_non_contiguous=True)
                idb2 = sb("idb2", [128, 128], BF16)
                load(idb2[:, :], "idb2", T["ident_bf"])
                Dg = sb("Dg", [128, 8, 128], BF16)
                for j_ in range(8):
                    P.op("dve", I("tensor_scalar", out=Dg[:, j_, :], in0=idb2[:, :], scalar1=dsk[:, j_:j_ + 1], scalar2=None, op0=ALU.mult), reads=["idb2", "dsk"], writes=["Dg"])
                HlS = sb("HlS", [128, 32, 2, NB], F32)
                U8 = [sb(f"U8{i}", [128, 8, NBLK], BF16) for i in range(2)]
                U8s = [sb(f"U8s{i}", [128, 8, NB], BF16) for i in range(2)]
                HA = [sb(f"HA{i}", [128, 2, NBLK], F32) for i in range(4)]
                HB = [sb(f"HB{i}", [128, 2, NBLK], F32) for i in range(4)]
                Hin = [sb(f"Hin{i}", [128, 2, NBLK], BF16) for i in range(4)]
                HsIn = sb("HsIn", [128, 32, 2, NB], F32)
                HsInb = sb("HsInb", [128, 32, 2, NB], BF16)
                hs1 = sb("hs1", [128, 32, NB], F32)
                hs2 = sb("hs2", [128, 32, NB], F32)
                hst = sb("hst", [128, 2, NB], F32)
                Yg = sb("Yg", [128, 8, NBLK], BF16)
                Ygs = sb("Ygs", [128, 8, NB], BF16)
                psH = [ps(f"psH{i}", [128, 2, NBLK], F32) for i in range(2)]
                psu8 = [ps(f"psu8{i}", [128, NBLK], F32) for i in range(2)]
                psY_ = ps("psY", [128, 2 * NBLK], F32)
                psY = [psY_[:, 0:NBLK], psY_[:, NBLK:2 * NBLK]]
                psZ = [ps(f"psZ{i}", [128, NBLK], F32) for i in range(2)]
                psS = ps("psS", [128, 16, NB], F32)
                for i in range(4):
                    P.op("pool", I("memset", Hin[i][:, :, 0:1], 0.0), writes=[f"Hin{i}"])

                def e4b(k):
                    return Em4[:, k, :].rearrange("p (q o) -> p q o", o=1).broadcast_to([128, 32, NB])
                h0r_, h0i_ = h0T[:, :, 0, :], h0T[:, :, 1, :]
                P.op("dve", I("tensor_tensor", out=hs1[:, :, :], in0=h0r_, in1=e4b(0), op=ALU.mult), reads=["h0T", "Em4"], writes=["hs1"])
                P.op("pool", I("tensor_tensor", out=hs2[:, :, :], in0=h0i_, in1=e4b(1), op=ALU.mult), reads=["h0T", "Em4"], writes=["hs2"])
                P.op("dve", I("tensor_tensor", out=HsIn[:, :, 0, :], in0=hs1[:, :, :], in1=hs2[:, :, :], op=ALU.subtract), reads=["hs1", "hs2"], writes=["HsIn"])
                P.op("dve", I("tensor_tensor", out=hs1[:, :, :], in0=h0i_, in1=e4b(0), op=ALU.mult), reads=["h0T", "Em4", "HsIn"], writes=["hs1"])
                P.op("pool", I("tensor_tensor", out=hs2[:, :, :], in0=h0r_, in1=e4b(1), op=ALU.mult), reads=["h0T", "Em4", "HsIn"], writes=["hs2"])
                P.op("dve", I("tensor_tensor", out=HsIn[:, :, 1, :], in0=hs1[:, :, :], in1=hs2[:, :, :], op=ALU.add), reads=["hs1", "hs2"], writes=["HsIn"])
                P.op("act", I("copy", out=HsInb[:, :, :, :], in_=HsIn[:, :, :, :]), reads=["HsIn"], writes=["HsInb"])

                def upv(j):
                    return ufm[:, j, 0:SEQ].rearrange("p (n s) -> p s n", s=TB)

                def usv(j):
                    return ufm[:, j, SEQ:NT].rearrange("p (b t) -> p t b", t=4)

                def stageU(j):
                    up, us = upv(j), usv(j)
                    u8, u8s = U8[j % 2], U8s[j % 2]
                    for gl in range(8):
                        pu = psu8[gl % 2]
                        P.op("pe", [I("matmul", pu[:, :], lhsT=sel[:, gl, s_, :], rhs=up[:, s_, :], start=(s_ == 0), stop=(s_ == 7)) for s_ in range(8)],
                             reads=["sel", "ufm"], writes=[f"psu8{gl % 2}/x"])
                        P.op("act", I("copy", out=u8[:, gl, :], in_=pu[:, :]), reads=[f"psu8{gl % 2}/x"], writes=[f"U8_{j % 2}_{gl}"])
                        P.op("pe", [I("matmul", psS[:, gl % 2, :], lhsT=sel[:, gl, s_, :], rhs=us[:, s_ - 4, :], start=(s_ == 4), stop=(s_ == 7)) for s_ in range(4, 8)],
                             reads=["sel", "ufm"], writes=[f"psS/{gl % 2}"])
                        P.op("act", I("copy", out=u8s[:, gl, :], in_=psS[:, gl % 2, :]), reads=[f"psS/{gl % 2}"], writes=[f"U8s_{j % 2}_{gl}"])

                def stage1(pr):
                    j, p4, e2, e4 = pr // 4, pr % 4, pr % 2, pr % 4
                    u8, u8s = U8[j % 2], U8s[j % 2]
                    ph = psH[e2]
                    fns, fns2 = [], []
                    for g2 in range(2):
                        gl = 2 * p4 + g2
                        g = 2 * pr + g2
                        for ri in range(2):
                            fns.append(I("matmul", ph[g2 * 64:(g2 + 1) * 64, ri, :], lhsT=W3[ri][:, g, :], rhs=u8[:, gl, :], start=True, stop=True))
                            fns2.append(I("matmul", psS[g2 * 64:(g2 + 1) * 64, 2 + 2 * e4 + ri, :], lhsT=W3[ri][:, g, :], rhs=u8s[:, gl, :], start=True, stop=True))
                    P.op("pe", fns, reads=["W30", "W31", f"U8_{j % 2}_{2 * p4}", f"U8_{j % 2}_{2 * p4 + 1}"], writes=[f"psH{e2}/x"])
                    P.op("pe", fns2, reads=["W30", "W31", f"U8s_{j % 2}_{2 * p4}", f"U8s_{j % 2}_{2 * p4 + 1}"], writes=[f"psS/{2 + 2 * e4}", f"psS/{3 + 2 * e4}"])
                    P.op("act", I("copy", out=HA[e4][:, :, :], in_=ph[:, :, :]), reads=[f"psH{e2}/x"], writes=[f"HA{e4}r", f"HA{e4}i", f"HA{e4}h"])

                def stage2(prs):
                    st = {}
                    for pr in prs:
                        e4 = pr % 4
                        st[pr] = [HA[e4], HB[e4], f"HA{e4}", f"HB{e4}"]
                    for k in range(8):
                        d = 1 << k
                        for pr in prs:
                            src, dst, sn, dn = st[pr]
                            P.op("pool", I("tensor_copy", out=dst[:, :, 0:d], in_=src[:, :, 0:d]), reads=[sn + "r", sn + "i", sn + "h"], writes=[dn + "h"])
                        for pr in prs:
                            src, dst, sn, dn = st[pr]
                            lr = Lr[:, k, pr:pr + 1]
                            P.op("dve", I("scalar_tensor_tensor", out=dst[:, :, d:NBLK], in0=src[:, :, 0:NBLK - d], scalar=lr, in1=src[:, :, d:NBLK], op0=ALU.mult, op1=ALU.add),
                                 reads=[sn + "r", sn + "i", sn + "h", "Lr"], writes=[dn + "r", dn + "i"])
                        for pr in prs:
                            src, dst, sn, dn = st[pr]
                            nli = nLi[:, k, pr:pr + 1]
                            P.op("dve", I("scalar_tensor_tensor", out=dst[:, 0, d:NBLK], in0=src[:, 1, 0:NBLK - d], scalar=nli, in1=dst[:, 0, d:NBLK], op0=ALU.mult, op1=ALU.add),
                                 reads=[sn + "i", sn + "h", dn + "r", "nLi"], writes=[dn + "r"])
                        for pr in prs:
                            src, dst, sn, dn = st[pr]
                            li = Li[:, k, pr:pr + 1]
                            P.op("dve", I("scalar_tensor_tensor", out=dst[:, 1, d:NBLK], in0=src[:, 0, 0:NBLK - d], scalar=li, in1=dst[:, 1, d:NBLK], op0=ALU.mult, op1=ALU.add),
                                 reads=[sn + "r", sn + "h", dn + "i", "Li"], writes=[dn + "i"])
                        for pr in prs:
                            src, dst, sn, dn = st[pr]
                            st[pr] = [dst, src, dn, sn]
                    return {pr: (st[pr][0], st[pr][2]) for pr in prs}

                def stage3(pr, src, sn):
                    j, p4, e2, e4 = pr // 4, pr % 4, pr % 2, pr % 4
                    u8, u8s = U8[j % 2], U8s[j % 2]
                    hin = Hin[e4]
                    P.op("act", I("copy", out=hin[:, :, 1:NBLK], in_=src[:, :, 0:NBLK - 1]), reads=[sn + "r", sn + "i", sn + "h"], writes=[f"Hin{e4}"])
                    P.op("act", I("copy", out=HfinV[:, :, pr:pr + 1], in_=src[:, :, NBLK - 1:NBLK]), reads=[sn + "r", sn + "i", sn + "h"], writes=["HfinP"])
                    sr_, si_ = f"psS/{2 + 2 * e4}", f"psS/{3 + 2 * e4}"
                    P.op("act", I("copy", out=HlS[:, pr, :, :], in_=psS[:, 2 + 2 * e4:4 + 2 * e4, :]), reads=[sr_, si_], writes=["HlS"])
                    for g2 in range(2):
                        gl = 2 * p4 + g2
                        g = 2 * pr + g2
                        rows = slice(g2 * 64, g2 * 64 + 64)
                        py = psY[g2]
                        P.op("pe", [I("matmul", py[:, 0:NBLK], lhsT=W1[:, g, :], rhs=u8[:, gl, :], start=True, stop=False),
                                    I("matmul", py[:, 0:NBLK], lhsT=W2r[rows, pr, :], rhs=hin[rows, 0, :], start=False, stop=False),
                                    I("matmul", py[:, 0:NBLK], lhsT=W2i[rows, pr, :], rhs=hin[rows, 1, :], start=False, stop=True)],
                             reads=["W1", "W2r", "W2i", f"U8_{j % 2}_{gl}", f"Hin{e4}"], writes=[f"psY/{g2}"])
                        P.op("act", I("copy", out=Yg[:, gl, :], in_=py[:, 0:NBLK]), reads=[f"psY/{g2}"], writes=[f"Yg{gl}"])
                        P.op("pe", [I("matmul", psS[:, 10 + g2, :], lhsT=W1[:, g, :], rhs=u8s[:, gl, :], start=True, stop=False),
                                    I("matmul", psS[:, 10 + g2, :], lhsT=W2r[rows, pr, :], rhs=HsInb[rows, pr, 0, :], start=False, stop=False),
                                    I("matmul", psS[:, 10 + g2, :], lhsT=W2i[rows, pr, :], rhs=HsInb[rows, pr, 1, :], start=False, stop=True)],
                             reads=["W1", "W2r", "W2i", f"U8s_{j % 2}_{gl}", "HsInb"], writes=[f"psS/{10 + g2}"])
                        P.op("act", I("copy", out=Ygs[:, gl, :], in_=psS[:, 10 + g2, :]), reads=[f"psS/{10 + g2}"], writes=[f"Ygs{gl}"])

                def stageZ(j):
                    up, us = upv(j), usv(j)
                    allY = [f"Yg{gl}" for gl in range(8)]
                    allYs = [f"Ygs{gl}" for gl in range(8)]
                    zp = zT[:, j, 0:SEQ].rearrange("p (n s) -> p s n", s=TB)
                    zs = zT[:, j, SEQ:NT].rearrange("p (b t) -> p t b", t=4)
                    for t in range(8):
                        pz = psZ[t % 2]
                        P.op("pe", [I("matmul", pz[:, :], lhsT=selT[:, t, gl, :], rhs=Yg[:, gl, :], start=(gl == 0), stop=False) for gl in range(8)]
                             + [I("matmul", pz[:, :], lhsT=Dg[:, j, :], rhs=up[:, t, :], start=False, stop=True)],
                             reads=["selT", "Dg", "ufm"] + allY, writes=[f"psZ{t % 2}/x"])
                        P.op("act", I("activation", out=zp[:, t, :], in_=pz[:, :], func=AF.Gelu_apprx_tanh), reads=[f"psZ{t % 2}/x"], writes=["zT"])
                    for t in range(4, 8):
                        k2 = 12 + t % 2
                        P.op("pe", [I("matmul", psS[:, k2, :], lhsT=selT[:, t, gl, :], rhs=Ygs[:, gl, :], start=(gl == 0), stop=False) for gl in range(8)]
                             + [I("matmul", psS[:, k2, :], lhsT=Dg[:, j, :], rhs=us[:, t - 4, :], start=False, stop=True)],
                             reads=["selT", "Dg", "ufm"] + allYs, writes=[f"psS/{k2}"])
                        P.op("act", I("activation", out=zs[:, t - 4, :], in_=psS[:, k2, :], func=AF.Gelu_apprx_tanh), reads=[f"psS/{k2}"], writes=["zT"])

                NPR = 4 * int(os.environ.get("KS_J", "8"))
                groups = [(a, a + 1) for a in range(0, NPR, 2)]
                stageU(0)
                for pr in groups[0]:
                    stage1(pr)
                for gi, grp in enumerate(groups):
                    if gi + 1 < len(groups):
                        nxt = groups[gi + 1]
                        if nxt[0] % 4 == 0:
                            stageU(nxt[0] // 4)
                        for pr in nxt:
                            stage1(pr)
                    res = stage2(grp)
                    for pr in grp:
                        stage3(pr, res[pr][0], res[pr][1])
                    if grp[1] % 4 == 3:
                        stageZ(grp[1] // 4)

                def l8b(tab):
                    return tab[:, 0, :].rearrange("p (q o) -> p q o", o=1).broadcast_to([128, 32, NB])
                hr_, hi_ = HsIn[:, :, 0, :], HsIn[:, :, 1, :]
                for (ri_, a_, b_, tb_, opb) in ((0, hr_, hi_, nLi, ALU.add), (1, hi_, hr_, Li, ALU.add)):
                    P.op("dve", I("tensor_tensor", out=hs1[:, :, :], in0=a_, in1=l8b(Lr), op=ALU.mult), reads=["HsIn", "Lr", "HsOut"], writes=["hs1"])
                    P.op("pool", I("tensor_tensor", out=hs2[:, :, :], in0=b_, in1=l8b(tb_), op=ALU.mult), reads=["HsIn", "Li", "nLi", "HsOut"], writes=["hs2"])
                    P.op("dve", I("tensor_tensor", out=hs1[:, :, :], in0=hs1[:, :, :], in1=hs2[:, :, :], op=ALU.add), reads=["hs1", "hs2"], writes=["hs1"])
                    P.op("dve", I("tensor_tensor", out=HsOut[:, ri_, :, :].rearrange("p b q -> p q b"), in0=hs1[:, :, :], in1=HlS[:, :, ri_, :], op=ALU.add), reads=["hs1", "HlS"], writes=["HsOut"])

          if KS == "b":
              return
          with phase() as (sb, ps):
              env = mk_norm_env(sb, "c")
              gpost = load_gain(sb, "gpost", T["g_post_mix"][1:2, :])
              wa = load_w(sb, "wga", T["s5_w_glu_a"][0], 8, D)
              wb = load_w(sb, "wgb", T["s5_w_glu_b"][0], 8, D)
              xt = [sb(f"xt{i}", [128, D], F32) for i in range(2)]
              xo = [sb(f"xo{i}", [128, D], F32) for i in range(2)]
              sg_ = [sb(f"sg{i}", [128, D], F32) for i in range(2)]
              mt = [sb(f"mt{i}", [128, D], F32) for i in range(2)]
              psA_ = [ps(f"pga{i}", [128, D], F32) for i in range(2)]
              psB_ = [ps(f"pgb{i}", [128, D], F32) for i in range(2)]
              rows = [(T["xa_p"], r, 128, r) for r in range(0, SEQ, 128)] + [(T["xa_s"], 0, 64, SEQ)]
              for i, (xa, r, n, col) in enumerate(rows):
                  k = i % 2
                  load(xt[k][0:n, :], f"xt{k}", xa[r:r + n, :], sem=f"xt{k}")
                  for (pp, w_, nm) in ((psA_[k], wa, f"pga{k}"), (psB_[k], wb, f"pgb{k}")):
                      fns = []
                      for half in range(2):
                          for kc in range(8):
                              fns.append(I("matmul", pp[0:n, half * 512:(half + 1) * 512], lhsT=zT[:, kc, col:col + n], rhs=w_[:, kc, half * 512:(half + 1) * 512], start=(kc == 0), stop=(kc == 7)))
                      P.op("pe", fns, reads=["zT", ("wga" if w_ is wa else "wgb")], writes=[nm + "/x"])
                  P.op("act", I("activation", out=sg_[k][0:n, :], in_=psB_[k][0:n, :], func=AF.Sigmoid), reads=[f"pgb{k}/x"], writes=["sg0"])
                  P.op("dve", I("tensor_tensor", out=mt[k][0:n, :], in0=psA_[k][0:n, :], in1=sg_[k][0:n, :], op=ALU.mult), reads=[f"pga{k}/x", "sg0"], writes=["mt0"])
                  postnorm_res(env, mt[k][0:n, :], "mt0", n, gpost, "gpost", xt[k][0:n, :], f"xt{k}", xo[k][0:n, :], f"xo{k}", eng2="pool")
                  store(xa[r:r + n, :], xo[k][0:n, :], f"xo{k}", f"xo{k}")

        if KS == "c":
            return
        with phase() as (sb, ps):
            psF = ps("psF", [128, 128], F32)
            Fo = sb("Fo", [128, 128], F32)
            P.op("pe", I("transpose", out=psF[:, :], in_=HfinP[:, :], identity=idf[:, :]), reads=["HfinP", "idf"], writes=["psF/x"])
            P.op("dve", I("tensor_copy", out=Fo[:, :], in_=psF[:, :]), reads=["psF/x"], writes=["Fo"])
            store(T["s5p"].rearrange("ri pr q -> (ri pr) q"), Fo[0:64, :], "Fo", "s5p")
            psO = [ps(f"psO{i}", [128, 128], F32) for i in range(2)]
            Xo = [sb(f"Xo{i}", [128, 128], F32) for i in range(2)]
            k = 0
            for ri in range(2):
                for b4 in range(NB // 4):
                    po, xo_ = psO[k % 2], Xo[k % 2]
                    P.op("pe", I("transpose", out=po[:, :], in_=HsOut[:, ri, 4 * b4:4 * b4 + 4, :].rearrange("p b q -> p (b q)"), identity=idf[:, :]),
                         reads=["HsOut", "idf"], writes=[f"psO{k % 2}/x"])
                    if k % 2 == 0:
                        P.op("dve", I("tensor_copy", out=xo_[:, :], in_=po[:, :]), reads=[f"psO{k % 2}/x"], writes=[f"Xo{k % 2}"])
                    else:
                        P.op("act", I("copy", out=xo_[:, :], in_=po[:, :]), reads=[f"psO{k % 2}/x"], writes=[f"Xo{k % 2}"])
                    for i4 in range(4):
                        store(T["s5s"][4 * b4 + i4, ri, :].rearrange("(pr q) -> pr q", q=128), xo_[32 * i4:32 * i4 + 32, :], f"Xo{k % 2}", f"s5s{k % 2}")
                    k += 1

def build(stage=99):
    nc = bass.Bass("TRN2", target_bir_lowering=False)
    T = {}
    for k, shp in IN_SPECS.items():
        T[k] = nc.dram_tensor(k, shp, F32, kind="ExternalInput").ap()
    for k, (shp, dt_) in CONST_SPECS.items():
        T[k] = nc.dram_tensor(k, shp, dt_, kind="ExternalInput").ap()
    for k, shp in OUT_SPECS.items():
        T[k] = nc.dram_tensor(k, shp, F32, kind="ExternalOutput").ap()
    T["xa_p"] = nc.dram_tensor("xa_p", [SEQ, D], F32).ap()
    T["xa_s"] = nc.dram_tensor("xa_s", [NS, D], F32).ap()

    with contextlib.ExitStack() as ges:
        P = Prog(nc, ges)
        uid = [0]

        def load(dst, dname, src, eng="sp", sem=None, **kw):
            P.dma(eng, I("dma_start", out=dst, in_=src, **kw), sem or ("ld_" + dname), writes=[dname])

        def store(dst, src, sname, sem):
            P.dma("sp", I("dma_start", out=dst, in_=src), sem, reads=(sname if isinstance(sname, list) else [sname]))

        @contextlib.contextmanager
        def phase():
            with contextlib.ExitStack() as st:
                def sb(n, s, d):
                    uid[0] += 1
                    return st.enter_context(nc.sbuf_tensor(f"{n}_u{uid[0]}", s, d))

                def ps(n, s, d):
                    uid[0] += 1
                    return st.enter_context(nc.psum_tensor(f"{n}_u{uid[0]}", s, d))
                yield sb, ps
                P.barrier()
                if os.environ.get("K_MULTIBLOCK", "0") == "1":
                    P.flush()

        def mk_norm_env(sb, tag, nhn=2, ntmp=1):
            env = {"r": 0}
            env["junk"] = sb("junk" + tag, [128, D], F32)
            env["stat"] = sb("stat" + tag, [128, 4, 4], F32)
            env["eps"] = sb("eps" + tag, [128, 1], F32)
            env["hn"] = [sb(f"hn{tag}{i}", [128, D], BF16) for i in range(nhn)]
            env["tmps"] = [sb(f"ntmp{tag}{i}", [128, D], F32) for i in range(ntmp)]
            env["tk"] = 0
            env["ident"] = sb("ident" + tag, [128, 128], BF16)
            P.op("dve", I("memset", env["eps"][:], EPS), writes=["eps"])
            load(env["ident"][:], "ident", T["ident_bf"])
            return env

        def rstd_ops(env, n, src_ap, srcn, width):
            r = env["r"] = (env["r"] + 1) % 4
            st_ = env["stat"]
            P.op("act", I("activation", out=env["junk"][0:n, 0:width], in_=src_ap, func=AF.Square, accum_out=st_[0:n, r, 0:1]),
                 reads=(srcn if isinstance(srcn, list) else [srcn]), writes=[f"st{r}0"])
            P.op("act", I("activation", out=st_[0:n, r, 1:2], in_=st_[0:n, r, 0:1], func=AF.Sqrt, scale=1.0 / width, bias=env["eps"][0:n, 0:1]),
                 reads=[f"st{r}0", "eps"], writes=[f"st{r}1"])
            P.op("dve", I("reciprocal", out=st_[0:n, r, 2:3], in_=st_[0:n, r, 1:2]), reads=[f"st{r}1"], writes=[f"st{r}2"])
            return st_[0:n, r, 2:3], f"st{r}2"

        def prenorm_a(env, x_ap, xn, n, gain, gn):
            rs, rsn = rstd_ops(env, n, x_ap, xn, D)
            k = env["hk"] = (env.get("hk", -1) + 1) % len(env["hn"])
            hn = env["hn"][k]
            P.op("dve", I("scalar_tensor_tensor", out=hn[0:n, :], in0=x_ap, scalar=rs, in1=gain[0:n, :], op0=ALU.mult, op1=ALU.mult),
                 reads=[xn, rsn, gn], writes=[f"hn{k}"])
            return k

        def prenorm_b(env, k, psT, psTn, n, dst, dstn):
            hn = env["hn"][k]
            idt = env["ident"]
            P.op("pe", [I("transpose", out=psT[:, kc, 0:n], in_=hn[0:n, kc * 128:(kc + 1) * 128], identity=idt[0:n, 0:n]) for kc in range(8)],
                 reads=[f"hn{k}", "ident"], writes=[psTn])
            P.op("dve", I("tensor_copy", out=dst, in_=psT[:, :, 0:n]), reads=[psTn], writes=[dstn])

        def prenorm(env, psT, psTn, x_ap, xn, n, gain, gn, dst, dstn):
            k = prenorm_a(env, x_ap, xn, n, gain, gn)
            prenorm_b(env, k, psT, psTn, n, dst, dstn)

        def postnorm_res(env, m_ap, mn, n, gain, gn, x_ap, xn, out_ap, outn, eng2="dve"):
            rs, rsn = rstd_ops(env, n, m_ap, mn, D)
            tk = env["tk"] = (env["tk"] + 1) % len(env["tmps"])
            tmp = env["tmps"][tk]
            P.op("dve", I("scalar_tensor_tensor", out=tmp[0:n, :], in0=m_ap, scalar=rs, in1=gain[0:n, :], op0=ALU.mult, op1=ALU.mult),
                 reads=(mn if isinstance(mn, list) else [mn]) + [rsn, gn], writes=[f"ntmp{tk}"])
            P.op(eng2, I("tensor_tensor", out=out_ap, in0=tmp[0:n, :], in1=x_ap, op=ALU.add), reads=[f"ntmp{tk}", xn], writes=[outn])

        def load_gain(sb, name, src_row):
            t = sb(name, [128, D], F32)
            load(t[:], name, src_row.partition_broadcast(128))
            return t

        def load_w(sb, name, src, kchunks, ncols):
            t = sb(name, [128, kchunks, ncols], BF16)
            half = max(1, kchunks // 2)
            for c0 in range(0, kchunks, half):
                load(t[:, c0:c0 + half, :], name, src[c0 * 128:(c0 + half) * 128, :].rearrange("(c p) n -> p c n", p=128), eng="pool")
            return t

        def gla_core(env, C, NCH, ntok, maskT, m01, x_rows_src, x_rows_dst, sample, W, x_rows_src2=None, x_rows_dst2=None, pre_done=False, next_src=None):
            win, wout, wg, nbg, gpre, gpost, ghead = W["win"], W["wout"], W["wg"], W["nbg"], W["gpre"], W["gpost"], W["ghead"]
            B = W["banks"]
            HN = H * NCH
            nct = ntok // 64
            tl = W["tiles"]
            hT, qT, kT, spt, csp, dtmp, etmp = tl["hT"], tl["qT"], tl["kT"], tl["sp"], tl["csp"], tl["dtmp"], tl["etmp"]
            qin, kin, qe, kdec, glT, vt, srt, decay = tl["qin"], tl["kin"], tl["qe"], tl["kdec"], tl["glT"], tl["v"], tl["sr"], tl["decay"]
            psT = B["psT"]
            def pre_chunk(c, src_fn):
                xt = tl["xt"][c % 4]
                load(xt[0:64, :], f"xt{c % 4}", src_fn(c), sem=f"xt{c % 4}")
                k_ = prenorm_a(env, xt[0:64, :], f"xt{c % 4}", 64, gpre, "gpre")
                prenorm_b(env, k_, psT, "bq0/psT", 64, hT[:, :, c * 64:(c + 1) * 64], "hT")

            if not pre_done:
                ks = []
                for c in range(nct):
                    xt = tl["xt"][c % 4]
                    load(xt[0:64, :], f"xt{c % 4}", x_rows_src(c), sem=f"xt{c % 4}")
                    ks.append(prenorm_a(env, xt[0:64, :], f"xt{c % 4}", 64, gpre, "gpre"))
                for c in range(nct):
                    prenorm_b(env, ks[c], psT, "bq0/psT", 64, hT[:, :, c * 64:(c + 1) * 64], "hT")
            CUT = os.environ.get("KG_CUT", "Z")
            if CUT == "A":
                return
            for j in range(8):
                pq = B["psq"][j % 2]
                P.op("pe", [I("matmul", pq[:, 0:ntok], lhsT=win[:, kc, j * 128:(j + 1) * 128], rhs=hT[:, kc, 0:ntok], start=(kc == 0), stop=(kc == 7)) for kc in range(8)],
                     reads=["win_qk", "hT"], writes=[f"bq{j % 2}/psq"])
                if j < 4:
                    P.op("act", I("mul", out=qT[:, j, 0:ntok], in_=pq[:, 0:ntok], mul=DK ** -0.5), reads=[f"bq{j % 2}/psq"], writes=["qT"])
                else:
                    P.op("dve", I("tensor_copy", out=kT[:, j - 4, 0:ntok], in_=pq[:, 0:ntok]), reads=[f"bq{j % 2}/psq"], writes=["kT"])
            if CUT == "B1":
                return
            pq = B["psq"][0]
            P.op("pe", [I("matmul", pq[0:16, 0:ntok], lhsT=win[:, kc, 3072:3088], rhs=hT[:, kc, 0:ntok], start=(kc == 0), stop=(kc == 7)) for kc in range(8)],
                 reads=["win_gl", "hT"], writes=["bq0/psq"])
            P.op("act", I("copy", out=glT[0:16, 0:ntok], in_=pq[0:16, 0:ntok]), reads=["bq0/psq"], writes=["glT"])
            if CUT == "B2":
                return
            for h in range(H):
                k2 = (h + 1) % 2
                pq = B["psq"][k2]
                P.op("pe", I("matmul", pq[:, 0:ntok], lhsT=wg[0:16, h * 128:(h + 1) * 128], rhs=glT[0:16, 0:ntok], start=True, stop=True),
                     reads=["wg", "glT"], writes=[f"bq{k2}/psq"])
                P.op("act", I("activation", out=spt[:, h, 0:ntok], in_=pq[:, 0:ntok], func=AF.Exp, scale=-1.0, bias=nbg[:, h:h + 1]),
                     reads=[f"bq{k2}/psq", "nbg"], writes=["sp"])
            if CUT == "B3":
                return
            P.op("act", I("activation", out=spt[:, :, 0:ntok], in_=spt[:, :, 0:ntok], func=AF.Ln, bias=1.0), reads=["sp"], writes=["sp"])
            if CUT == "B4":
                return
            def emit_piece(c, piece):
                pv = B["psv"][piece % 2]
                P.op("pe", [I("matmul", pv[0:64, :], lhsT=hT[:, kc, c * 64:(c + 1) * 64], rhs=win[:, kc, 1024 + piece * 512:1536 + piece * 512], start=(kc == 0), stop=(kc == 7)) for kc in range(8)],
                     reads=["win_vr", "hT"], writes=[f"bv{piece % 2}/psv"])
                if piece < 2:
                    P.op("dve", I("tensor_copy", out=vt[0:64, c, piece * 512:(piece + 1) * 512], in_=pv[0:64, :]), reads=[f"bv{piece % 2}/psv"], writes=["v"])
                else:
                    sl = slice((piece - 2) * 512, (piece - 1) * 512)
                    P.op("act", I("activation", out=srt[0:64, c, sl], in_=pv[0:64, :], func=AF.Silu), reads=[f"bv{piece % 2}/psv"], writes=["sr"])
                    P.op("dve", I("tensor_tensor", out=srt[0:64, c, sl], in0=srt[0:64, c, sl], in1=ghead[0:64, sl], op=ALU.mult), reads=["sr", "ghead"], writes=["sr"])

            b2_ops = ([(lambda c=c, piece=piece: emit_piece(c, piece)) for c in range(nct) for piece in (0, 1)]
                      + [(lambda c=c, piece=piece: emit_piece(c, piece)) for c in range(nct) for piece in (2, 3)])
            if CUT == "B":
                for f in b2_ops:
                    f()
                return

            def v3(t):
                return t[:, :, 0:ntok].rearrange("p h (c t) -> p (h c) t", t=C)
            n_el = H * ntok
            cv = v3(csp)
            EX = lambda src_, sc_: I("activation", out=etmp[:, :, 0:ntok], in_=src_[:, :, 0:ntok], func=AF.Exp, scale=sc_)
            MUL = lambda dst_, a_: I("tensor_tensor", out=dst_[:, :, 0:ntok], in0=a_[:, :, 0:ntok], in1=etmp[:, :, 0:ntok], op=ALU.mult)
            c_ops = [
                lambda: P.op("dve", I("tensor_tensor_scan", out=csp[:, :, :].rearrange("p h t -> p (h t)"), data0=m01[:, 0:n_el], data1=spt[:, :, :].rearrange("p h t -> p (h t)"), initial=0.0, op0=ALU.mult, op1=ALU.add),
                             reads=["sp", "m01"], writes=["csp"]),
                lambda: P.op("dve", I("tensor_tensor", out=v3(dtmp), in0=cv, in1=cv[:, :, C // 2:C // 2 + 1].broadcast_to([128, HN, C]), op=ALU.subtract), reads=["csp"], writes=["dtmp"]),
                lambda: P.op("act", EX(dtmp, -1.0 / 16.0), reads=["dtmp"], writes=["etmp"]),
                lambda: P.op("dve", MUL(qin, qT), reads=["qT", "etmp"], writes=["qin"]),
                lambda: P.op("act", EX(dtmp, 1.0 / 16.0), reads=["dtmp"], writes=["etmp"]),
                lambda: P.op("dve", MUL(kin, kT), reads=["kT", "etmp"], writes=["kin"]),
                lambda: P.op("act", EX(csp, -1.0 / 16.0), reads=["csp"], writes=["etmp"]),
                lambda: P.op("dve", MUL(qe, qT), reads=["qT", "etmp"], writes=["qe"]),
                lambda: P.op("pool", I("tensor_copy", out=decay[:, 0:HN, :], in_=v3(etmp)[:, :, C - 1:C]), reads=["etmp"], writes=["decay"]),
                lambda: P.op("dve", I("tensor_tensor", out=v3(dtmp), in0=cv, in1=cv[:, :, C - 1:C].broadcast_to([128, HN, C]), op=ALU.subtract), reads=["csp"], writes=["dtmp"]),
                lambda: P.op("act", EX(dtmp, 1.0 / 16.0), reads=["dtmp"], writes=["etmp"]),
                lambda: P.op("dve", MUL(kdec, kT), reads=["kT", "etmp"], writes=["kdec"]),
            ]
            bi = ci = 0
            while bi < len(b2_ops) or ci < len(c_ops):
                if bi < len(b2_ops):
                    b2_ops[bi]()
                    bi += 1
                if ci < len(c_ops):
                    c_ops[ci]()
                    ci += 1
            if CUT == "C":
                return
            S, Sbf = W["S"], W["Sbf"]
            att, kds, sq, ssum, ogb, oT, xr, xo = tl["att"], tl["kds"], tl["sq"], tl["ssum"], tl["ogb"], tl["oT"], tl["xr"], tl["xo"]
            psa, psK, pso, psu, psTo, psm = B["psa"], B["psK"], B["pso"], B["psu"], B["psTo"], B["psm"]
            idt = env["ident"]
            osb = tl["osb"]

            def og_stage(c):
                ob = ogb[c % 2]
                P.op("act", I("activation", out=sq[0:64, :], in_=osb[0:64, :], func=AF.Square), reads=["osb"], writes=["sq"])
                P.op("dve", I("tensor_reduce", out=ssum[0:64, 0, :], in_=sq[0:64, :].rearrange("p (h v) -> p h v", h=H), axis=AX.X, op=ALU.add), reads=["sq"], writes=["ssum0"])
                P.op("act", I("activation", out=ssum[0:64, 1, :], in_=ssum[0:64, 0, :], func=AF.Sqrt, scale=1.0 / DV, bias=env["eps"][0:64, 0:1]), reads=["ssum0", "eps"], writes=["ssum1"])
                P.op("dve", I("reciprocal", out=ssum[0:64, 2, :], in_=ssum[0:64, 1, :]), reads=["ssum1"], writes=["ssum2"])
                for h in range(H):
                    P.op("dve", I("scalar_tensor_tensor", out=ob[0:64, h * DV:(h + 1) * DV], in0=osb[0:64, h * DV:(h + 1) * DV], scalar=ssum[0:64, 2, h:h + 1], in1=srt[0:64, c, h * DV:(h + 1) * DV], op0=ALU.mult, op1=ALU.mult),
                         reads=["osb", "ssum2", "sr"], writes=[f"ogb{c % 2}"])

            def op_stage(cs):
                n = 64 * len(cs)
                fns = []
                for i_, c in enumerate(cs):
                    ob = ogb[c % 2]
                    for kc in range(8):
                        fns.append(I("transpose", out=psTo[:, kc, 64 * i_:64 * i_ + 64], in_=ob[0:64, kc * 128:(kc + 1) * 128], identity=idt[0:64, 0:64]))
                P.op("pe", fns, reads=[f"ogb{c % 2}" for c in cs] + ["ident"], writes=["b0/psTo"])
                P.op("act", I("copy", out=oT[:, :, 0:n], in_=psTo[:, :, 0:n]), reads=["b0/psTo"], writes=["oT"])
                fns = []
                for half in range(2):
                    for kc in range(8):
                        fns.append(I("matmul", psm[0:n, half * 512:(half + 1) * 512], lhsT=oT[:, kc, 0:n], rhs=wout[:, kc, half * 512:(half + 1) * 512], start=(kc == 0), stop=(kc == 7)))
                P.op("pe", fns, reads=["oT", "wout"], writes=["bv0/psv", "bv1/psv"])
                k2 = (cs[0] // 2) % 2
                xr_, xo_ = xr[0], xo[k2]
                src_ap, dst_ap = x_rows_src2(cs[0], n), x_rows_dst2(cs[0], n)
                load(xr_[0:n, :], "xr0", src_ap, sem="xr0")
                postnorm_res(env, psm[0:n, :], ["bv0/psv", "bv1/psv"], n, gpost, "gpost", xr_[0:n, :], "xr0", xo_[0:n, :], f"xo{k2}", eng2="pool")
                store(dst_ap, xo_[0:n, :], f"xo{k2}", f"xo{k2}")

            def p1_stage(c):
                c0, c1 = c * 64, (c + 1) * 64
                k = c % 2
                for h in range(H):
                    P.op("pe", I("matmul", psa[0:64, h * 64:(h + 1) * 64], lhsT=kin[:, h, c0:c1], rhs=qin[:, h, c0:c1], start=True, stop=True), reads=["kin", "qin"], writes=[f"bq0/psa{h}"])
                    P.op("pe", I("transpose", out=psK[0:64, h * 128:(h + 1) * 128], in_=kdec[:, h, c0:c1], identity=idt[:, :]), reads=["kdec", "ident"], writes=[f"bq1/psK{h}"])
                for h in range(H):
                    P.op("dve", I("tensor_tensor", out=att[k * H + h][0:64, :], in0=psa[0:64, h * 64:(h + 1) * 64], in1=maskT[0:64, :], op=ALU.mult), reads=[f"bq0/psa{h}", "maskT"], writes=[f"att{k}_{h}"])
                    P.op("act", I("copy", out=kds[k * H + h][0:64, :], in_=psK[0:64, h * 128:(h + 1) * 128]), reads=[f"bq1/psK{h}"], writes=[f"kds{k}_{h}"])

            p1_stage(0)
            for c in range(nct):
                c0, c1 = c * 64, (c + 1) * 64
                if c + 1 < nct:
                    p1_stage(c + 1)
                for h in range(H):
                    k = c % 2
                    att_h, kds_h = att[k * H + h], kds[k * H + h]
                    attn, kdsn = f"att{k}_{h}", f"kds{k}_{h}"
                    vh = vt[0:64, c, h * DV:(h + 1) * DV]
                    if not sample:
                        P.op("pe", [I("matmul", pso[0:64, h, :], lhsT=att_h[0:64, :], rhs=vh, start=True, stop=False),
                                    I("matmul", pso[0:64, h, :], lhsT=qe[:, h, c0:c1], rhs=Sbf[:, h, :], start=False, stop=True)],
                             reads=[attn, "v", "qe", f"Sbf{h}"], writes=[f"bo{h // 2}/pso{h}"])
                        pu = psu[h % 2]
                        P.op("pe", I("matmul", pu[:, :], lhsT=kds_h[0:64, :], rhs=vh, start=True, stop=True), reads=[kdsn, "v"], writes=[f"bu/psu{h % 2}"])
                        di = h * NCH + c
                        P.op("dve", I("scalar_tensor_tensor", out=S[:, h, :], in0=S[:, h, :], scalar=decay[:, di, 0:1], in1=pu[:, :], op0=ALU.mult, op1=ALU.add),
                             reads=[f"S{h}", "decay", f"bu/psu{h % 2}"], writes=[f"S{h}"])
                        P.op("act", I("copy", out=Sbf[:, h, :], in_=S[:, h, :]), reads=[f"S{h}"], writes=[f"Sbf{h}"])
                    else:
                        S0, S0bf, Sn, qeM, kdM = W["S0"], W["S0bf"], W["Sn"], W["qeM"], W["kdM"]
                        colmask, rowmask = W["colmask"], W["rowmask"]
                        HB_ = NB // 2
                        for hf in range(2):
                            bs = slice(hf * HB_, (hf + 1) * HB_)
                            load(S0[:, bs, :], f"S0_{hf}", T["sg"][bs, h, :, :].rearrange("b k v -> k b v"), sem=f"S0_{hf}")
                        P.op("dve", I("tensor_tensor", out=qeM[:, :, :], in0=colmask[:, :, :], in1=qe[:, h:h + 1, 0:64].broadcast_to([128, NB, 64]), op=ALU.mult),
                             reads=["qe", "colmask"], writes=["qeM"])
                        P.op("dve", I("tensor_tensor", out=kdM[0:64, :, :], in0=kds_h[0:64, :].rearrange("p (o k) -> p o k", o=1).broadcast_to([64, NB, 128]),
                                      in1=rowmask[0:64, :].rearrange("p (b o) -> p b o", o=1).broadcast_to([64, NB, 128]), op=ALU.mult),
                             reads=[kdsn, "rowmask"], writes=["kdM"])
                        P.op("pe", I("matmul", pso[0:64, h, :], lhsT=att_h[0:64, :], rhs=vh, start=True, stop=False), reads=[attn, "v"], writes=[f"bo{h // 2}/pso{h}"])
                        for hf in range(2):
                            bs = slice(hf * HB_, (hf + 1) * HB_)
                            P.op("act", I("copy", out=S0bf[:, bs, :], in_=S0[:, bs, :]), reads=[f"S0_{hf}"], writes=[f"S0bf_{hf}"])
                            fns = []
                            for b in range(hf * HB_, (hf + 1) * HB_):
                                fns.append(I("matmul", pso[0:64, h, :], lhsT=qeM[:, b, :], rhs=S0bf[:, b, :], start=False, stop=(b == NB - 1)))
                            P.op("pe", fns, reads=["qeM", f"S0bf_{hf}"], writes=[f"bo{h // 2}/pso{h}"])
                            for b in range(hf * HB_, (hf + 1) * HB_):
                                pu = psu[b % 2]
                                P.op("pe", I("matmul", pu[:, :], lhsT=kdM[0:64, b, :], rhs=vh, start=True, stop=True), reads=["kdM", "v"], writes=[f"bu/psu{b % 2}"])
                                di = h * NCH + b
                                P.op("dve", I("scalar_tensor_tensor", out=Sn[:, b, :], in0=S0[:, b, :], scalar=decay[:, di, 0:1], in1=pu[:, :], op0=ALU.mult, op1=ALU.add),
                                     reads=[f"S0_{hf}", "decay", f"bu/psu{b % 2}"], writes=[f"Sn_{hf}"])
                            store(T["glas"][bs, h, :, :].rearrange("b k v -> k b v"), Sn[:, bs, :], f"Sn_{hf}", f"glas{hf}")
                if CUT == "D":
                    continue
                allpso = [f"bo{h // 2}/pso{h}" for h in range(H)]
                P.op("act", I("copy", out=osb[0:64, :], in_=pso[0:64, :, :].rearrange("p h v -> p (h v)")), reads=allpso, writes=["osb"])
                if c >= 2 and c % 2 == 0:
                    op_stage([c - 2, c - 1])
                og_stage(c)
                if next_src is not None:
                    pre_chunk(c, next_src)
            if CUT != "D":
                if nct % 2 == 0:
                    op_stage([nct - 2, nct - 1])
                else:
                    op_stage([nct - 1])

        def gla_phase():
            with phase() as (sb, ps):
                env = mk_norm_env(sb, "g", nhn=4)
                W = {}
                win_t = sb("win", [128, 8, GIN], BF16)
                for c0 in (0, 4):
                    P.dma("pool", I("dma_start", out=win_t[:, c0:c0 + 4, :], in_=T["gla_w_in"][0][c0 * 128:(c0 + 4) * 128, :].rearrange("(c p) n -> p c n", p=128)), "ld_win", writes=["win_qk", "win_gl", "win_vr"])
                W["win"] = win_t
                W["wout"] = load_w(sb, "wout", T["gla_w_out"][0], 8, D)
                wg = sb("wg", [16, 512], BF16)
                load(wg[:, :], "wg", T["gla_w_gate_up"][0], eng="pool")
                W["wg"] = wg
                nbg = sb("nbg", [128, H], F32)
                load(nbg[:, :], "nbg", T["gla_b_gate"].rearrange("o (h p) -> p (o h)", p=128), allow_slow_non_contiguous=True)
                P.op("dve", I("tensor_scalar", out=nbg[:, :], in0=nbg[:, :], scalar1=-1.0, scalar2=None, op0=ALU.mult), reads=["nbg"], writes=["nbg"])
                W["nbg"] = nbg
                W["gpre"] = load_gain(sb, "gpre", T["g_pre_mix"][0:1, :])
                W["gpost"] = load_gain(sb, "gpost", T["g_post_mix"][0:1, :])
                W["ghead"] = load_gain(sb, "ghead", T["gla_g_head"][0:1, :])
                b23 = ps("b23", [128, 1024], F32)
                b67 = ps("b67", [128, H, DV], F32)
                bA = ps("bA", [128, 8, 128], BF16)
                bq0 = ps("bq0", [128, 512], F32)
                bq1 = ps("bq1", [128, 512], F32)
                bE = ps("bE", [128, 512], F32)
                bq0b = bq0.bitcast(BF16)
                bq1b = bq1.bitcast(BF16)
                W["banks"] = {"psT": bq0b[:, 512:1024].rearrange("p (k t) -> p k t", k=8), "psTo": bA, "psq": [bq0[:, 0:256], bq1[:, 0:256]],
                              "psv": [b23[:, 0:512], b23[:, 512:1024]], "psm": b23, "psu": [bE[:, 0:256], bE[:, 256:512]],
                              "psa": bq0[:, 0:256], "psK": bq1b[:, 0:512], "pso": b67}
                m01p = sb("m01p", [128, 1024], F32)
                load(m01p[:, :], "m01", T["m01p"])
                maskTp = sb("maskTp", [64, 64], F32)
                load(maskTp[:, :], "maskT", T["maskT_p"])
                S = sb("S", [128, H, DV], F32)
                Sbf = sb("Sbf", [128, H, DV], BF16)
                for h in range(H):
                    P.op("dve", I("memset", S[:, h, :], 0.0), writes=[f"S{h}"])
                    P.op("pool", I("memset", Sbf[:, h, :], 0.0), writes=[f"Sbf{h}"])
                W["S"], W["Sbf"] = S, Sbf

                def mk_tiles(sbx, ntok, HNmax):
                    tl = {}
                    tl["xt"] = [sbx(f"xt{i}", [64, D], F32) for i in range(4)] if ntok > 64 else [sbx("xt0", [64, D], F32)] * 4
                    tl["xr"] = [sbx("xr0", [128, D], F32)] * 2
                    tl["xo"] = [sbx(f"xo{i}", [128, D], F32) for i in range(2)] if ntok > 64 else [sbx("xo0", [128, D], F32)] * 2
                    tl["hT"] = sbx("hT", [128, 8, ntok], BF16)
                    for nme in ("qT", "kT", "sp", "csp", "dtmp", "etmp"):
                        tl[nme] = sbx(nme, [128, H, ntok], F32)
                    for nme in ("qin", "kin", "qe", "kdec"):
                        tl[nme] = sbx(nme, [128, H, ntok], BF16)
                    tl["glT"] = sbx("glT", [16, ntok], BF16)
                    tl["v"] = sbx("v", [64, ntok // 64, D], BF16)
                    tl["sr"] = sbx("sr", [64, ntok // 64, D], BF16)
                    tl["decay"] = sbx("decay", [128, HNmax, 1], F32)
                    tl["att"] = [sbx(f"att{i}", [64, 64], BF16) for i in range(2 * H)]
                    tl["kds"] = [sbx(f"kds{i}", [64, 128], BF16) for i in range(2 * H)]
                    tl["sq"] = sbx("sq", [64, D], F32)
                    tl["ssum"] = sbx("ssum", [64, 3, H], F32)
                    tl["osb"] = sbx("osb", [64, D], F32)
                    tl["ogb"] = [sbx(f"ogb{i}", [64, D], BF16) for i in range(2)]
                    tl["oT"] = sbx("oT", [128, 8, 128], BF16)
                    return tl

                with contextlib.ExitStack() as st2:
                    def sb2(n, s, d, st2=st2):
                        uid[0] += 1
                        return st2.enter_context(nc.sbuf_tensor(f"{n}_u{uid[0]}", s, d))
                    W["tiles"] = mk_tiles(sb2, 256, 16)
                    NTL = int(os.environ.get("KG_TILES", SEQ // 256))
                    for t in range(NTL):
                        nsrc = (lambda c, t=t: T["xp"][(t + 1) * 256 + c * 64:(t + 1) * 256 + (c + 1) * 64, :]) if t + 1 < NTL else None
                        gla_core(env, 64, 4, 256, maskTp, m01p,
                                 lambda c, t=t: T["xp"][t * 256 + c * 64:t * 256 + (c + 1) * 64, :],
                                 lambda c, t=t: T["xa_p"][t * 256 + c * 64:t * 256 + (c + 1) * 64, :], False, W,
                                 lambda c, n, t=t: T["xp"][t * 256 + c * 64:t * 256 + c * 64 + n, :],
                                 lambda c, n, t=t: T["xa_p"][t * 256 + c * 64:t * 256 + c * 64 + n, :],
                                 pre_done=(t > 0), next_src=nsrc)
                    store(T["glap"].rearrange("h k v -> k h v"), S[:, :, :], [f"S{h}" for h in range(H)], "glap")
                    P.barrier()
                    if os.environ.get("K_MULTIBLOCK", "0") == "1":
                        P.flush()
                with contextlib.ExitStack() as st2:
                  if os.environ.get("KG_SAMPLE", "1") == "1":
                    def sb2(n, s, d, st2=st2):
                        uid[0] += 1
                        return st2.enter_context(nc.sbuf_tensor(f"{n}_u{uid[0]}", s, d))
                    W["tiles"] = mk_tiles(sb2, 64, 64)
                    m01s = sb2("m01s", [128, 256], F32)
                    load(m01s[:, :], "m01", T["m01s"])
                    maskTs = sb2("maskTs", [64, 64], F32)
                    load(maskTs[:, :], "maskT", T["maskT_s"])
                    W["colmask"] = sb2("colmask", [128, NB, 64], F32)
                    load(W["colmask"][:, :, :], "colmask", T["colmask"])
                    W["rowmask"] = sb2("rowmask", [64, NB], F32)
                    load(W["rowmask"][:, :], "rowmask", T["rowmask"])
                    W["S0"] = sb2("S0", [128, NB, DV], F32)
                    W["S0bf"] = sb2("S0bf", [128, NB, DV], BF16)
                    W["Sn"] = sb2("Sn", [128, NB, DV], F32)
                    W["qeM"] = sb2("qeM", [128, NB, 64], BF16)
                    W["kdM"] = sb2("kdM", [64, NB, 128], BF16)
                    gla_core(env, 4, 16, 64, maskTs, m01s, lambda c: T["xs"][0:64, :], lambda c: T["xa_s"][0:64, :], True, W,
                             lambda c, n: T["xs"][0:n, :], lambda c, n: T["xa_s"][0:n, :])
                    P.barrier()
                    if os.environ.get("K_MULTIBLOCK", "0") == "1":
                        P.flush()

        def mlp_phase(layer, src_p, src_s, dst_p, dst_s):
            NT = SEQ + NS
            NBF = FF // 512
            with phase() as (sb, ps):
                env = mk_norm_env(sb, "m", nhn=2, ntmp=2)
                gpre = load_gain(sb, "gpre", T["g_pre_mlp"][layer:layer + 1, :])
                gpost = load_gain(sb, "gpost", T["g_post_mlp"][layer:layer + 1, :])
                hT = sb("hT", [128, 8, NT], BF16)
                facc = sb("facc", [128, 17, D], F32)
                aT = [sb(f"aT{i}", [128, 4, NT], BF16) for i in range(2)]
                wu = [sb(f"wu{i}", [128, 8, 512], BF16) for i in range(2)]
                wd = sb("wd", [128, 4, D], BF16)
                xt = [sb(f"xt{i}", [128, D], F32) for i in range(2)]
                xo = [sb(f"xo{i}", [128, D], F32) for i in range(2)]
                ar = [sb(f"ar{i}", [128, 512], BF16) for i in range(2)]
                psd = [ps(f"psd{i}", [128, D], F32) for i in range(2)]
                psT = ps("psT", [128, 8, 128], BF16)
                psu = [ps(f"psu{i}", [128, 512], F32) for i in range(2)]
                tiles = [(src_p, dst_p, r, 128, r) for r in range(0, SEQ, 128)] + [(src_s, dst_s, 0, NS, SEQ)]
                blocks = [(t0, 512) for t0 in range(0, SEQ, 512)] + [(SEQ, NS)]

                def load_wu(b):
                    for k0 in (0, 4):
                        load(wu[b % 2][:, k0:k0 + 4, :], f"wu{b % 2}", T["w_up"][layer][k0 * 128:(k0 + 4) * 128, b * 512:(b + 1) * 512].rearrange("(c p) n -> p c n", p=128), eng="pool", sem=f"wu{b % 2}")

                def load_wd(b):
                    load(wd[:, :, :], "wd", T["w_down"][layer][b * 512:(b + 1) * 512, :].rearrange("(c p) n -> p c n", p=128), eng="pool", sem="wd")

                def pre_block(bi_):
                    t0, n = blocks[bi_]
                    its = [tt for tt in tiles if t0 <= tt[4] < t0 + n]
                    for p0 in range(0, len(its), 2):
                        ks = []
                        for i_, (src, dst, r, m, col) in enumerate(its[p0:p0 + 2]):
                            load(xt[i_][0:m, :], f"xt{i_}", src[r:r + m, :], sem=f"xt{i_}")
                            ks.append(prenorm_a(env, xt[i_][0:m, :], f"xt{i_}", m, gpre, "gpre"))
                        for i_, (src, dst, r, m, col) in enumerate(its[p0:p0 + 2]):
                            prenorm_b(env, ks[i_], psT, "psT/x", m, hT[:, :, col:col + m], f"hT{bi_}")

                load_wu(0)
                load_wd(0)
                load_wu(1)
                ku = [0]

                def up_tb(b, bi_):
                    t0, n = blocks[bi_]
                    a_b = aT[b % 2]
                    for fc in range(4):
                        k_ = ku[0] % 2
                        pu, a_ = psu[k_], ar[k_]
                        P.op("pe", [I("matmul", pu[:, 0:n], lhsT=wu[b % 2][:, kc, fc * 128:(fc + 1) * 128], rhs=hT[:, kc, t0:t0 + n], start=(kc == 0), stop=(kc == 7)) for kc in range(8)],
                             reads=[f"wu{b % 2}", f"hT{bi_}"], writes=[f"psu{k_}/x"])
                        P.op("act", I("activation", out=a_[:, 0:n], in_=pu[:, 0:n], func=AF.Relu), reads=[f"psu{k_}/x"], writes=[f"ar{k_}"])
                        P.op("dve", I("tensor_tensor", out=a_b[:, fc, t0:t0 + n], in0=a_[:, 0:n], in1=a_[:, 0:n], op=ALU.mult), reads=[f"ar{k_}"], writes=[f"aT{b % 2}_{bi_}"])
                        ku[0] += 1

                def down_tb(b, bi_):
                    t0, n = blocks[bi_]
                    a_b = aT[b % 2]
                    for i_, (src, dst, r, m, col) in enumerate(tiles):
                        if not (t0 <= col < t0 + n):
                            continue
                        pd = psd[i_ % 2]
                        fns = []
                        for half in range(2):
                            for fc in range(4):
                                fns.append(I("matmul", pd[0:m, half * 512:(half + 1) * 512], lhsT=a_b[:, fc, col:col + m], rhs=wd[:, fc, half * 512:(half + 1) * 512], start=(fc == 0), stop=(fc == 3)))
                        P.op("pe", fns, reads=[f"aT{b % 2}_{bi_}", "wd"], writes=[f"psd{i_ % 2}/x"])
                        if b == 0:
                            P.op("act", I("copy", out=facc[0:m, i_, :], in_=pd[0:m, :]), reads=[f"psd{i_ % 2}/x"], writes=[f"f{i_}"])
                        else:
                            P.op("dve", I("tensor_tensor", out=facc[0:m, i_, :], in0=pd[0:m, :], in1=facc[0:m, i_, :], op=ALU.add), reads=[f"psd{i_ % 2}/x", f"f{i_}"], writes=[f"f{i_}"])
                        if b == NBF - 1:
                            xi = i_ % 2
                            load(xt[xi][0:m, :], f"xt{xi}", src[r:r + m, :], sem=f"xt{xi}")
                            postnorm_res(env, facc[0:m, i_, :], f"f{i_}", m, gpost, "gpost", xt[xi][0:m, :], f"xt{xi}", xo[i_ % 2][0:m, :], f"xo{i_ % 2}", eng2="pool")
                            store(dst[r:r + m, :], xo[i_ % 2][0:m, :], f"xo{i_ % 2}", f"xo{i_ % 2}")

                nb_ = len(blocks)
                for b in range(NBF):
                    if b < NBF - 1:
                        for bi_ in range(nb_):
                            if b == 0:
                                if bi_ == 0:
                                    pre_block(0)
                                if bi_ + 1 < nb_:
                                    pre_block(bi_ + 1)
                            up_tb(b, bi_)
                        if b + 2 < NBF:
                            load_wu(b + 2)
                        for bi_ in range(nb_):
                            down_tb(b, bi_)
                        load_wd(b + 1)
                    else:
                        up_tb(b, 0)
                        for bi_ in range(nb_):
                            if bi_ + 1 < nb_:
                                up_tb(b, bi_ + 1)
                            down_tb(b, bi_)

        def copy_phase(src_p, src_s, dst_p, dst_s):
            with phase() as (sb, ps):
                t = [sb(f"cp{i}", [128, D], F32) for i in range(2)]
                rows = [(src_p, dst_p, r, 128) for r in range(0, SEQ, 128)] + [(src_s, dst_s, 0, 64)]
                for i, (s_, d_, r, n) in enumerate(rows):
                    load(t[i % 2][0:n, :], f"cp{i % 2}", s_[r:r + n, :], sem=f"cpl{i % 2}")
                    store(d_[r:r + n, :], t[i % 2][0:n, :], f"cp{i % 2}", f"cps{i % 2}")

        if stage == 0:
            copy_phase(T["xp"], T["xs"], T["xa_p"], T["xa_s"])
            copy_phase(T["xa_p"], T["xa_s"], T["yp"], T["ys"])
            return nc
        gla_phase()
        if stage == 1:
            copy_phase(T["xa_p"], T["xa_s"], T["yp"], T["ys"])
            return nc
        if stage == 2:
            mlp_phase(0, T["xa_p"], T["xa_s"], T["yp"], T["ys"])
            return nc
        mlp_phase(0, T["xa_p"], T["xa_s"], T["xa_p"], T["xa_s"])
        s5_phase(nc, P, T, phase, mk_norm_env, prenorm, postnorm_res, load, store, load_gain, load_w, prenorm_a, prenorm_b)
        if stage == 3:
            copy_phase(T["xa_p"], T["xa_s"], T["yp"], T["ys"])
            return nc
        mlp_phase(1, T["xa_p"], T["xa_s"], T["yp"], T["ys"])
        P.barrier()
        P.flush()
    return nc


_NC_CACHE = {}


def _in_maps(inputs):
    consts = _consts()
    maps = []
    f = lambda a: np.ascontiguousarray(np.asarray(a, dtype=np.float32))
    for i in range(8):
        m = {}
        m["xp"] = f(inputs["x_prompt"][i])
        m["xs"] = f(inputs["x_sample"][NB * i:NB * (i + 1)]).reshape(NS, D)
        m["sg"] = f(inputs["state_gla"][0, NB * i:NB * (i + 1)])
        m["sre"] = f(inputs["state_s5_re"][0, NB * i:NB * (i + 1)]).reshape(NB, NG * NP)
        m["sim"] = f(inputs["state_s5_im"][0, NB * i:NB * (i + 1)]).reshape(NB, NG * NP)
        for k in IN_SPECS:
            if k not in m:
                m[k] = f(inputs[k])
        m.update(consts)
        maps.append(m)
    return maps


def kernel(**inputs):
    stage = int(os.environ.get("KSTAGE", "99"))
    ncores = int(os.environ.get("KCORES", "8"))
    if stage not in _NC_CACHE:
        _NC_CACHE[stage] = build(stage)
    nc = _NC_CACHE[stage]
    maps = _in_maps(inputs)[:ncores]
    res = run_bass_kernel_spmd(nc, maps, core_ids=list(range(ncores)))
    R = res.results
    nb = ncores
    y_prompt = np.stack([R[i]["yp"] for i in range(nb)])
    y_sample = np.concatenate([R[i]["ys"].reshape(NB, 4, D) for i in range(nb)])
    gla_prompt = np.stack([R[i]["glap"] for i in range(nb)])[None]
    gla_sample = np.concatenate([R[i]["glas"] for i in range(nb)])[None]
    s5p = np.stack([R[i]["s5p"].reshape(2, NG, NP) for i in range(nb)])
    s5s = np.concatenate([R[i]["s5s"].reshape(NB, 2, NG, NP) for i in range(nb)])
    outs = (y_prompt, y_sample, gla_prompt, gla_sample,
            s5p[:, 0][None], s5p[:, 1][None], s5s[:, 0][None], s5s[:, 1][None])
    return tuple(np.ascontiguousarray(o, dtype=np.float32) for o in outs)
```
